# Optimizing a Trainium2 kernel written in Bass

```python
import math
import jax
import jax.numpy as jnp
from jax import lax
import numpy as np

D_MODEL = 2048
BATCH = 2
SEQ = 16384
DEPTH = 4

N_META = 16
BLOCK = 128
N_PAD = BLOCK - N_META
NORM_EPS = 1e-6

MIX_DIM = D_MODEL
ATTN_DIM = MIX_DIM // 2
HEAD_DIM = 64
ATTN_HEADS = ATTN_DIM // HEAD_DIM
ATTN_KV_HEADS = ATTN_HEADS // 4
ATTN_GROUP = ATTN_HEADS // ATTN_KV_HEADS
WINDOW = 128
QK_EPS = 1e-6

RWKV_DIM = MIX_DIM - ATTN_DIM
RWKV_HEAD = 64
RWKV_HEADS = RWKV_DIM // RWKV_HEAD
DECAY_LORA = max(32, int(round(1.8 * RWKV_DIM ** 0.5 / 32)) * 32)
AAA_LORA = max(32, int(round(1.8 * RWKV_DIM ** 0.5 / 32)) * 32)
GATE_LORA = max(32, int(round(0.6 * RWKV_DIM ** 0.8 / 32)) * 32)
RWKV_GN_EPS = 64e-5

Q_COLS = ATTN_HEADS * HEAD_DIM
KV_COLS = ATTN_KV_HEADS * HEAD_DIM
ATTN_COLS = Q_COLS + 2 * KV_COLS
RWKV_COLS = 3 * RWKV_DIM + DECAY_LORA + AAA_LORA + GATE_LORA
AR_IN_COLS = ATTN_COLS + RWKV_COLS

SSD_INNER = 2 * D_MODEL
SSD_HEAD_DIM = 64
SSD_HEADS = SSD_INNER // SSD_HEAD_DIM
SSD_GROUPS = 8
SSD_HPG = SSD_HEADS // SSD_GROUPS
SSD_STATE = 128
SSD_CONV = 4
SSD_CONV_DIM = SSD_INNER + 2 * SSD_GROUPS * SSD_STATE
SSD_IN_COLS = SSD_INNER + SSD_CONV_DIM + SSD_HEADS
SSD_NORM_EPS = 1e-5

FFN_DIM = 256 * (-(-8 * D_MODEL // (3 * 256)))
FFN_CONV = 3

N_EVEN = (DEPTH + 1) // 2
N_ODD = DEPTH // 2

kernel_name = "hybrid_swa_rwkv7_ssd_convffn"


def rms_norm(x, w, eps):
    xf = x.astype(jnp.float32)
    y = xf * lax.rsqrt(jnp.mean(xf * xf, axis=-1, keepdims=True) + eps)
    return (y * w.astype(jnp.float32)).astype(x.dtype)


def causal_dwconv(x, w, b):
    K = w.shape[0]
    T = x.shape[1]
    xp = jnp.pad(x, ((0, 0), (K - 1, 0), (0, 0)))
    y = b
    for j in range(K):
        y = y + w[j] * xp[:, j:j + T]
    return y


def _banded(t, nb):
    tb = t.reshape((t.shape[0], nb, BLOCK) + t.shape[2:])
    prev = jnp.pad(tb[:, :-1], [(0, 0), (1, 0)] + [(0, 0)] * (tb.ndim - 2))
    return jnp.concatenate([prev, tb], axis=2)


def swa_sink_attention(q, k, v, sinks, valid):
    f32 = jnp.float32
    bsz, P = q.shape[0], q.shape[1]
    nb = P // BLOCK
    qb = q.astype(f32).reshape(bsz, nb, BLOCK, ATTN_KV_HEADS, ATTN_GROUP, HEAD_DIM)
    kb = _banded(k.astype(f32), nb)
    vb = _banded(v.astype(f32), nb)
    kvalid = _banded(valid[None], nb)[0]
    km = k[:, N_PAD:BLOCK].astype(f32)
    vm = v[:, N_PAD:BLOCK].astype(f32)
    qpos = jnp.arange(nb)[:, None] * BLOCK + jnp.arange(BLOCK)[None, :]
    kpos = jnp.arange(nb)[:, None] * BLOCK - BLOCK + jnp.arange(2 * BLOCK)[None, :]
    dist = qpos[:, :, None] - kpos[:, None, :]
    band = (dist >= 0) & (dist < WINDOW) & kvalid[:, None, :]
    mdist = qpos[:, :, None] - (N_PAD + jnp.arange(N_META))[None, None, :]
    mvis = mdist >= WINDOW
    scale = HEAD_DIM ** -0.5
    s = jnp.einsum('bnqhgd,bnkhd->bnhgqk', qb, kb) * scale
    sm = jnp.einsum('bnqhgd,bmhd->bnhgqm', qb, km) * scale
    s = jnp.where(band[None, :, None, None], s, -jnp.inf)
    sm = jnp.where(mvis[None, :, None, None], sm, -jnp.inf)
    sink = sinks.astype(f32).reshape(1, 1, ATTN_KV_HEADS, ATTN_GROUP, 1, 1)
    mx = jnp.maximum(jnp.maximum(s.max(-1, keepdims=True), sm.max(-1, keepdims=True)), sink)
    p = jnp.exp(s - mx)
    pm = jnp.exp(sm - mx)
    inv = 1.0 / (p.sum(-1, keepdims=True) + pm.sum(-1, keepdims=True) + jnp.exp(sink - mx))
    o = (jnp.einsum('bnhgqk,bnkhd->bnqhgd', p * inv, vb)
         + jnp.einsum('bnhgqm,bmhd->bnqhgd', pm * inv, vm))
    return o.reshape(bsz, P, ATTN_HEADS * HEAD_DIM)


def rwkv7_time_mix(zr, valid, w0, w_up, a0, a_up, g_up, k_k, k_a, r_k, ln_w, ln_b):
    f32 = jnp.float32
    bsz, P = zr.shape[0], zr.shape[1]
    zr = zr.astype(f32)
    r, k, v, xw, xa, xg = jnp.split(
        zr, [RWKV_DIM, 2 * RWKV_DIM, 3 * RWKV_DIM, 3 * RWKV_DIM + DECAY_LORA,
             3 * RWKV_DIM + DECAY_LORA + AAA_LORA], axis=-1)
    w_log = -jnp.exp(-jax.nn.softplus(-(w0.astype(f32) + jnp.tanh(xw) @ w_up.astype(f32))) - 0.5)
    w_log = jnp.where(valid[None, :, None], w_log, 0.0)
    a = jax.nn.sigmoid(a0.astype(f32) + xa @ a_up.astype(f32))
    g = jax.nn.sigmoid(xg) @ g_up.astype(f32)
    heads = lambda t: t.reshape(bsz, P, RWKV_HEADS, RWKV_HEAD)
    hp = lambda t: t.astype(f32).reshape(RWKV_HEADS, RWKV_HEAD)
    kk = heads(k * k_k.astype(f32))
    kk = kk / jnp.maximum(jnp.linalg.norm(kk, axis=-1, keepdims=True), 1e-12)
    k = k * (1.0 + (a - 1.0) * k_a.astype(f32))
    r, k, v, a, w = heads(r), heads(k), heads(v), heads(a), heads(jnp.exp(w_log))

    def step(S, inp):
        r_t, w_t, k_t, v_t, kk_t, a_t = inp
        sa = jnp.einsum('bhij,bhj->bhi', S, -kk_t)
        S = (S * w_t[:, :, None, :] + sa[..., None] * (kk_t * a_t)[:, :, None, :]
             + v_t[..., None] * k_t[:, :, None, :])
        return S, jnp.einsum('bhij,bhj->bhi', S, r_t)

    seq_first = lambda t: jnp.moveaxis(t, 1, 0)
    S0 = jnp.zeros((bsz, RWKV_HEADS, RWKV_HEAD, RWKV_HEAD), f32)
    _, y = lax.scan(step, S0, tuple(seq_first(t) for t in (r, w, k, v, kk, a)))
    y = jnp.moveaxis(y, 0, 1)
    mu = jnp.mean(y, axis=-1, keepdims=True)
    var = jnp.mean(jnp.square(y - mu), axis=-1, keepdims=True)
    y = (y - mu) * lax.rsqrt(var + RWKV_GN_EPS) * hp(ln_w) + hp(ln_b)
    y = y + jnp.sum(r * k * r_k.astype(f32), axis=-1, keepdims=True) * v
    return y.reshape(bsz, P, RWKV_DIM) * g


def attn_rwkv_mixer(h, valid, w_in, shift_mu, q_norm_w, k_norm_w, sinks, w0, w_up, a0, a_up,
                    g_up, k_k, k_a, r_k, ln_w, ln_b, w_out):
    bsz, P, _ = h.shape
    zin = h @ w_in
    za, zr = zin[..., :ATTN_COLS], zin[..., ATTN_COLS:]
    q, k, v = jnp.split(za, [Q_COLS, Q_COLS + KV_COLS], axis=-1)
    q = rms_norm(q.reshape(bsz, P, ATTN_HEADS, HEAD_DIM), q_norm_w, QK_EPS)
    k = rms_norm(k.reshape(bsz, P, ATTN_KV_HEADS, HEAD_DIM), k_norm_w, QK_EPS)
    v = v.reshape(bsz, P, ATTN_KV_HEADS, HEAD_DIM)
    attn = swa_sink_attention(q, k, v, sinks, valid)
    zr_prev = jnp.pad(zr[:, :-1], ((0, 0), (1, 0), (0, 0)))
    zr = zr + (zr_prev - zr) * shift_mu
    tm = rwkv7_time_mix(zr, valid, w0, w_up, a0, a_up, g_up, k_k, k_a, r_k, ln_w, ln_b)
    return jnp.concatenate([attn.astype(h.dtype), tm.astype(h.dtype)], axis=-1) @ w_out


def mamba2_ssd_mixer(h, valid, w_in, conv_w, conv_b, dt_bias, a_log, d_skip, norm_w, w_out):
    f32 = jnp.float32
    bsz, P, _ = h.shape
    nc = P // BLOCK
    zxbcdt = h @ w_in
    z, xbc, dt = jnp.split(zxbcdt, [SSD_INNER, SSD_INNER + SSD_CONV_DIM], axis=-1)
    xbc = jax.nn.silu(causal_dwconv(xbc, conv_w, conv_b)).astype(f32) * valid[None, :, None]
    x, bm, cm = jnp.split(xbc, [SSD_INNER, SSD_INNER + SSD_GROUPS * SSD_STATE], axis=-1)
    dt = jax.nn.softplus(dt.astype(f32) + dt_bias.astype(f32)) * valid[None, :, None]
    a = -jnp.exp(a_log.astype(f32)).reshape(SSD_GROUPS, SSD_HPG)
    x = x.reshape(bsz, P, SSD_GROUPS, SSD_HPG, SSD_HEAD_DIM)
    dtg = dt.reshape(bsz, P, SSD_GROUPS, SSD_HPG)
    adt = dtg * a
    xdt = x * dtg[..., None]
    bm = bm.reshape(bsz, P, SSD_GROUPS, SSD_STATE)
    cm = cm.reshape(bsz, P, SSD_GROUPS, SSD_STATE)
    chunks = lambda t: jnp.moveaxis(t.reshape((bsz, nc, BLOCK) + t.shape[2:]), 1, 0)
    causal = jnp.tril(jnp.ones((BLOCK, BLOCK), bool))

    def step(state, inp):
        xc, ac, bc, cc = inp
        cum = jnp.cumsum(ac, axis=1)
        seg = cum[:, :, None] - cum[:, None, :]
        decay = jnp.exp(jnp.where(causal[None, :, :, None, None], seg, -jnp.inf))
        cb = jnp.einsum('btgn,bsgn->btsg', cc, bc)
        y = jnp.einsum('btsg,btsgj,bsgjp->btgjp', cb, decay, xc)
        y = y + jnp.einsum('btgn,bgjpn->btgjp', cc, state) * jnp.exp(cum)[..., None]
        to_end = jnp.exp(cum[:, -1:] - cum)
        state = (state * jnp.exp(cum[:, -1])[..., None, None]
                 + jnp.einsum('bsgn,bsgj,bsgjp->bgjpn', bc, to_end, xc))
        return state, y

    state0 = jnp.zeros((bsz, SSD_GROUPS, SSD_HPG, SSD_HEAD_DIM, SSD_STATE), f32)
    _, y = lax.scan(step, state0, (chunks(xdt), chunks(adt), chunks(bm), chunks(cm)))
    y = jnp.moveaxis(y, 0, 1).reshape(bsz, P, SSD_GROUPS, SSD_HPG, SSD_HEAD_DIM)
    y = y + d_skip.astype(f32).reshape(SSD_GROUPS, SSD_HPG)[..., None] * x
    y = y.reshape(bsz, P, SSD_INNER) * jax.nn.silu(z.astype(f32))
    yg = y.reshape(bsz, P, SSD_GROUPS, SSD_INNER // SSD_GROUPS)
    yg = yg * lax.rsqrt(jnp.mean(yg * yg, axis=-1, keepdims=True) + SSD_NORM_EPS)
    y = yg.reshape(bsz, P, SSD_INNER) * norm_w.astype(f32)
    return y.astype(h.dtype) @ w_out


def conv_glu_ffn(h, w_up, conv_w, conv_b, w_down):
    gate, val = jnp.split(h @ w_up, 2, axis=-1)
    gate = causal_dwconv(gate, conv_w, conv_b)
    return (jax.nn.silu(gate) * val) @ w_down


def setup_inputs(seed: int = 0) -> dict:
    key = jax.random.key(seed)
    ks = iter(jax.random.split(key, 40))
    nrm = lambda shape, scale: scale * jax.random.normal(next(ks), shape, jnp.float32)
    uni = lambda shape, lo, hi: jax.random.uniform(next(ks), shape, jnp.float32, lo, hi)
    E, O = N_EVEN, N_ODD
    out_scale = (2 * DEPTH) ** -0.5
    dt0 = jnp.exp(uni((O, SSD_HEADS), math.log(1e-3), math.log(1e-1)))
    return {
        "x": nrm((BATCH, SEQ, D_MODEL), 1.0),
        "meta_tokens": nrm((N_META, D_MODEL), 1.0),
        "mix_norm_w": 1.0 + nrm((DEPTH, D_MODEL), 0.05),
        "ffn_norm_w": 1.0 + nrm((DEPTH, D_MODEL), 0.05),
        "ar_w_in": nrm((E, D_MODEL, AR_IN_COLS), D_MODEL ** -0.5),
        "ar_shift_mu": uni((E, RWKV_COLS), 0.0, 1.0),
        "attn_q_norm_w": 1.0 + nrm((E, HEAD_DIM), 0.05),
        "attn_k_norm_w": 1.0 + nrm((E, HEAD_DIM), 0.05),
        "attn_sinks": nrm((E, ATTN_HEADS), 0.5),
        "rwkv_w0": uni((E, RWKV_DIM), -6.0, -1.0),
        "rwkv_w_up": nrm((E, DECAY_LORA, RWKV_DIM), DECAY_LORA ** -0.5),
        "rwkv_a0": nrm((E, RWKV_DIM), 0.1),
        "rwkv_a_up": nrm((E, AAA_LORA, RWKV_DIM), AAA_LORA ** -0.5),
        "rwkv_g_up": nrm((E, GATE_LORA, RWKV_DIM), GATE_LORA ** -0.5),
        "rwkv_k_k": 0.85 + nrm((E, RWKV_DIM), 0.05),
        "rwkv_k_a": 1.0 + nrm((E, RWKV_DIM), 0.05),
        "rwkv_r_k": nrm((E, RWKV_HEADS, RWKV_HEAD), 0.1),
        "rwkv_ln_w": 1.0 + nrm((E, RWKV_DIM), 0.05),
        "rwkv_ln_b": nrm((E, RWKV_DIM), 0.02),
        "ar_w_out": nrm((E, MIX_DIM, D_MODEL), MIX_DIM ** -0.5 * out_scale),
        "ssd_w_in": nrm((O, D_MODEL, SSD_IN_COLS), D_MODEL ** -0.5),
        "ssd_conv_w": nrm((O, SSD_CONV, SSD_CONV_DIM), SSD_CONV ** -0.5),
        "ssd_conv_b": nrm((O, SSD_CONV_DIM), 0.02),
        "ssd_dt_bias": dt0 + jnp.log(-jnp.expm1(-dt0)),
        "ssd_a_log": jnp.log(uni((O, SSD_HEADS), 1.0, 16.0)),
        "ssd_d": 1.0 + nrm((O, SSD_HEADS), 0.1),
        "ssd_norm_w": 1.0 + nrm((O, SSD_INNER), 0.05),
        "ssd_w_out": nrm((O, SSD_INNER, D_MODEL), SSD_INNER ** -0.5 * out_scale),
        "ffn_w_up": nrm((DEPTH, D_MODEL, 2 * FFN_DIM), D_MODEL ** -0.5),
        "ffn_conv_w": nrm((DEPTH, FFN_CONV, FFN_DIM), FFN_CONV ** -0.5),
        "ffn_conv_b": nrm((DEPTH, FFN_DIM), 0.02),
        "ffn_w_down": nrm((DEPTH, FFN_DIM, D_MODEL), FFN_DIM ** -0.5 * out_scale),
    }


def reference(x, meta_tokens, mix_norm_w, ffn_norm_w, ar_w_in, ar_shift_mu, attn_q_norm_w,
              attn_k_norm_w, attn_sinks, rwkv_w0, rwkv_w_up, rwkv_a0, rwkv_a_up, rwkv_g_up,
              rwkv_k_k, rwkv_k_a, rwkv_r_k, rwkv_ln_w, rwkv_ln_b, ar_w_out, ssd_w_in, ssd_conv_w,
              ssd_conv_b, ssd_dt_bias, ssd_a_log, ssd_d, ssd_norm_w, ssd_w_out, ffn_w_up,
              ffn_conv_w, ffn_conv_b, ffn_w_down):
    bsz, seq, _ = x.shape
    P = N_PAD + N_META + seq
    valid = jnp.arange(P) >= N_PAD
    vmask = valid[None, :, None].astype(x.dtype)
    res = jnp.concatenate([
        jnp.zeros((bsz, N_PAD, D_MODEL), x.dtype),
        jnp.broadcast_to(meta_tokens.astype(x.dtype)[None], (bsz, N_META, D_MODEL)),
        x], axis=1)
    for layer in range(DEPTH):
        i = layer // 2
        h = rms_norm(res, mix_norm_w[layer], NORM_EPS) * vmask
        if layer % 2 == 0:
            mix = attn_rwkv_mixer(h, valid, ar_w_in[i], ar_shift_mu[i], attn_q_norm_w[i],
                                  attn_k_norm_w[i], attn_sinks[i], rwkv_w0[i], rwkv_w_up[i],
                                  rwkv_a0[i], rwkv_a_up[i], rwkv_g_up[i], rwkv_k_k[i], rwkv_k_a[i],
                                  rwkv_r_k[i], rwkv_ln_w[i], rwkv_ln_b[i], ar_w_out[i])
        else:
            mix = mamba2_ssd_mixer(h, valid, ssd_w_in[i], ssd_conv_w[i], ssd_conv_b[i],
                                   ssd_dt_bias[i], ssd_a_log[i], ssd_d[i], ssd_norm_w[i],
                                   ssd_w_out[i])
        res = res + mix.astype(res.dtype)
        h = rms_norm(res, ffn_norm_w[layer], NORM_EPS) * vmask
        res = res + conv_glu_ffn(h, ffn_w_up[layer], ffn_conv_w[layer], ffn_conv_b[layer],
                                 ffn_w_down[layer]).astype(res.dtype)
    return res[:, N_PAD + N_META:]
```

```python
from contextlib import ExitStack
import numpy as np
import concourse.bass as bass
import concourse.mybir as mybir
from concourse.bass_utils import run_bass_kernel_spmd

F32 = mybir.dt.float32
BF16 = mybir.dt.bfloat16
AF = mybir.ActivationFunctionType
ALU = mybir.AluOpType
AX = mybir.AxisListType


class Res:
    __slots__ = ("w", "r", "excl")

    def __init__(self, excl=False):
        self.w = None
        self.r = {}
        self.excl = excl


class _Eng:
    def __init__(self, name, sem):
        self.name = name
        self.sem = sem
        self.count = 0
        self.waited = {}
        self.ops = []


class _Rec:
    def __init__(self):
        self.call = None

    def __getattr__(self, name):
        def f(*args, **kwargs):
            self.call = (name, args, kwargs)
            return self
        return f


class _Slot:
    def __init__(self, key, sem):
        self.key = key
        self.sem = sem
        self.uses = 0


class Prog:
    ENGS = ("sp", "act", "dve", "pool", "pe")

    def __init__(self, nslots=6, self_sync=True):
        self.nc = bass.Bass("TRN2", target_bir_lowering=False)
        self.es = ExitStack()
        self.self_sync = self_sync
        self.E = {}
        for n in self.ENGS:
            self.E[n] = _Eng(n, self.es.enter_context(self.nc.semaphore("s_" + n)))
        self.slots = {}
        self.slot_rr = {}
        for q in ("sp", "pool", "act"):
            self.slots[q] = [_Slot("d_%s%d" % (q, i), self.es.enter_context(self.nc.semaphore("d_%s%d" % (q, i))))
                             for i in range(nslots)]
            self.slot_rr[q] = 0
        self._n = 0

    def dram(self, name, shape, dtype, kind):
        return self.nc.dram_tensor(name, list(shape), dtype, kind=kind).ap()

    def sbuf(self, shape, dtype, name=None):
        self._n += 1
        return self.es.enter_context(self.nc.sbuf_tensor(name or "sb%d" % self._n, list(shape), dtype))

    def psum(self, shape, dtype, name=None):
        self._n += 1
        return self.es.enter_context(self.nc.psum_tensor(name or "ps%d" % self._n, list(shape), dtype))

    def _wait(self, E, tok):
        key, sem, val = tok
        if E.waited.get(key, 0) >= val:
            return
        E.waited[key] = val
        E.ops.append(lambda h, sem=sem, val=val: h.wait_ge(sem, val))

    def _deps(self, E, reads, writes, ekey):
        toks = []
        for r in reads:
            if r.w is not None:
                toks.append(r.w)
        for w in writes:
            if w.w is not None:
                toks.append(w.w)
            for k, t in w.r.items():
                if k == ekey:
                    continue
                toks.append(t)
        for t in toks:
            if t[0] == ekey and (E.name == "pe" or not self.self_sync):
                continue
            self._wait(E, t)

    def op(self, eng, fn, reads=(), writes=()):
        E = self.E[eng]
        ekey = "s_" + eng
        ex = [r for r in reads if r.excl]
        if ex:
            writes = list(writes) + ex
            reads = [r for r in reads if not r.excl]
        self._deps(E, reads, writes, ekey)
        E.count += 1
        tok = (ekey, E.sem, E.count)
        rec = _Rec()
        fn(rec)
        E.ops.append(lambda h, c=rec.call, sem=E.sem: getattr(h, c[0])(*c[1], **c[2]).then_inc(sem, 1))
        E.waited[ekey] = max(E.waited.get(ekey, 0), 0)
        for w in writes:
            w.w = tok
            w.r = {}
        for r in reads:
            r.r[ekey] = tok
        return tok

    def dma(self, q, out, in_, reads=(), writes=()):
        E = self.E[q]
        sl = self.slots[q][self.slot_rr[q] % len(self.slots[q])]
        self.slot_rr[q] += 1
        if sl.uses > 0:
            self._wait(E, (sl.key, sl.sem, 16 * sl.uses))
        self._deps(E, reads, writes, None)
        sl.uses += 1
        tok = (sl.key, sl.sem, 16 * sl.uses)
        E.ops.append(lambda h, out=out, in_=in_, sem=sl.sem: h.dma_start(out=out, in_=in_).then_inc(sem, 16))
        for w in writes:
            w.w = tok
            w.r = {}
        for r in reads:
            r.r[sl.key] = tok
        return tok

    def finalize(self):
        for q, sls in self.slots.items():
            for sl in sls:
                if sl.uses > 0:
                    self._wait(self.E[q], (sl.key, sl.sem, 16 * sl.uses))
        nc = self.nc
        with nc.Block() as block:
            decos = {"sp": block.sync, "act": block.scalar, "dve": block.vector,
                     "pool": block.gpsimd, "pe": block.tensor}
            for n in self.ENGS:
                ops = self.E[n].ops
                if not ops:
                    continue

                def body(h, ops=ops):
                    for f in ops:
                        f(h)
                decos[n](body)
        self.es.close()
        return nc

    def ninstr(self):
        return {n: len(self.E[n].ops) for n in self.ENGS}

NT = 352


class LinCtx:
    def __init__(self, P, KCmax_small=16, TSmax=1408, KCmax=44, opnd_elems=44 * 704):
        self.P = P
        self.opnd = P.sbuf([128, opnd_elems], BF16, "opnd")
        self.opnd_r = [Res() for _ in range(KCmax)]
        self.NST = 3
        self.stg = [P.sbuf([128, TSmax + 2], F32, "stg%d" % i) for i in range(self.NST)]
        self.stg_r = [Res() for _ in range(self.NST)]
        self.stg_i = 0
        self.stg2 = [P.sbuf([128, 704], F32, "stgv%d" % i) for i in range(2)]
        self.stg2_r = [Res() for _ in range(2)]
        self.stg2_i = 0
        self.tmp = [P.sbuf([128, TSmax], F32, "tmp%d" % i) for i in range(2)]
        self.tmp_r = [Res() for _ in range(2)]
        self.tmp_i = 0
        self.sq = [P.sbuf([128, TSmax], BF16, "sq%d" % i) for i in range(2)]
        self.sq_r = [Res() for _ in range(2)]
        self.rstd = P.sbuf([128, TSmax], F32, "rstd")
        self.rstd_r = Res()
        self.maskb = P.sbuf([128, TSmax], F32, "maskb")
        self.maskb_r = Res()
        self.wst = [P.sbuf([128, KCmax * 128], F32, "wst%d" % i) for i in range(2)]
        self.wst_r = [Res() for _ in range(2)]
        self.wbf = [P.sbuf([128, KCmax * 128], BF16, "wbf%d" % i) for i in range(2)]
        self.wbf_r = [[Res(), Res()] for _ in range(2)]
        self.w_i = 0
        self.osb = [P.sbuf([128, TSmax], F32, "osb%d" % i) for i in range(2)]
        self.osb_r = [[Res() for _ in range(4)] for _ in range(2)]
        self.rsb = [P.sbuf([128, TSmax], F32, "rsb%d" % i) for i in range(2)]
        self.rsb_r = [Res() for _ in range(2)]
        self.o_i = 0
        self.ps = [P.psum([128, 512], F32, "lps%d" % i) for i in range(8)]
        self.ps_r = [Res(excl=True) for _ in range(8)]
        self.ps_i = 0
        self.ones = P.sbuf([128, 128], BF16, "ones_bf")
        self.ones_r = Res()
        P.op("dve", lambda h: h.memset(self.ones[:], 1.0), writes=[self.ones_r])
        self.small = {}

    def const(self, name, shape, src_ap, q="sp"):
        t = self.P.sbuf(shape, F32, name)
        r = Res()
        self.P.dma(q, t[:], src_ap, writes=[r])
        return t, r


def linear_stage(L, *, mode, K, C, N, W, dst, src=None, src2=None, gamma=None, mask=None,
                 convw=None, convb=None, res=None, eps=1e-6):
    P = L.P
    KC = K // 128
    CB = (C + 127) // 128
    TS = 1408 if (K == 2048 and N % 1408 == 0) else 704
    assert N % TS == 0
    nsub = TS // NT
    opnd = L.opnd
    opv = opnd[:, 0:KC * TS].rearrange("p (k t) -> p k t", k=KC)

    if mode == "norm":
        gam_t, gam_r = L.const("gam%d" % id(gamma), [128, KC], gamma)
    if mode == "convglu":
        cw_t, cw_r = L.const("cw%d" % id(convw), [128, KC * 3], convw)
        cb_t, cb_r = L.const("cb%d" % id(convb), [128, KC], convb)

    for ts in range(N // TS):
        t0 = ts * TS
        if mode == "plain":
            for kc in range(KC):
                i = L.stg_i % L.NST
                L.stg_i += 1
                st, sr = L.stg[i], L.stg_r[i]
                P.dma("sp", st[:, 0:TS], src[kc * 128:(kc + 1) * 128, t0:t0 + TS], writes=[sr])
                if kc % 2 == 0:
                    P.op("dve", lambda h, st=st, kc=kc: h.tensor_copy(opv[:, kc, :], st[:, 0:TS]),
                         reads=[sr], writes=[L.opnd_r[kc]])
                else:
                    P.op("act", lambda h, st=st, kc=kc: h.copy(opv[:, kc, :], st[:, 0:TS]),
                         reads=[sr], writes=[L.opnd_r[kc]])
        elif mode == "norm":
            P.dma("pool", L.maskb[:, 0:TS], mask[0:1, t0:t0 + TS].partition_broadcast(128), writes=[L.maskb_r])
            banks = []
            for s in range(nsub):
                b = L.ps_i % 8
                L.ps_i += 1
                banks.append(b)
            for kc in range(KC):
                i = L.stg_i % L.NST
                L.stg_i += 1
                st, sr = L.stg[i], L.stg_r[i]
                P.dma("sp", st[:, 0:TS], src[kc * 128:(kc + 1) * 128, t0:t0 + TS], writes=[sr])
                sq, sqr = L.sq[kc % 2], L.sq_r[kc % 2]
                P.op("act", lambda h, st=st, sq=sq: h.activation(sq[:, 0:TS], st[:, 0:TS], AF.Square),
                     reads=[sr], writes=[sqr])
                for s in range(nsub):
                    b = banks[s]
                    P.op("pe", lambda h, b=b, sq=sq, s=s, kc=kc: h.matmul(
                        L.ps[b][:, 0:NT], L.ones[:, :], sq[:, s * NT:(s + 1) * NT],
                        start=(kc == 0), stop=(kc == KC - 1)),
                        reads=[sqr, L.ones_r], writes=[L.ps_r[b]])
            for s in range(nsub):
                b = banks[s]
                P.op("dve", lambda h, b=b, s=s: h.tensor_scalar(
                    L.rstd[:, s * NT:(s + 1) * NT], L.ps[b][:, 0:NT], 1.0 / K, eps, ALU.mult, ALU.add),
                    reads=[L.ps_r[b]], writes=[L.rstd_r])
            P.op("act", lambda h: h.activation(L.rstd[:, 0:TS], L.rstd[:, 0:TS], AF.Ln),
                 reads=[L.rstd_r], writes=[L.rstd_r])
            P.op("act", lambda h: h.activation(L.rstd[:, 0:TS], L.rstd[:, 0:TS], AF.Exp, scale=-0.5),
                 reads=[L.rstd_r], writes=[L.rstd_r])
            P.op("dve", lambda h: h.tensor_tensor(L.rstd[:, 0:TS], L.rstd[:, 0:TS], L.maskb[:, 0:TS], ALU.mult),
                 reads=[L.rstd_r, L.maskb_r], writes=[L.rstd_r])
            for kc in range(KC):
                i = L.stg_i % L.NST
                L.stg_i += 1
                st, sr = L.stg[i], L.stg_r[i]
                P.dma("sp", st[:, 0:TS], src[kc * 128:(kc + 1) * 128, t0:t0 + TS], writes=[sr])
                eng = "dve"
                P.op(eng, lambda h, st=st, kc=kc: h.scalar_tensor_tensor(
                    opv[:, kc, :], st[:, 0:TS], gam_t[:, kc:kc + 1], L.rstd[:, 0:TS], ALU.mult, ALU.mult),
                    reads=[sr, gam_r, L.rstd_r], writes=[L.opnd_r[kc]])
        elif mode == "convglu":
            for kc in range(KC):
                i = L.stg_i % L.NST
                L.stg_i += 1
                st, sr = L.stg[i], L.stg_r[i]
                rows = slice(kc * 128, (kc + 1) * 128)
                if t0 == 0:
                    P.op("pool", lambda h, st=st: h.memset(st[:, 0:2], 0.0), writes=[sr])
                    P.dma("sp", st[:, 2:TS + 2], src[rows, 0:TS], writes=[sr])
                else:
                    P.dma("sp", st[:, 0:TS + 2], src[rows, t0 - 2:t0 + TS], writes=[sr])
                j = L.stg2_i % 2
                L.stg2_i += 1
                sv, svr = L.stg2[j], L.stg2_r[j]
                P.dma("pool", sv[:, 0:TS], src2[rows, t0:t0 + TS], writes=[svr])
                k = L.tmp_i % 2
                L.tmp_i += 1
                tm, tmr = L.tmp[k], L.tmp_r[k]
                e1 = "dve"
                P.op(e1, lambda h, st=st, tm=tm, kc=kc: h.tensor_scalar(
                    tm[:, 0:TS], st[:, 2:TS + 2], cw_t[:, kc * 3 + 2:kc * 3 + 3], cb_t[:, kc:kc + 1],
                    ALU.mult, ALU.add), reads=[sr, cw_r, cb_r], writes=[tmr])
                P.op(e1, lambda h, st=st, tm=tm, kc=kc: h.scalar_tensor_tensor(
                    tm[:, 0:TS], st[:, 1:TS + 1], cw_t[:, kc * 3 + 1:kc * 3 + 2], tm[:, 0:TS],
                    ALU.mult, ALU.add), reads=[sr, cw_r, tmr], writes=[tmr])
                P.op(e1, lambda h, st=st, tm=tm, kc=kc: h.scalar_tensor_tensor(
                    tm[:, 0:TS], st[:, 0:TS], cw_t[:, kc * 3:kc * 3 + 1], tm[:, 0:TS],
                    ALU.mult, ALU.add), reads=[sr, cw_r, tmr], writes=[tmr])
                P.op("act", lambda h, tm=tm: h.activation(tm[:, 0:TS], tm[:, 0:TS], AF.Silu),
                     reads=[tmr], writes=[tmr])
                P.op("pool", lambda h, tm=tm, sv=sv, kc=kc: h.tensor_tensor(
                    opv[:, kc, :], tm[:, 0:TS], sv[:, 0:TS], ALU.mult),
                    reads=[tmr, svr], writes=[L.opnd_r[kc]])
        else:
            raise ValueError(mode)

        for cb in range(CB):
            M = min(128, C - cb * 128)
            wi = L.w_i % 2
            L.w_i += 1
            wst, wstr, wbf, wbfr = L.wst[wi], L.wst_r[wi], L.wbf[wi], L.wbf_r[wi]
            if cb == 0:
                P.dma("sp", wst[:, 0:KC * 128], W[0], writes=[wstr])
            if cb + 1 < CB:
                P.dma("sp", L.wst[1 - wi][:, 0:KC * 128], W[cb + 1], writes=[L.wst_r[1 - wi]])
            half = (KC // 2) * 128
            P.op("act", lambda h, wst=wst, wbf=wbf: h.copy(wbf[:, 0:half], wst[:, 0:half]),
                 reads=[wstr], writes=[wbfr[0]])
            P.op("pool", lambda h, wst=wst, wbf=wbf: h.tensor_copy(wbf[:, half:KC * 128], wst[:, half:KC * 128]),
                 reads=[wstr], writes=[wbfr[1]])
            oi = L.o_i % 2
            L.o_i += 1
            osb, osbr, rsb, rsbr = L.osb[oi], L.osb_r[oi], L.rsb[oi], L.rsb_r[oi]
            if res is not None:
                P.dma("pool", rsb[0:M, 0:TS], res[cb * 128:cb * 128 + M, t0:t0 + TS], writes=[rsbr])
            for s in range(nsub):
                b = L.ps_i % 8
                L.ps_i += 1
                for kc in range(KC):
                    P.op("pe", lambda h, b=b, wbf=wbf, kc=kc, s=s, M=M: h.matmul(
                        L.ps[b][0:M, 0:NT], wbf[:, kc * 128:kc * 128 + M], opv[:, kc, s * NT:(s + 1) * NT],
                        start=(kc == 0), stop=(kc == KC - 1)),
                        reads=[wbfr[0], wbfr[1], L.opnd_r[kc]], writes=[L.ps_r[b]])
                if res is not None:
                    P.op("dve", lambda h, b=b, s=s, M=M, osb=osb, rsb=rsb: h.tensor_tensor(
                        osb[0:M, s * NT:(s + 1) * NT], L.ps[b][0:M, 0:NT], rsb[0:M, s * NT:(s + 1) * NT], ALU.add),
                        reads=[L.ps_r[b], rsbr], writes=[osbr[s]])
                else:
                    e = "act" if s % 2 == 0 else "dve"
                    if e == "act":
                        P.op("act", lambda h, b=b, s=s, M=M, osb=osb: h.copy(
                            osb[0:M, s * NT:(s + 1) * NT], L.ps[b][0:M, 0:NT]),
                            reads=[L.ps_r[b]], writes=[osbr[s]])
                    else:
                        P.op("dve", lambda h, b=b, s=s, M=M, osb=osb: h.tensor_copy(
                            osb[0:M, s * NT:(s + 1) * NT], L.ps[b][0:M, 0:NT]),
                            reads=[L.ps_r[b]], writes=[osbr[s]])
            P.dma("sp", dst[cb * 128:cb * 128 + M, t0:t0 + TS], osb[0:M, 0:TS], reads=osbr[0:nsub])


def tile_w(W, dtype=np.float32):
    K, C = W.shape
    CB = (C + 127) // 128
    KC = K // 128
    Wp = np.zeros((K, CB * 128), dtype)
    Wp[:, :C] = W
    return np.ascontiguousarray(Wp.reshape(KC, 128, CB, 128).transpose(2, 1, 0, 3).reshape(CB, 128, KC * 128))

TQ = 384


def attn_stage(P, *, PN, qT, kT, v, qw, kw, sinks, masks, out):
    NB = PN // 128
    NTL = PN // TQ
    assert PN % TQ == 0
    ones64 = P.sbuf([64, 64], F32, "a_ones64")
    ones_r = Res()
    P.op("dve", lambda h: h.memset(ones64[:], 1.0), writes=[ones_r])
    qw_t = P.sbuf([64, 1], F32, "a_qw")
    kw_t = P.sbuf([64, 1], F32, "a_kw")
    cr = Res()
    P.dma("sp", qw_t[:], qw, writes=[cr])
    P.dma("sp", kw_t[:], kw, writes=[cr])
    P.op("dve", lambda h: h.tensor_scalar(qw_t[:], qw_t[:], 0.125, None, ALU.mult), reads=[cr], writes=[cr])
    esink = P.sbuf([128, 4], F32, "a_esink")
    es_r = Res()
    P.dma("sp", esink[:], sinks[0:1, :].partition_broadcast(128), writes=[es_r])
    P.op("act", lambda h: h.activation(esink[:], esink[:], AF.Exp), reads=[es_r], writes=[es_r])
    mstage = P.sbuf([128, 512], F32, "a_mstage")
    ms_r = Res()
    mk = []
    mk_r = []
    for i in range(4):
        m = P.sbuf([128, 512], BF16, "a_mask%d" % i)
        r = Res()
        P.dma("sp", mstage[:], masks[i], writes=[ms_r])
        P.op("dve", lambda h, m=m: h.tensor_copy(m[:], mstage[:]), reads=[ms_r], writes=[r])
        mk.append(m)
        mk_r.append(r)
    M_CUR, M_PREV, M_CUR0, M_PREV1 = 0, 1, 2, 3

    khat = P.sbuf([64, PN], BF16, "a_khat")
    khat_r = [Res() for _ in range(NTL)]
    vaug = P.sbuf([128, NB * 65], BF16, "a_vaug")
    vaug3 = vaug[:, :].rearrange("p (n c) -> p n c", c=65)
    vaug_r = [Res() for _ in range(NTL)]
    vones_r = Res()
    P.op("pool", lambda h: h.memset(vaug[:], 1.0), writes=[vones_r])
    vmeta = P.sbuf([16, 65], BF16, "a_vmeta")
    vmeta_s = P.sbuf([16, 64], F32, "a_vmeta_s")
    vm_r = Res()
    P.op("pool", lambda h: h.memset(vmeta[:], 1.0), writes=[vm_r])
    P.dma("sp", vmeta_s[:], v[112:128, :], writes=[vm_r])
    P.op("dve", lambda h: h.tensor_copy(vmeta[:, 0:64], vmeta_s[:]), reads=[vm_r], writes=[vm_r])

    NBUF = 2
    qst = [P.sbuf([64, 4 * TQ], F32, "a_qst%d" % i) for i in range(NBUF)]
    qst_r = [Res() for _ in range(NBUF)]
    kst = [P.sbuf([64, TQ], F32, "a_kst%d" % i) for i in range(NBUF)]
    kst_r = [Res() for _ in range(NBUF)]
    vst = [P.sbuf([128, 3 * 64], F32, "a_vst%d" % i) for i in range(NBUF)]
    vst_r = [Res() for _ in range(NBUF)]
    sqb = [P.sbuf([64, 5 * TQ], F32, "a_sq%d" % i) for i in range(NBUF)]
    sqb_r = [Res() for _ in range(NBUF)]
    rq = [P.sbuf([64, 5 * TQ], F32, "a_rq%d" % i) for i in range(NBUF)]
    rq_r = [Res() for _ in range(NBUF)]
    qhat = [P.sbuf([64, 4 * TQ], BF16, "a_qhat%d" % i) for i in range(NBUF)]
    qhat_r = [Res() for _ in range(NBUF)]
    osb = [P.sbuf([128, 3 * 256], F32, "a_osb%d" % i) for i in range(NBUF)]
    osb_r = [Res() for _ in range(NBUF)]
    NPT = 3
    pt = [P.sbuf([128, 512], BF16, "a_pt%d" % i) for i in range(NPT * 3)]
    pt_r = [Res() for _ in range(NPT * 3)]
    den = [P.sbuf([128, 8], F32, "a_den%d" % i) for i in range(2)]
    den_r = [Res() for _ in range(2)]
    ps_sq = [P.psum([64, 512], F32, "a_pssq%d" % i) for i in range(2)]
    ps_sq_r = [Res(excl=True) for _ in range(2)]
    ps_sc = [P.psum([128, 512], F32, "a_pssc%d" % i) for i in range(4)]
    ps_sc_r = [Res(excl=True) for _ in range(4)]
    ps_o = [P.psum([128, 4 * 65], F32, "a_pso%d" % i) for i in range(2)]
    ps_o_r = [Res(excl=True) for _ in range(2)]
    cnt = {"sq": 0, "sc": 0, "o": 0, "pt": 0}

    for tl in range(NTL):
        bi = tl % NBUF
        t0 = tl * TQ
        P.dma("sp", qst[bi][:, :].rearrange("p (h t) -> p h t", h=4), qT[:, :, t0:t0 + TQ], writes=[qst_r[bi]])
        P.dma("sp", kst[bi][:, :], kT[:, t0:t0 + TQ], writes=[kst_r[bi]])
        P.dma("pool", vst[bi][:, :].rearrange("p (n c) -> p n c", c=64),
              v[t0:t0 + TQ, :].rearrange("(n p) c -> p n c", p=128), writes=[vst_r[bi]])
        P.op("act", lambda h, bi=bi: h.activation(sqb[bi][:, 0:4 * TQ], qst[bi][:, :], AF.Square),
             reads=[qst_r[bi]], writes=[sqb_r[bi]])
        P.op("act", lambda h, bi=bi: h.activation(sqb[bi][:, 4 * TQ:5 * TQ], kst[bi][:, :], AF.Square),
             reads=[kst_r[bi]], writes=[sqb_r[bi]])
        for j in range(5):
            b = cnt["sq"] % 2
            cnt["sq"] += 1
            P.op("pe", lambda h, b=b, j=j, bi=bi: h.matmul(ps_sq[b][:, 0:TQ], ones64[:, :], sqb[bi][:, j * TQ:(j + 1) * TQ],
                                                           start=True, stop=True),
                 reads=[ones_r, sqb_r[bi]], writes=[ps_sq_r[b]])
            P.op("dve", lambda h, b=b, j=j, bi=bi: h.tensor_scalar(rq[bi][:, j * TQ:(j + 1) * TQ], ps_sq[b][:, 0:TQ],
                                                                   1.0 / 64, 1e-6, ALU.mult, ALU.add),
                 reads=[ps_sq_r[b]], writes=[rq_r[bi]])
        P.op("act", lambda h, bi=bi: h.activation(rq[bi][:, :], rq[bi][:, :], AF.Ln), reads=[rq_r[bi]], writes=[rq_r[bi]])
        P.op("act", lambda h, bi=bi: h.activation(rq[bi][:, :], rq[bi][:, :], AF.Exp, scale=-0.5),
             reads=[rq_r[bi]], writes=[rq_r[bi]])
        P.op("dve", lambda h, bi=bi: h.scalar_tensor_tensor(qhat[bi][:, :], qst[bi][:, :], qw_t[:, 0:1], rq[bi][:, 0:4 * TQ],
                                                            ALU.mult, ALU.mult),
             reads=[qst_r[bi], cr, rq_r[bi]], writes=[qhat_r[bi]])
        P.op("dve", lambda h, bi=bi, t0=t0: h.scalar_tensor_tensor(khat[:, t0:t0 + TQ], kst[bi][:, :], kw_t[:, 0:1],
                                                                    rq[bi][:, 4 * TQ:5 * TQ], ALU.mult, ALU.mult),
             reads=[kst_r[bi], cr, rq_r[bi]], writes=[khat_r[tl]])
        P.op("pool", lambda h, bi=bi, tl=tl: h.tensor_copy(vaug3[:, tl * 3:tl * 3 + 3, 0:64],
                                                           vst[bi][:, :].rearrange("p (n c) -> p n c", c=64)),
             reads=[vst_r[bi], vones_r], writes=[vaug_r[tl]])
        qh3 = qhat[bi][:, :].rearrange("p (h t) -> p h t", h=4)
        for bl in range(3):
            n = tl * 3 + bl
            parts = []
            if n >= 1:
                ptl = (n - 1) // 3
                parts.append(("prev", 128, khat[:, (n - 1) * 128:n * 128], M_PREV1 if n == 1 else M_PREV,
                              vaug3[:, n - 1, :], [khat_r[ptl], vaug_r[ptl]]))
            parts.append(("cur", 128, khat[:, n * 128:(n + 1) * 128], M_CUR0 if n == 0 else M_CUR,
                          vaug3[:, n, :], [khat_r[tl], vaug_r[tl]]))
            if n >= 2:
                parts.append(("meta", 16, khat[:, 112:128], None, vmeta[:, :], [khat_r[0], vm_r]))
            pts = []
            for (kind, kp, lhsT, mi, vr, deps) in parts:
                b = cnt["sc"] % 4
                cnt["sc"] += 1
                P.op("pe", lambda h, b=b, kp=kp, lhsT=lhsT, bl=bl, qh3=qh3: h.matmul(
                    ps_sc[b][0:kp, :], lhsT, qh3[:, :, bl * 128:(bl + 1) * 128], start=True, stop=True),
                    reads=[deps[0], qhat_r[bi]], writes=[ps_sc_r[b]])
                pi = cnt["pt"] % len(pt)
                cnt["pt"] += 1
                P.op("act", lambda h, b=b, kp=kp, pi=pi: h.activation(pt[pi][0:kp, :], ps_sc[b][0:kp, :], AF.Exp),
                     reads=[ps_sc_r[b]], writes=[pt_r[pi]])
                if mi is not None:
                    P.op("dve", lambda h, pi=pi, mi=mi: h.tensor_tensor(pt[pi][:, :], pt[pi][:, :], mk[mi][:, :], ALU.mult),
                         reads=[pt_r[pi], mk_r[mi]], writes=[pt_r[pi]])
                pts.append((pi, kp, vr, deps[1]))
            ob = cnt["o"] % 2
            cnt["o"] += 1
            po3 = ps_o[ob][:, :].rearrange("p (h c) -> p h c", c=65)
            for hh in range(4):
                for idx, (pi, kp, vr, vdep) in enumerate(pts):
                    P.op("pe", lambda h, pi=pi, kp=kp, vr=vr, hh=hh, idx=idx, po3=po3, npt=len(pts): h.matmul(
                        po3[:, hh, :], pt[pi][0:kp, hh * 128:(hh + 1) * 128], vr if kp == 128 else vr[0:kp, :],
                        start=(idx == 0), stop=(idx == npt - 1)),
                        reads=[pt_r[pi], vdep], writes=[ps_o_r[ob]])
            dn, dnr = den[ob], den_r[ob]
            P.op("dve", lambda h, dn=dn, po3=po3: h.tensor_tensor(dn[:, 0:4], po3[:, :, 64], esink[:, :], ALU.add),
                 reads=[ps_o_r[ob], es_r], writes=[dnr])
            P.op("dve", lambda h, dn=dn: h.reciprocal(dn[:, 4:8], dn[:, 0:4]), reads=[dnr], writes=[dnr])
            for hh in range(4):
                if hh % 2 == 0:
                    P.op("dve", lambda h, dn=dn, po3=po3, hh=hh, bl=bl, bi=bi: h.tensor_scalar(
                        osb[bi][:, bl * 256 + hh * 64: bl * 256 + hh * 64 + 64], po3[:, hh, 0:64], dn[:, 4 + hh:5 + hh], None,
                        ALU.mult), reads=[ps_o_r[ob], dnr], writes=[osb_r[bi]])
                else:
                    P.op("act", lambda h, dn=dn, po3=po3, hh=hh, bl=bl, bi=bi: h.activation(
                        osb[bi][:, bl * 256 + hh * 64: bl * 256 + hh * 64 + 64], po3[:, hh, 0:64], AF.Copy,
                        scale=dn[:, 4 + hh:5 + hh]), reads=[ps_o_r[ob], dnr], writes=[osb_r[bi]])
        P.dma("sp", out[t0:t0 + TQ, :].rearrange("(n p) c -> p n c", p=128),
              osb[bi][:, :].rearrange("p (n c) -> p n c", c=256), reads=[osb_r[bi]])


def attn_masks():
    k = np.arange(128)[:, None]
    q = np.arange(128)[None, :]
    cur = (k <= q)
    prev = (k > q)
    cur0 = cur & (k >= 112)
    prev1 = np.broadcast_to(k >= 112, (128, 128))
    return np.stack([np.tile(m.astype(np.float32), (1, 4)) for m in (cur, prev, cur0, prev1)])

LC = 64
WSC = -0.6065306597126334


def rwkv_consts():
    L = LC
    s = np.arange(L)[:, None]
    t = np.arange(L)[None, :]
    tri_incl = (s <= t).astype(np.float32)
    tri_strict = (s < t).astype(np.float32)
    upper = (s > t).astype(np.float32)
    c = {}
    c["TT1"] = np.concatenate([tri_strict, tri_incl], 1) * WSC
    c["TT2"] = np.concatenate([tri_incl, tri_incl], 1) * WSC
    c["TT3"] = np.concatenate([upper, upper], 1) * WSC
    c["negones"] = np.full((64, 64), WSC, np.float32)
    m = np.concatenate([tri_strict, tri_incl], 1)
    c["mask128"] = np.concatenate([m, m], 0)
    c["maskN"] = np.ascontiguousarray(tri_strict.T)
    c["ident"] = np.eye(128, dtype=np.float32)
    return {k: np.ascontiguousarray(v, dtype=np.float32) for k, v in c.items()}


class Rot:
    def __init__(self, items):
        self.items = items
        self.i = 0

    def next(self):
        x = self.items[self.i % len(self.items)]
        self.i += 1
        return x


def rwkv_stage(P, *, PN, rkvp, xwT, xaT, xgT, mu_rkv, mu_w, mu_a, mu_g, rows, w_up, a_up, g_up, consts, out,
               nchunks=None):
    NCH = PN // LC if nchunks is None else nchunks
    sb = lambda shape, name, dt=F32: P.sbuf(shape, dt, "r_" + name)

    cr = Res()
    ct = {}
    for nm, shp in (("TT1", [64, 128]), ("TT2", [64, 128]), ("TT3", [64, 128]), ("negones", [64, 64]),
                    ("mask128", [128, 128]), ("maskN", [64, 64]), ("ident", [128, 128])):
        ct[nm] = sb(shp, nm)
        P.dma("sp", ct[nm][:], consts[nm], writes=[cr])
    mu_b = sb([128, 768], "mu_b")
    P.dma("sp", mu_b[:], mu_rkv[0:1, :].partition_broadcast(128), writes=[cr])
    rowb = sb([128, 7 * 256], "rowb")
    for i in range(7):
        P.dma("sp", rowb[:, i * 256:(i + 1) * 256], rows[i:i + 1, :].partition_broadcast(128), writes=[cr])
    W0, A0, KK, KA, RK, LNW, LNB = [rowb[:, i * 256:(i + 1) * 256] for i in range(7)]
    muw = sb([64, 1], "muw"); mua = sb([64, 1], "mua"); mug0 = sb([128, 1], "mug0"); mug1 = sb([32, 1], "mug1")
    P.dma("sp", muw[:], mu_w, writes=[cr]); P.dma("sp", mua[:], mu_a, writes=[cr])
    P.dma("sp", mug0[:], mu_g[0:128, :], writes=[cr]); P.dma("sp", mug1[:], mu_g[128:160, :], writes=[cr])
    wup = sb([64, 256], "wup"); aup = sb([64, 256], "aup"); gup0 = sb([128, 256], "gup0"); gup1 = sb([32, 256], "gup1")
    P.dma("sp", wup[:], w_up, writes=[cr]); P.dma("sp", aup[:], a_up, writes=[cr])
    P.dma("sp", gup0[:], g_up[0:128, :], writes=[cr]); P.dma("sp", gup1[:], g_up[128:160, :], writes=[cr])

    NB = 2
    def bufs(shape, name, n=NB):
        return Rot([(sb(shape, "%s%d" % (name, i)), Res()) for i in range(n)])
    cur_b = bufs([128, 768], "cur"); prv_b = bufs([128, 768], "prv")
    xw_b = bufs([64, 65], "xw"); xa_b = bufs([64, 65], "xa"); xg0_b = bufs([128, 65], "xg0"); xg1_b = bufs([32, 65], "xg1")
    tw_b = bufs([64, 128], "tw"); ta_b = bufs([64, 128], "ta"); tg0_b = bufs([128, 64], "tg0"); tg1_b = bufs([32, 64], "tg1")
    sig_b = bufs([128, 256], "sig"); a_b = bufs([128, 256], "a"); g_b = bufs([64, 256], "g")
    kk_b = bufs([128, 256], "kk"); sq_b = bufs([128, 256], "sq"); ss_b = bufs([128, 8], "ss")
    t1_b = bufs([128, 256], "t1"); bkr_b = bufs([128, 256], "bkr")
    e1_b = bufs([128, 256], "e1"); e2_b = bufs([128, 256], "e2"); e3_b = bufs([128, 256], "e3")
    glb_b = bufs([64, 256], "glb")
    ar_b = bufs([128, 256], "ar"); bk_b = bufs([128, 256], "bk"); bkb_b = bufs([128, 256], "bkb")
    uv_b = [bufs([128, 64], "uv%d" % u) for u in range(4)]
    art_b = [bufs([64, 128], "art%d" % u) for u in range(4)]
    bkt_b = [bufs([64, 128], "bkt%d" % u) for u in range(4)]
    mts_b = [bufs([128, 128], "mts%d" % u) for u in range(4)]
    p_b = [bufs([64, 64], "p%d" % u, 3) for u in range(4)]
    q_b = [bufs([64, 64], "q%d" % u, 3) for u in range(4)]
    z_b = [bufs([64, 128], "z%d" % u, 3) for u in range(4)]
    ap_b = [bufs([64, 64], "ap%d" % u) for u in range(4)]
    phit_b = [bufs([64, 64], "phit%d" % u) for u in range(4)]
    dgl_b = [bufs([64, 64], "dgl%d" % u) for u in range(4)]
    psi_b = [bufs([64, 64], "psi%d" % u) for u in range(4)]
    rpt_b = [bufs([64, 64], "rpt%d" % u) for u in range(4)]
    h_b = [bufs([64, 64], "h%d" % u, 2) for u in range(4)]
    ysq_b = bufs([64, 256], "ysq"); st_b = bufs([64, 16], "st"); yn_b = bufs([64, 256], "yn")
    rk_b = bufs([64, 256], "rk"); ob_b = bufs([64, 256], "ob")
    pbanks = [P.psum([128, 512], F32, "r_ps%d" % i) for i in range(8)]
    half = Rot([(pbanks[b][:, 0:256], Res(excl=True)) for b in range(2)])
    pso_reg = (pbanks[2][:, 0:256], Res(excl=True))
    quar = Rot([(pbanks[b][:, 0:128], Res(excl=True)) for b in range(3, 8)])

    H = []
    for u in range(4):
        ht, hr = h_b[u].next()
        P.op("pool", lambda h, ht=ht: h.memset(ht[:], 0.0), writes=[hr])
        H.append((ht, hr))

    def mm(ps, psr, lhsT, rhs, reads, start=True, stop=True):
        P.op("pe", lambda h: h.matmul(ps, lhsT, rhs, start=start, stop=stop), reads=reads, writes=[psr])

    def acopy(dst, dstr, src, srcr):
        P.op("act", lambda h: h.copy(dst, src), reads=[srcr], writes=[dstr])

    for c in range(NCH):
        t0 = c * LC
        cur, curr = cur_b.next(); prv, prvr = prv_b.next()
        for hh in range(2):
            P.dma("sp", cur[hh * 64:(hh + 1) * 64, :], rkvp[1 + t0:1 + t0 + 64, :], writes=[curr])
            P.dma("pool", prv[hh * 64:(hh + 1) * 64, :], rkvp[t0:t0 + 64, :], writes=[prvr])
        xw, xwr = xw_b.next(); xa, xar = xa_b.next(); xg0, xg0r = xg0_b.next(); xg1, xg1r = xg1_b.next()
        P.dma("sp", xw[:], xwT[:, t0:t0 + 65], writes=[xwr])
        P.dma("sp", xa[:], xaT[:, t0:t0 + 65], writes=[xar])
        P.dma("pool", xg0[:], xgT[0:128, t0:t0 + 65], writes=[xg0r])
        P.dma("pool", xg1[:], xgT[128:160, t0:t0 + 65], writes=[xg1r])
        P.op("dve", lambda h: h.tensor_tensor(prv[:], prv[:], cur[:], ALU.subtract), reads=[prvr, curr], writes=[prvr])
        P.op("pool", lambda h: h.tensor_tensor(prv[:], prv[:], mu_b[:], ALU.mult), reads=[prvr, cr], writes=[prvr])
        P.op("dve", lambda h: h.tensor_tensor(prv[:], prv[:], cur[:], ALU.add), reads=[prvr, curr], writes=[prvr])
        zs, zsr = prv, prvr
        Rr, Kr, Vr = zs[:, 0:256], zs[:, 256:512], zs[:, 512:768]
        tw, twr = tw_b.next(); ta, tar = ta_b.next(); tg0, tg0r = tg0_b.next(); tg1, tg1r = tg1_b.next()
        for (x, xr, mu, np_, dst, dstr, fn) in ((xw, xwr, muw, 64, tw, twr, AF.Tanh), (xa, xar, mua, 64, ta, tar, AF.Copy),
                                                (xg0, xg0r, mug0, 128, tg0, tg0r, AF.Sigmoid),
                                                (xg1, xg1r, mug1, 32, tg1, tg1r, AF.Sigmoid)):
            tmp, tmpr = (sq_b.next())
            P.op("dve", lambda h, x=x, np_=np_, tmp=tmp: h.tensor_tensor(tmp[0:np_, 0:64], x[0:np_, 0:64], x[0:np_, 1:65],
                                                                         ALU.subtract), reads=[xr], writes=[tmpr])
            P.op("dve", lambda h, x=x, np_=np_, tmp=tmp, mu=mu: h.scalar_tensor_tensor(
                tmp[0:np_, 0:64], tmp[0:np_, 0:64], mu[0:np_, 0:1], x[0:np_, 1:65], ALU.mult, ALU.add),
                reads=[xr, tmpr, cr], writes=[tmpr])
            if fn == AF.Tanh:
                P.op("act", lambda h, dst=dst, np_=np_, tmp=tmp: h.activation(dst[0:np_, 0:64], tmp[0:np_, 0:64], AF.Sigmoid,
                                                                             scale=2.0), reads=[tmpr], writes=[dstr])
                P.op("dve", lambda h, dst=dst, np_=np_: h.tensor_scalar(dst[0:np_, 0:64], dst[0:np_, 0:64], 2.0, -1.0,
                                                                        ALU.mult, ALU.add), reads=[dstr], writes=[dstr])
            else:
                P.op("act", lambda h, dst=dst, np_=np_, tmp=tmp, fn=fn: h.activation(dst[0:np_, 0:64], tmp[0:np_, 0:64], fn),
                     reads=[tmpr], writes=[dstr])
            if dst is tw or dst is ta:
                P.op("pool", lambda h, dst=dst: h.tensor_copy(dst[:, 64:128], dst[:, 0:64]), reads=[dstr], writes=[dstr])
        sig, sigr = sig_b.next(); a, ar_ = a_b.next(); g, gr = g_b.next()
        ps, psr = half.next()
        mm(ps, psr, tw[:, :], wup[:, :], [twr, cr])
        P.op("dve", lambda h, ps=ps: h.tensor_tensor(sig[:], ps, W0, ALU.add), reads=[psr, cr], writes=[sigr])
        P.op("act", lambda h: h.activation(sig[:], sig[:], AF.Sigmoid), reads=[sigr], writes=[sigr])
        ps, psr = half.next()
        mm(ps, psr, ta[:, :], aup[:, :], [tar, cr])
        P.op("dve", lambda h, ps=ps: h.tensor_tensor(a[:], ps, A0, ALU.add), reads=[psr, cr], writes=[ar_])
        P.op("act", lambda h: h.activation(a[:], a[:], AF.Sigmoid), reads=[ar_], writes=[ar_])
        ps, psr = half.next()
        mm(ps[0:64, :], psr, tg0[:, :], gup0[:, :], [tg0r, cr], True, False)
        mm(ps[0:64, :], psr, tg1[:, :], gup1[:, :], [tg1r, cr], False, True)
        acopy(g[:], gr, ps[0:64, :], psr)
        kk, kkr = kk_b.next(); sq, sqr = sq_b.next(); ss, ssr = ss_b.next()
        P.op("pool", lambda h: h.tensor_tensor(kk[:], Kr, KK, ALU.mult), reads=[zsr, cr], writes=[kkr])
        P.op("pool", lambda h: h.tensor_tensor(sq[:], kk[:], kk[:], ALU.mult), reads=[kkr], writes=[sqr])
        P.op("dve", lambda h: h.tensor_reduce(ss[:, 0:4], sq[:, :].rearrange("p (u d) -> p u d", u=4), AX.X, ALU.add),
             reads=[sqr], writes=[ssr])
        P.op("dve", lambda h: h.tensor_scalar(ss[:, 0:4], ss[:, 0:4], 1e-24, None, ALU.add), reads=[ssr], writes=[ssr])
        P.op("act", lambda h: h.activation(ss[:, 0:4], ss[:, 0:4], AF.Ln), reads=[ssr], writes=[ssr])
        P.op("act", lambda h: h.activation(ss[:, 4:8], ss[:, 0:4], AF.Exp, scale=-0.5), reads=[ssr], writes=[ssr])
        for u in range(4):
            P.op("dve", lambda h, u=u: h.tensor_scalar(kk[:, u * 64:(u + 1) * 64], kk[:, u * 64:(u + 1) * 64],
                                                       ss[:, 4 + u:5 + u], None, ALU.mult), reads=[kkr, ssr], writes=[kkr])
        t1, t1r = t1_b.next(); bkr, bkrr = bkr_b.next()
        P.op("dve", lambda h: h.scalar_tensor_tensor(t1[:], a[:], -1.0, KA, ALU.add, ALU.mult), reads=[ar_, cr], writes=[t1r])
        P.op("dve", lambda h: h.scalar_tensor_tensor(t1[:], t1[:], 1.0, Kr, ALU.add, ALU.mult), reads=[t1r, zsr], writes=[t1r])
        P.op("pool", lambda h: h.tensor_tensor(bkr[0:64, :], kk[0:64, :], a[0:64, :], ALU.mult), reads=[kkr, ar_], writes=[bkrr])
        P.op("pool", lambda h: h.tensor_copy(bkr[64:128, :], t1[64:128, :]), reads=[t1r], writes=[bkrr])
        e1, e1r = e1_b.next(); e2, e2r = e2_b.next(); e3, e3r = e3_b.next(); glb, glbr = glb_b.next()
        for (TT, e, er, sc) in ((ct["TT1"], e1, e1r, 1.0), (ct["TT2"], e2, e2r, -1.0), (ct["TT3"], e3, e3r, 1.0)):
            ps, psr = half.next()
            mm(ps, psr, TT[:, :], sig[0:64, :], [cr, sigr])
            P.op("act", lambda h, e=e, ps=ps, sc=sc: h.activation(e[:], ps, AF.Exp, scale=sc), reads=[psr], writes=[er])
        ps, psr = half.next()
        for u in range(4):
            mm(ps[0:64, u * 64:(u + 1) * 64], psr, sig[0:64, u * 64:(u + 1) * 64], ct["negones"][:, :], [sigr, cr])
        P.op("act", lambda h, ps=ps: h.activation(glb[:], ps[0:64, :], AF.Exp), reads=[psr], writes=[glbr])
        ar, arr = ar_b.next(); bk, bkr_ = bk_b.next(); bkb, bkbr = bkb_b.next()
        P.op("dve", lambda h: h.scalar_tensor_tensor(ar[0:64, :], kk[0:64, :], -1.0, e1[0:64, :], ALU.mult, ALU.mult),
             reads=[kkr, e1r], writes=[arr])
        P.op("pool", lambda h: h.tensor_tensor(ar[64:128, :], Rr[64:128, :], e1[64:128, :], ALU.mult),
             reads=[zsr, e1r], writes=[arr])
        P.op("pool", lambda h: h.tensor_tensor(bk[:], bkr[:], e2[:], ALU.mult), reads=[bkrr, e2r], writes=[bkr_])
        P.op("dve", lambda h: h.tensor_tensor(bkb[:], bkr[:], e3[:], ALU.mult), reads=[bkrr, e3r], writes=[bkbr])
        UV = []
        for u in range(4):
            uv, uvr = uv_b[u].next()
            P.op("pool", lambda h, uv=uv, u=u: h.tensor_copy(uv[64:128, :], Vr[64:128, u * 64:(u + 1) * 64]),
                 reads=[zsr], writes=[uvr])
            UV.append((uv, uvr))
        U = [dict() for _ in range(4)]
        for u in range(4):
            cs = slice(u * 64, (u + 1) * 64)
            d = U[u]
            d["art"], d["artr"] = art_b[u].next(); d["bkt"], d["bktr"] = bkt_b[u].next()
            ps, psr = quar.next()
            P.op("pe", lambda h, ps=ps, cs=cs: h.transpose(ps[0:64, :], ar[:, cs], ct["ident"][:, :]), reads=[arr, cr], writes=[psr])
            acopy(d["art"][:], d["artr"], ps[0:64, :], psr)
            ps, psr = quar.next()
            P.op("pe", lambda h, ps=ps, cs=cs: h.transpose(ps[0:64, :], bk[:, cs], ct["ident"][:, :]), reads=[bkr_, cr], writes=[psr])
            P.op("dve", lambda h, ps=ps, d=d: h.tensor_copy(d["bkt"][:], ps[0:64, :]), reads=[psr], writes=[d["bktr"]])
        for u in range(4):
            d = U[u]
            d["mts"], d["mtsr"] = mts_b[u].next()
            ps, psr = quar.next()
            mm(ps, psr, d["bkt"][:, :], d["art"][:, :], [d["bktr"], d["artr"]])
            P.op("dve", lambda h, ps=ps, d=d: h.tensor_tensor(d["mts"][:], ps, ct["mask128"][:, :], ALU.mult),
                 reads=[psr, cr], writes=[d["mtsr"]])
            d["p"], d["pr"] = p_b[u].next()
            ps, psr = quar.next()
            mm(ps[0:64, 0:64], psr, d["art"][:, 0:64], d["bkt"][:, 0:64], [d["bktr"], d["artr"]])
            P.op("dve", lambda h, ps=ps, d=d: h.tensor_tensor(d["p"][:], ps[0:64, 0:64], ct["maskN"][:, :], ALU.mult),
                 reads=[psr, cr], writes=[d["pr"]])
        for u in range(4):
            cs = slice(u * 64, (u + 1) * 64)
            d = U[u]
            uv, uvr = UV[u]
            d["z"], d["zr"] = z_b[u].next()
            ps, psr = quar.next()
            mm(ps[0:64, 0:64], psr, d["mts"][64:128, 0:64], uv[64:128, :], [d["mtsr"], uvr])
            acopy(d["z"][:, 64:128], d["zr"], ps[0:64, 0:64], psr)
            P.op("pool", lambda h, d=d, cs=cs: h.tensor_copy(d["z"][:, 0:64], ar[0:64, cs]), reads=[arr], writes=[d["zr"]])
            d["q"], d["qr"] = d["mts"][0:64, 0:64], d["mtsr"]
        for lvl in range(6):
            for u in range(4):
                d = U[u]
                uv, uvr = UV[u]
                ps, psr = quar.next()
                mm(ps[0:64, :], psr, d["q"], d["z"][:, :], [d["qr"], d["zr"]])
                if lvl < 5:
                    zn, znr = z_b[u].next()
                    P.op("dve", lambda h, ps=ps, d=d, zn=zn: h.tensor_tensor(zn[:], ps[0:64, :], d["z"][:, :], ALU.add),
                         reads=[psr, d["zr"]], writes=[znr])
                    pn, pnr = p_b[u].next(); qn, qnr = q_b[u].next()
                    ps1, ps1r = quar.next()
                    mm(ps1[0:64, 0:64], ps1r, d["q"], d["p"][:, :], [d["qr"], d["pr"]])
                    acopy(pn[:], pnr, ps1[0:64, 0:64], ps1r)
                    ps2, ps2r = quar.next()
                    mm(ps2[0:64, 0:64], ps2r, d["p"][:, :], d["q"], [d["qr"], d["pr"]])
                    acopy(qn[:], qnr, ps2[0:64, 0:64], ps2r)
                    d["z"], d["zr"] = zn, znr
                    d["p"], d["pr"] = pn, pnr
                    d["q"], d["qr"] = qn[:, :], qnr
                else:
                    d["ap"], d["apr"] = ap_b[u].next()
                    P.op("dve", lambda h, ps=ps, d=d: h.tensor_tensor(d["ap"][:], ps[0:64, 0:64], d["z"][:, 0:64], ALU.add),
                         reads=[psr, d["zr"]], writes=[d["apr"]])
                    P.op("dve", lambda h, ps=ps, d=d, uv=uv: h.tensor_tensor(uv[0:64, :], ps[0:64, 64:128], d["z"][:, 64:128],
                                                                              ALU.add), reads=[psr, d["zr"]], writes=[uvr])
        pso, psor = pso_reg
        for u in range(4):
            cs = slice(u * 64, (u + 1) * 64)
            d = U[u]
            uv, uvr = UV[u]
            phit, phitr = phit_b[u].next(); dgl, dglr = dgl_b[u].next()
            P.op("pool", lambda h, dgl=dgl, cs=cs: h.tensor_tensor(dgl[:], ct["ident"][0:64, 0:64], glb[:, cs], ALU.mult),
                 reads=[cr, glbr], writes=[dglr])
            ps, psr = quar.next()
            mm(ps[0:64, 0:64], psr, d["ap"][:, :], bkb[0:64, cs], [d["apr"], bkbr])
            P.op("dve", lambda h, ps=ps, phit=phit, dgl=dgl: h.tensor_tensor(phit[:], ps[0:64, 0:64], dgl[:], ALU.add),
                 reads=[psr, dglr], writes=[phitr])
            psi, psir = psi_b[u].next()
            ps, psr = quar.next()
            mm(ps[0:64, 0:64], psr, bkb[:, cs], uv[:, :], [bkbr, uvr])
            acopy(psi[:], psir, ps[0:64, 0:64], psr)
            rpt, rptr = rpt_b[u].next()
            ps, psr = quar.next()
            mm(ps[0:64, 0:64], psr, d["ap"][:, :], d["mts"][0:64, 64:128], [d["apr"], d["mtsr"]])
            P.op("dve", lambda h, ps=ps, rpt=rpt, d=d: h.tensor_tensor(rpt[:], ps[0:64, 0:64], d["art"][:, 64:128], ALU.add),
                 reads=[psr, d["artr"]], writes=[rptr])
            ht, hr = H[u]
            mm(pso[0:64, cs], psor, rpt[:, :], ht[:, :], [rptr, hr], True, False)
            mm(pso[0:64, cs], psor, d["mts"][:, 64:128], uv[:, :], [d["mtsr"], uvr], False, True)
            ps, psr = quar.next()
            mm(ps[0:64, 0:64], psr, phit[:, :], ht[:, :], [phitr, hr])
            hn, hnr = h_b[u].next()
            P.op("dve", lambda h, ps=ps, hn=hn, psi=psi: h.tensor_tensor(hn[:], ps[0:64, 0:64], psi[:], ALU.add),
                 reads=[psr, psir], writes=[hnr])
            H[u] = (hn, hnr)
        ysq, ysqr = ysq_b.next(); st, str_ = st_b.next(); yn, ynr = yn_b.next(); rk, rkr = rk_b.next(); ob, obr = ob_b.next()
        y3 = pso[0:64, :].rearrange("p (u d) -> p u d", u=4)
        P.op("act", lambda h: h.activation(ysq[:], pso[0:64, :], AF.Square), reads=[psor], writes=[ysqr])
        P.op("dve", lambda h: h.tensor_reduce(st[:, 0:4], y3, AX.X, ALU.add), reads=[psor], writes=[str_])
        P.op("dve", lambda h: h.tensor_reduce(st[:, 4:8], ysq[:, :].rearrange("p (u d) -> p u d", u=4), AX.X, ALU.add),
             reads=[ysqr], writes=[str_])
        P.op("dve", lambda h: h.tensor_scalar(st[:, 0:8], st[:, 0:8], 1.0 / 64, None, ALU.mult), reads=[str_], writes=[str_])
        P.op("dve", lambda h: h.tensor_tensor(st[:, 8:12], st[:, 0:4], st[:, 0:4], ALU.mult), reads=[str_], writes=[str_])
        P.op("dve", lambda h: h.tensor_tensor(st[:, 8:12], st[:, 4:8], st[:, 8:12], ALU.subtract), reads=[str_], writes=[str_])
        P.op("dve", lambda h: h.tensor_scalar(st[:, 8:12], st[:, 8:12], 64e-5, None, ALU.add), reads=[str_], writes=[str_])
        P.op("act", lambda h: h.activation(st[:, 8:12], st[:, 8:12], AF.Ln), reads=[str_], writes=[str_])
        P.op("act", lambda h: h.activation(st[:, 8:12], st[:, 8:12], AF.Exp, scale=-0.5), reads=[str_], writes=[str_])
        for u in range(4):
            cs = slice(u * 64, (u + 1) * 64)
            P.op("dve", lambda h, u=u, cs=cs: h.tensor_scalar(yn[:, cs], pso[0:64, cs], st[:, u:u + 1], st[:, 8 + u:9 + u],
                                                             ALU.subtract, ALU.mult), reads=[psor, str_], writes=[ynr])
        P.op("pool", lambda h: h.tensor_tensor(yn[:], yn[:], LNW[0:64, :], ALU.mult), reads=[ynr, cr], writes=[ynr])
        P.op("pool", lambda h: h.tensor_tensor(yn[:], yn[:], LNB[0:64, :], ALU.add), reads=[ynr, cr], writes=[ynr])
        P.op("pool", lambda h: h.tensor_tensor(rk[:], Rr[0:64, :], t1[0:64, :], ALU.mult), reads=[zsr, t1r], writes=[rkr])
        P.op("pool", lambda h: h.tensor_tensor(rk[:], rk[:], RK[0:64, :], ALU.mult), reads=[rkr, cr], writes=[rkr])
        P.op("dve", lambda h: h.tensor_reduce(st[:, 12:16], rk[:, :].rearrange("p (u d) -> p u d", u=4), AX.X, ALU.add),
             reads=[rkr], writes=[str_])
        for u in range(4):
            cs = slice(u * 64, (u + 1) * 64)
            P.op("dve", lambda h, u=u, cs=cs: h.scalar_tensor_tensor(yn[:, cs], Vr[0:64, cs], st[:, 12 + u:13 + u], yn[:, cs],
                                                                    ALU.mult, ALU.add), reads=[zsr, str_, ynr], writes=[ynr])
        P.op("pool", lambda h: h.tensor_tensor(ob[:], yn[:], g[:], ALU.mult), reads=[ynr, gr], writes=[obr])
        P.dma("sp", out[t0:t0 + 64, :], ob[:], reads=[obr])

SB3 = 384


def ssd_consts():
    k = np.arange(128)[:, None]
    t = np.arange(128)[None, :]
    c = {"tri": (k <= t).astype(np.float32), "ones": np.ones((128, 128), np.float32),
         "maskneg": np.where(k <= t, 0.0, -30000.0).astype(np.float32), "ident": np.eye(128, dtype=np.float32)}
    return c


def ssd_stage(P, *, PN, xbcT, convw, convb, z, dtT, dtb, alog, dskip, normw, consts, out):
    NSB = PN // SB3
    assert PN % SB3 == 0
    sb = lambda shape, name, dt=F32: P.sbuf(shape, dt, "s_" + name)
    cr = Res()
    ct = {}
    for nm in ("tri", "ones", "maskneg", "ident"):
        ct[nm] = sb([128, 128], nm)
        P.dma("sp", ct[nm][:], consts[nm], writes=[cr])
    cw = sb([128, 48], "cw"); cb = sb([128, 12], "cb")
    P.dma("sp", cw[:], convw, writes=[cr]); P.dma("sp", cb[:], convb, writes=[cr])
    dtb_t = sb([16, 1], "dtb"); acol = sb([16, 1], "acol")
    P.dma("sp", dtb_t[:], dtb, writes=[cr]); P.dma("sp", acol[:], alog, writes=[cr])
    P.op("act", lambda h: h.activation(acol[:], acol[:], AF.Exp), reads=[cr], writes=[cr])
    P.op("dve", lambda h: h.tensor_scalar(acol[:], acol[:], -1.0, None, ALU.mult), reads=[cr], writes=[cr])
    dsk = sb([128, 1024], "dsk"); nw = sb([128, 1024], "nw")
    P.dma("sp", dsk[:], dskip[0:1, :].partition_broadcast(128), writes=[cr])
    P.dma("sp", nw[:], normw[0:1, :].partition_broadcast(128), writes=[cr])

    def bufs(shape, name, n=2, dt=F32):
        return Rot([(sb(shape, "%s%d" % (name, i), dt), Res()) for i in range(n)])
    xin_b = bufs([128, 12 * (SB3 + 3)], "xin")
    cv_b = bufs([128, 12 * SB3], "cv")
    ctmp_b = bufs([128, SB3], "ctmp", 3)
    dtin_b = bufs([16, SB3], "dtin"); dtf_b = bufs([16, 2 * SB3], "dtf")
    xtok_b = bufs([128, 1024], "xtok"); btok_b = bufs([128, 256], "btok", 2, BF16)
    bct_b = bufs([128, 4 * 128], "bct", 2, BF16)
    dta_b = bufs([128, 64], "dta")
    te_b = bufs([128, 32], "te")
    abc_b = bufs([128, 128], "abc", 3)
    seg_b = bufs([128, 128], "seg", 3)
    wj_b = bufs([128, 128], "wj", 3, BF16)
    ebc_b = bufs([128, 128], "ebc", 3)
    ctj_b = bufs([128, 128], "ctj", 3, BF16)
    cbt_b = bufs([128, 256], "cbt")
    xdt_b = bufs([128, 1024], "xdt", 2, BF16); xs_b = bufs([128, 1024], "xs", 2, BF16)
    z_b = bufs([128, 1024], "z"); yy_b = bufs([128, 1024], "yy"); sq_b = bufs([128, 1024], "sq", 1)
    st_b = bufs([128, 8], "st")
    state = sb([128, 1024], "state"); state_r = [Res(), Res()]
    stbf = sb([128, 1024], "statebf", BF16); stbf_r = [Res(), Res()]
    P.op("pool", lambda h: h.memset(state[:], 0.0), writes=state_r)
    P.op("pool", lambda h: h.memset(stbf[:], 0.0), writes=stbf_r)
    pb = [P.psum([128, 512], F32, "s_ps%d" % i) for i in range(8)]
    ybank = [(pb[0], Res(excl=True)), (pb[1], Res(excl=True))]
    cbank = Rot([(pb[2], Res(excl=True)), (pb[3], Res(excl=True))])
    mbank = Rot([(pb[4], Res(excl=True)), (pb[5], Res(excl=True))])
    sbank = Rot([(pb[6], Res(excl=True)), (pb[7], Res(excl=True))])

    def mm(ps, psr, lhsT, rhs, reads, start=True, stop=True):
        P.op("pe", lambda h: h.matmul(ps, lhsT, rhs, start=start, stop=stop), reads=reads, writes=[psr])

    for sbi in range(NSB):
        t0 = sbi * SB3
        xin, xinr = xin_b.next(); cv, cvr = cv_b.next()
        xin3 = xin[:, :].rearrange("p (k t) -> p k t", k=12)
        cv3 = cv[:, :].rearrange("p (k t) -> p k t", k=12)
        for hf in range(2):
            P.dma("sp" if hf == 0 else "pool", xin3[:, hf * 6:(hf + 1) * 6, :],
                  xbcT[hf * 768:(hf + 1) * 768, t0:t0 + SB3 + 3].rearrange("(k p) t -> p k t", p=128), writes=[xinr])
        for kc in range(12):
            tm, tmr = ctmp_b.next()
            P.op("dve", lambda h: h.tensor_scalar(tm[:], xin3[:, kc, 3:SB3 + 3], cw[:, kc * 4 + 3:kc * 4 + 4], cb[:, kc:kc + 1],
                                                  ALU.mult, ALU.add), reads=[xinr, cr], writes=[tmr])
            for j in (2, 1, 0):
                P.op("dve", lambda h: h.scalar_tensor_tensor(tm[:], xin3[:, kc, j:SB3 + j], cw[:, kc * 4 + j:kc * 4 + j + 1], tm[:],
                                                             ALU.mult, ALU.add), reads=[xinr, cr, tmr], writes=[tmr])
            P.op("act", lambda h: h.activation(cv3[:, kc, :], tm[:], AF.Silu), reads=[tmr], writes=[cvr])
        if sbi == 0:
            P.op("pool", lambda h: h.memset(cv3[:, :, 0:112], 0.0), writes=[cvr])
        dtin, dtinr = dtin_b.next(); dtf, dtfr = dtf_b.next()
        P.dma("sp", dtin[:], dtT[:, t0:t0 + SB3], writes=[dtinr])
        P.op("act", lambda h: h.activation(dtf[:, 0:SB3], dtin[:], AF.Exp, bias=dtb_t[:, 0:1]), reads=[dtinr, cr], writes=[dtfr])
        P.op("act", lambda h: h.activation(dtf[:, 0:SB3], dtf[:, 0:SB3], AF.Ln, bias=1.0), reads=[dtfr], writes=[dtfr])
        if sbi == 0:
            P.op("pool", lambda h: h.memset(dtf[:, 0:112], 0.0), writes=[dtfr])
        P.op("dve", lambda h: h.tensor_scalar(dtf[:, SB3:2 * SB3], dtf[:, 0:SB3], acol[:, 0:1], None, ALU.mult),
             reads=[dtfr, cr], writes=[dtfr])
        for ci in range(3):
            c0 = ci * 128
            tc0 = t0 + c0
            xtok, xtokr = xtok_b.next(); btok, btokr = btok_b.next(); bct, bctr = bct_b.next(); dta, dtar = dta_b.next()
            for kc in range(8):
                ps, psr = mbank.next()
                P.op("pe", lambda h: h.transpose(ps[:, 0:128], cv3[:, kc, c0:c0 + 128], ct["ident"][:, :]), reads=[cvr, cr], writes=[psr])
                if kc % 2 == 0:
                    P.op("act", lambda h: h.copy(xtok[:, kc * 128:(kc + 1) * 128], ps[:, 0:128]), reads=[psr], writes=[xtokr])
                else:
                    P.op("dve", lambda h: h.tensor_copy(xtok[:, kc * 128:(kc + 1) * 128], ps[:, 0:128]), reads=[psr], writes=[xtokr])
            for g in range(2):
                ps, psr = mbank.next()
                P.op("pe", lambda h: h.transpose(ps[:, 0:128], cv3[:, 8 + g, c0:c0 + 128], ct["ident"][:, :]), reads=[cvr, cr], writes=[psr])
                P.op("act", lambda h: h.copy(btok[:, g * 128:(g + 1) * 128], ps[:, 0:128]), reads=[psr], writes=[btokr])
            P.op("pool", lambda h: h.tensor_copy(bct[:, :].rearrange("p (k t) -> p k t", k=4), cv3[:, 8:12, c0:c0 + 128]),
                 reads=[cvr], writes=[bctr])
            ps, psr = mbank.next()
            for q in range(2):
                P.op("pe", lambda h: h.transpose(ps[:, q * 16:(q + 1) * 16], dtf[:, q * SB3 + c0:q * SB3 + c0 + 128], ct["ident"][0:16, 0:16]),
                     reads=[dtfr, cr], writes=[psr])
            P.op("dve", lambda h: h.tensor_copy(dta[:, 0:32], ps[:, 0:32]), reads=[psr], writes=[dtar])
            te, ter = te_b.next()
            ps, psr = mbank.next()
            mm(ps[:, 0:16], psr, ct["tri"][:, :], dta[:, 16:32], [cr, dtar])
            mm(ps[:, 16:32], psr, ct["ones"][:, :], dta[:, 16:32], [cr, dtar])
            P.op("dve", lambda h: h.tensor_copy(dta[:, 32:48], ps[:, 0:16]), reads=[psr], writes=[dtar])
            P.op("dve", lambda h: h.tensor_tensor(te[:, 0:16], ps[:, 16:32], dta[:, 32:48], ALU.subtract), reads=[psr, dtar], writes=[ter])
            P.op("act", lambda h: h.activation(te[:, 16:32], ps[:, 16:32], AF.Exp), reads=[psr], writes=[ter])
            P.op("act", lambda h: h.activation(te[:, 0:16], te[:, 0:16], AF.Exp), reads=[ter], writes=[ter])
            P.op("dve", lambda h: h.tensor_tensor(dta[:, 48:64], dta[:, 0:16], te[:, 0:16], ALU.mult), reads=[dtar, ter], writes=[dtar])
            xdt, xdtr = xdt_b.next(); xs, xsr = xs_b.next()
            for j in range(16):
                e = "dve" if j % 2 == 0 else "pool"
                P.op(e, lambda h: h.tensor_scalar(xdt[:, j * 64:(j + 1) * 64], xtok[:, j * 64:(j + 1) * 64], dta[:, j:j + 1], None, ALU.mult),
                     reads=[xtokr, dtar], writes=[xdtr])
                e = "pool" if j % 2 == 0 else "dve"
                P.op(e, lambda h: h.tensor_scalar(xs[:, j * 64:(j + 1) * 64], xtok[:, j * 64:(j + 1) * 64], dta[:, 48 + j:49 + j], None, ALU.mult),
                     reads=[xtokr, dtar], writes=[xsr])
            cbt, cbtr = cbt_b.next()
            for g in range(2):
                ps, psr = mbank.next()
                mm(ps[:, 0:128], psr, bct[:, g * 128:(g + 1) * 128], bct[:, (2 + g) * 128:(3 + g) * 128], [bctr])
                P.op("act", lambda h: h.copy(cbt[:, g * 128:(g + 1) * 128], ps[:, 0:128]), reads=[psr], writes=[cbtr])
            for j in range(16):
                g = j // 8
                yb, ybr = ybank[g]
                abc, abcr = abc_b.next(); seg, segr = seg_b.next(); wj, wjr = wj_b.next()
                ebc, ebcr = ebc_b.next(); ctj, ctjr = ctj_b.next()
                P.op("pool", lambda h: h.tensor_scalar(abc[:], ct["ones"][:, :], dta[:, 16 + j:17 + j], None, ALU.mult),
                     reads=[cr, dtar], writes=[abcr])
                ps, psr = cbank.next()
                mm(ps[:, 0:128], psr, abc[:, :], ct["tri"][:, :], [abcr, cr])
                P.op("dve", lambda h: h.scalar_tensor_tensor(seg[:], ps[:, 0:128], dta[:, 32 + j:33 + j], ct["maskneg"][:, :],
                                                             ALU.subtract, ALU.add), reads=[psr, dtar, cr], writes=[segr])
                P.op("act", lambda h: h.activation(ebc[:], ps[:, 0:128], AF.Exp), reads=[psr], writes=[ebcr])
                P.op("act", lambda h: h.activation(seg[:], seg[:], AF.Exp), reads=[segr], writes=[segr])
                P.op("dve", lambda h: h.tensor_tensor(wj[:], seg[:], cbt[:, g * 128:(g + 1) * 128], ALU.mult), reads=[segr, cbtr], writes=[wjr])
                P.op("pool", lambda h: h.tensor_tensor(ctj[:], ebc[:], bct[:, (2 + g) * 128:(3 + g) * 128], ALU.mult),
                     reads=[ebcr, bctr], writes=[ctjr])
                jj = j % 8
                mm(yb[:, jj * 64:(jj + 1) * 64], ybr, wj[:, :], xdt[:, j * 64:(j + 1) * 64], [wjr, xdtr], True, False)
                mm(yb[:, jj * 64:(jj + 1) * 64], ybr, ctj[:, :], stbf[:, j * 64:(j + 1) * 64], [ctjr, stbf_r[g]], False, True)
            for g in range(2):
                ps, psr = sbank.next()
                mm(ps[:, :], psr, btok[:, g * 128:(g + 1) * 128], xs[:, g * 512:(g + 1) * 512], [btokr, xsr])
                for jj in range(8):
                    j = g * 8 + jj
                    P.op("dve", lambda h: h.scalar_tensor_tensor(state[:, j * 64:(j + 1) * 64], state[:, j * 64:(j + 1) * 64],
                                                                 te[:, 16 + j:17 + j], ps[:, jj * 64:(jj + 1) * 64], ALU.mult, ALU.add),
                         reads=[psr, ter, state_r[g]], writes=[state_r[g]])
                P.op("act", lambda h: h.copy(stbf[:, g * 512:(g + 1) * 512], state[:, g * 512:(g + 1) * 512]),
                     reads=[state_r[g]], writes=[stbf_r[g]])
            zt, ztr = z_b.next(); yy, yyr = yy_b.next(); sq, sqr = sq_b.next(); st, str_ = st_b.next()
            P.dma("sp", zt[:], z[tc0:tc0 + 128, :], writes=[ztr])
            P.op("pool", lambda h: h.tensor_tensor(yy[:], xtok[:], dsk[:], ALU.mult), reads=[xtokr, cr], writes=[yyr])
            for g in range(2):
                yb, ybr = ybank[g]
                P.op("dve", lambda h: h.tensor_tensor(yy[:, g * 512:(g + 1) * 512], yy[:, g * 512:(g + 1) * 512], yb[:, :], ALU.add),
                     reads=[ybr, yyr], writes=[yyr])
            P.op("act", lambda h: h.activation(zt[:], zt[:], AF.Silu), reads=[ztr], writes=[ztr])
            P.op("pool", lambda h: h.tensor_tensor(yy[:], yy[:], zt[:], ALU.mult), reads=[yyr, ztr], writes=[yyr])
            P.op("act", lambda h: h.activation(sq[:], yy[:], AF.Square), reads=[yyr], writes=[sqr])
            P.op("dve", lambda h: h.tensor_reduce(st[:, 0:2], sq[:, :].rearrange("p (g c) -> p g c", g=2), AX.X, ALU.add),
                 reads=[sqr], writes=[str_])
            P.op("dve", lambda h: h.tensor_scalar(st[:, 0:2], st[:, 0:2], 1.0 / 512, 1e-5, ALU.mult, ALU.add), reads=[str_], writes=[str_])
            P.op("act", lambda h: h.activation(st[:, 0:2], st[:, 0:2], AF.Ln), reads=[str_], writes=[str_])
            P.op("act", lambda h: h.activation(st[:, 2:4], st[:, 0:2], AF.Exp, scale=-0.5), reads=[str_], writes=[str_])
            for g in range(2):
                P.op("dve", lambda h: h.scalar_tensor_tensor(yy[:, g * 512:(g + 1) * 512], yy[:, g * 512:(g + 1) * 512],
                                                             st[:, 2 + g:3 + g], nw[:, g * 512:(g + 1) * 512], ALU.mult, ALU.mult),
                     reads=[yyr, str_, cr], writes=[yyr])
            P.dma("sp", out[tc0:tc0 + 128, :], yy[:], reads=[yyr])

D_MODEL = 2048
FFN = 5632
_PROGS = {}


def _add_barrier(P):
    toks = []
    for q, sls in P.slots.items():
        for sl in sls:
            if sl.uses > 0:
                toks.append((sl.key, sl.sem, 16 * sl.uses))
    for q in ("sp", "pool", "act"):
        for t in toks:
            P._wait(P.E[q], t)


def _build_tp(N, Kmix, Cnext, first):
    key = ("tp", N, Kmix, Cnext, first)
    if key in _PROGS:
        return _PROGS[key]
    P = Prog()
    d = lambda n, s: P.dram(n, s, F32, "ExternalInput")
    L = LinCtx(P)
    resT = d("resT", [D_MODEL, N])
    mask = d("mask", [1, N])
    if not first:
        mixT = d("mixT", [Kmix, N])
        Wout = d("Wout", [16, 128, Kmix])
        gam_f = d("gam_f", [128, 16])
        Wup = d("Wup", [88, 128, D_MODEL])
        cw = d("cw", [128, 44 * 3])
        cb = d("cb", [128, 44])
        Wdown = d("Wdown", [16, 128, FFN])
        res1 = P.dram("res1", [D_MODEL, N], F32, "Internal")
        up = P.dram("up", [2 * FFN, N], F32, "Internal")
        res2 = P.dram("res2", [D_MODEL, N], F32, "ExternalOutput")
        linear_stage(L, mode="plain", K=Kmix, C=D_MODEL, N=N, W=Wout, dst=res1, src=mixT, res=resT)
        _add_barrier(P)
        linear_stage(L, mode="norm", K=D_MODEL, C=2 * FFN, N=N, W=Wup, dst=up, src=res1, gamma=gam_f, mask=mask)
        _add_barrier(P)
        linear_stage(L, mode="convglu", K=FFN, C=D_MODEL, N=N, W=Wdown, dst=res2, src=up[0:FFN, :], src2=up[FFN:2 * FFN, :],
                     convw=cw, convb=cb, res=res1)
        src_next = res2
    else:
        src_next = resT
    if Cnext:
        gam_n = d("gam_n", [128, 16])
        CBn = (Cnext + 127) // 128
        Win = d("Win", [CBn, 128, D_MODEL])
        znext = P.dram("znext", [Cnext, N], F32, "ExternalOutput")
        if not first:
            _add_barrier(P)
        linear_stage(L, mode="norm", K=D_MODEL, C=Cnext, N=N, W=Win, dst=znext, src=src_next, gamma=gam_n, mask=mask)
    nc = P.finalize()
    _PROGS[key] = nc
    return nc


def _build_attn(PN):
    key = ("attn", PN)
    if key in _PROGS:
        return _PROGS[key]
    P = Prog()
    d = lambda n, s: P.dram(n, s, F32, "ExternalInput")
    qT = d("qT", [64, 4, PN]); kT = d("kT", [64, PN]); vv = d("v", [PN, 64])
    qwd = d("qw", [64, 1]); kwd = d("kw", [64, 1]); sk = d("sinks", [1, 4]); mk = d("masks", [4, 128, 512])
    out = P.dram("out", [PN, 256], F32, "ExternalOutput")
    attn_stage(P, PN=PN, qT=qT, kT=kT, v=vv, qw=qwd, kw=kwd, sinks=sk, masks=mk, out=out)
    nc = P.finalize()
    _PROGS[key] = nc
    return nc


def _build_rwkv(PN):
    key = ("rwkv", PN)
    if key in _PROGS:
        return _PROGS[key]
    P = Prog()
    d = lambda n, s: P.dram(n, s, F32, "ExternalInput")
    C = rwkv_consts()
    rkvp = d("rkvp", [PN + 1, 768]); xwT = d("xwT", [64, PN + 1]); xaT = d("xaT", [64, PN + 1]); xgT = d("xgT", [160, PN + 1])
    mu_rkv = d("mu_rkv", [1, 768]); mu_w = d("mu_w", [64, 1]); mu_a = d("mu_a", [64, 1]); mu_g = d("mu_g", [160, 1])
    rows = d("rows", [7, 256]); wup = d("w_up", [64, 256]); aup = d("a_up", [64, 256]); gup = d("g_up", [160, 256])
    consts = {k: d("c_" + k, list(v.shape)) for k, v in C.items()}
    out = P.dram("out", [PN, 256], F32, "ExternalOutput")
    rwkv_stage(P, PN=PN, rkvp=rkvp, xwT=xwT, xaT=xaT, xgT=xgT, mu_rkv=mu_rkv, mu_w=mu_w, mu_a=mu_a, mu_g=mu_g, rows=rows,
               w_up=wup, a_up=aup, g_up=gup, consts=consts, out=out)
    nc = P.finalize()
    _PROGS[key] = nc
    return nc


def _build_ssd(PN):
    key = ("ssd", PN)
    if key in _PROGS:
        return _PROGS[key]
    P = Prog()
    d = lambda n, s: P.dram(n, s, F32, "ExternalInput")
    C = ssd_consts()
    xbcT = d("xbcT", [1536, PN + 3]); convw = d("convw", [128, 48]); convb = d("convb", [128, 12]); z = d("z", [PN, 1024])
    dtT = d("dtT", [16, PN]); dtb = d("dtb", [16, 1]); alog = d("alog", [16, 1]); dskip = d("dskip", [1, 1024])
    normw = d("normw", [1, 1024])
    consts = {k: d("c_" + k, [128, 128]) for k in C}
    out = P.dram("out", [PN, 1024], F32, "ExternalOutput")
    ssd_stage(P, PN=PN, xbcT=xbcT, convw=convw, convb=convb, z=z, dtT=dtT, dtb=dtb, alog=alog, dskip=dskip, normw=normw,
              consts=consts, out=out)
    nc = P.finalize()
    _PROGS[key] = nc
    return nc


def _c(a):
    return np.ascontiguousarray(a, dtype=np.float32)


def _gam(g):
    return _c(g.reshape(16, 128).T)


def _launch(nc, maps):
    res = run_bass_kernel_spmd(nc, maps, core_ids=list(range(8)))
    return res.results


def kernel(x, meta_tokens, mix_norm_w, ffn_norm_w, ar_w_in, ar_shift_mu, attn_q_norm_w, attn_k_norm_w, attn_sinks,
           rwkv_w0, rwkv_w_up, rwkv_a0, rwkv_a_up, rwkv_g_up, rwkv_k_k, rwkv_k_a, rwkv_r_k, rwkv_ln_w, rwkv_ln_b,
           ar_w_out, ssd_w_in, ssd_conv_w, ssd_conv_b, ssd_dt_bias, ssd_a_log, ssd_d, ssd_norm_w, ssd_w_out,
           ffn_w_up, ffn_conv_w, ffn_conv_b, ffn_w_down):
    f = lambda a: np.asarray(a, dtype=np.float32)
    x = f(x)
    B, SEQ, D = x.shape
    depth = mix_norm_w.shape[0]
    PN = SEQ + 128
    TOT = B * PN
    stride = TOT // 8
    assert stride * 8 == TOT
    N = -(-(stride + 2) // 704) * 704
    H = N - stride

    def windows_T(glob):
        Cc = glob.shape[1]
        outs = []
        for c in range(8):
            lo = c * stride - H
            w = np.zeros((Cc, N), np.float32)
            s0 = max(lo, 0)
            w[:, s0 - lo:] = glob[s0:lo + N].T
            outs.append(w)
        return outs

    def unwindow(outs, name):
        Cc = outs[0][name].shape[0]
        glob = np.empty((TOT, Cc), np.float32)
        for c in range(8):
            glob[c * stride:(c + 1) * stride] = outs[c][name][:, H:].T
        return glob

    res_glob = np.zeros((TOT, D), np.float32)
    valid = np.zeros((TOT, 1), np.float32)
    for b in range(B):
        res_glob[b * PN + 112:b * PN + 128] = f(meta_tokens)
        res_glob[b * PN + 128:(b + 1) * PN] = x[b]
        valid[b * PN + 112:(b + 1) * PN] = 1.0
    mask_w = [_c(w) for w in windows_T(valid)]

    def in_proj_weights(layer):
        i = layer // 2
        if layer % 2 == 0:
            return tile_w(f(ar_w_in[i])), ar_w_in.shape[2]
        return tile_w(f(ssd_w_in[i])), ssd_w_in.shape[2]

    Win, Cn = in_proj_weights(0)
    nc = _build_tp(N, 0, Cn, True)
    res_w = windows_T(res_glob)
    gam_n = _gam(f(mix_norm_w[0]))
    outs = _launch(nc, [{"resT": res_w[c], "mask": mask_w[c], "gam_n": gam_n, "Win": Win} for c in range(8)])
    z_glob = unwindow(outs, "znext")
    del Win

    amasks = attn_masks()
    rconsts = rwkv_consts()
    sconsts = ssd_consts()
    for layer in range(depth):
        i = layer // 2
        if layer % 2 == 0:
            nc = _build_attn(PN)
            maps = []
            for c in range(8):
                b, g = c // 4, c % 4
                zb = z_glob[b * PN:(b + 1) * PN]
                maps.append({"qT": _c(zb[:, g * 256:(g + 1) * 256].reshape(PN, 4, 64).transpose(2, 1, 0)),
                             "kT": _c(zb[:, 1024 + g * 64:1024 + (g + 1) * 64].T),
                             "v": _c(zb[:, 1280 + g * 64:1280 + (g + 1) * 64]),
                             "qw": _c(f(attn_q_norm_w[i])[:, None]), "kw": _c(f(attn_k_norm_w[i])[:, None]),
                             "sinks": _c(f(attn_sinks[i])[None, 4 * g:4 * g + 4]), "masks": amasks})
            outs = _launch(nc, maps)
            mix_glob = np.empty((TOT, 2048), np.float32)
            for c in range(8):
                b, g = c // 4, c % 4
                mix_glob[b * PN:(b + 1) * PN, g * 256:(g + 1) * 256] = outs[c]["out"]
            nc = _build_rwkv(PN)
            mu = f(ar_shift_mu[i])
            w0, a0, k_k, k_a = f(rwkv_w0[i]), f(rwkv_a0[i]), f(rwkv_k_k[i]), f(rwkv_k_a[i])
            r_k, ln_w, ln_b = f(rwkv_r_k[i]).reshape(-1), f(rwkv_ln_w[i]), f(rwkv_ln_b[i])
            w_up, a_up, g_up = f(rwkv_w_up[i]), f(rwkv_a_up[i]), f(rwkv_g_up[i])
            maps = []
            for c in range(8):
                b, u = c // 4, c % 4
                z0 = z_glob[b * PN:(b + 1) * PN, 1536:]
                cs = slice(256 * u, 256 * u + 256)
                rkv = np.concatenate([z0[:, 0:1024][:, cs], z0[:, 1024:2048][:, cs], z0[:, 2048:3072][:, cs]], 1)
                m = {"rkvp": np.concatenate([np.zeros((1, 768), np.float32), rkv], 0),
                     "xwT": np.concatenate([np.zeros((64, 1), np.float32), z0[:, 3072:3136].T], 1),
                     "xaT": np.concatenate([np.zeros((64, 1), np.float32), z0[:, 3136:3200].T], 1),
                     "xgT": np.concatenate([np.zeros((160, 1), np.float32), z0[:, 3200:3360].T], 1),
                     "mu_rkv": np.concatenate([mu[0:1024][cs], mu[1024:2048][cs], mu[2048:3072][cs]])[None, :],
                     "mu_w": mu[3072:3136, None], "mu_a": mu[3136:3200, None], "mu_g": mu[3200:3360, None],
                     "rows": np.stack([w0[cs], a0[cs], k_k[cs], k_a[cs], r_k[cs], ln_w[cs], ln_b[cs]]),
                     "w_up": w_up[:, cs], "a_up": a_up[:, cs], "g_up": g_up[:, cs]}
                for k, v in rconsts.items():
                    m["c_" + k] = v
                maps.append({k: _c(v) for k, v in m.items()})
            outs = _launch(nc, maps)
            for c in range(8):
                b, u = c // 4, c % 4
                mix_glob[b * PN:(b + 1) * PN, 1024 + u * 256:1024 + (u + 1) * 256] = outs[c]["out"]
            Wout = tile_w(f(ar_w_out[i]))
            Kmix = 2048
        else:
            nc = _build_ssd(PN)
            conv_w, conv_b = f(ssd_conv_w[i]), f(ssd_conv_b[i])
            dt_bias, a_log, d_skip, norm_w = f(ssd_dt_bias[i]), f(ssd_a_log[i]), f(ssd_d[i]), f(ssd_norm_w[i])
            maps = []
            for c in range(8):
                b, cidx = c // 4, c % 4
                z0 = z_glob[b * PN:(b + 1) * PN]
                g0 = 2 * cidx
                xs_ = slice(4096 + g0 * 512, 4096 + g0 * 512 + 1024)
                bs_ = slice(8192 + g0 * 128, 8192 + g0 * 128 + 256)
                cs_ = slice(9216 + g0 * 128, 9216 + g0 * 128 + 256)
                chan = np.concatenate([np.arange(4096)[g0 * 512:g0 * 512 + 1024], 4096 + np.arange(1024)[g0 * 128:g0 * 128 + 256],
                                       5120 + np.arange(1024)[g0 * 128:g0 * 128 + 256]])
                xbc = np.concatenate([z0[:, xs_], z0[:, bs_], z0[:, cs_]], 1)
                hs = slice(g0 * 8, g0 * 8 + 16)
                m = {"xbcT": np.concatenate([np.zeros((1536, 3), np.float32), xbc.T], 1),
                     "convw": conv_w[:, chan].T.reshape(12, 128, 4).transpose(1, 0, 2).reshape(128, 48),
                     "convb": conv_b[chan].reshape(12, 128).T,
                     "z": z0[:, g0 * 512:g0 * 512 + 1024],
                     "dtT": z0[:, 10240 + g0 * 8:10240 + g0 * 8 + 16].T, "dtb": dt_bias[hs, None], "alog": a_log[hs, None],
                     "dskip": np.repeat(d_skip[hs], 64)[None, :], "normw": norm_w[g0 * 512:g0 * 512 + 1024][None, :]}
                for k, v in sconsts.items():
                    m["c_" + k] = v
                maps.append({k: _c(v) for k, v in m.items()})
            outs = _launch(nc, maps)
            mix_glob = np.empty((TOT, 4096), np.float32)
            for c in range(8):
                b, cidx = c // 4, c % 4
                mix_glob[b * PN:(b + 1) * PN, cidx * 1024:(cidx + 1) * 1024] = outs[c]["out"]
            Wout = tile_w(f(ssd_w_out[i]))
            Kmix = 4096
        del z_glob, outs
        last = (layer == depth - 1)
        if not last:
            Win, Cn = in_proj_weights(layer + 1)
        else:
            Win, Cn = None, 0
        nc = _build_tp(N, Kmix, Cn, False)
        mix_w = windows_T(mix_glob)
        res_w = windows_T(res_glob)
        del mix_glob
        cwv = f(ffn_conv_w[layer])
        shared = {"Wout": Wout, "gam_f": _gam(f(ffn_norm_w[layer])), "Wup": tile_w(f(ffn_w_up[layer])),
                  "cw": _c(cwv.T.reshape(44, 128, 3).transpose(1, 0, 2).reshape(128, 132)),
                  "cb": _c(f(ffn_conv_b[layer]).reshape(44, 128).T), "Wdown": tile_w(f(ffn_w_down[layer]))}
        if not last:
            shared["gam_n"] = _gam(f(mix_norm_w[layer + 1]))
            shared["Win"] = Win
        maps = []
        for c in range(8):
            m = dict(shared)
            m["mixT"] = mix_w[c]
            m["resT"] = res_w[c]
            m["mask"] = mask_w[c]
            maps.append(m)
        outs = _launch(nc, maps)
        del maps, shared, mix_w, res_w, Wout, Win
        res_glob = unwindow(outs, "res2")
        if not last:
            z_glob = unwindow(outs, "znext")
        del outs
    out = np.empty((B, SEQ, D), np.float32)
    for b in range(B):
        out[b] = res_glob[b * PN + 128:(b + 1) * PN]
    return out
```

```python
from contextlib import ExitStack
import numpy as np
import concourse.bass as bass
import concourse.mybir as mybir
from concourse.bass_utils import run_bass_kernel_spmd

F32 = mybir.dt.float32
BF16 = mybir.dt.bfloat16
AF = mybir.ActivationFunctionType
ALU = mybir.AluOpType
AX = mybir.AxisListType


class Res:
    __slots__ = ("w", "r", "excl")

    def __init__(self, excl=False):
        self.w = None
        self.r = {}
        self.excl = excl


class _Eng:
    def __init__(self, name, sem):
        self.name = name
        self.sem = sem
        self.count = 0
        self.waited = {}
        self.ops = []


class _Rec:
    def __init__(self):
        self.call = None

    def __getattr__(self, name):
        def f(*args, **kwargs):
            self.call = (name, args, kwargs)
            return self
        return f


class _Slot:
    def __init__(self, key, sem):
        self.key = key
        self.sem = sem
        self.uses = 0


class Prog:
    ENGS = ("sp", "act", "dve", "pool", "pe")

    def __init__(self, nslots=6, self_sync=True):
        self.nc = bass.Bass("TRN2", target_bir_lowering=False)
        self.es = ExitStack()
        self.self_sync = self_sync
        self.E = {}
        for n in self.ENGS:
            self.E[n] = _Eng(n, self.es.enter_context(self.nc.semaphore("s_" + n)))
        self.slots = {}
        self.slot_rr = {}
        for q in ("sp", "pool", "act"):
            self.slots[q] = [_Slot("d_%s%d" % (q, i), self.es.enter_context(self.nc.semaphore("d_%s%d" % (q, i))))
                             for i in range(nslots)]
            self.slot_rr[q] = 0
        self._n = 0

    def dram(self, name, shape, dtype, kind):
        return self.nc.dram_tensor(name, list(shape), dtype, kind=kind).ap()

    def sbuf(self, shape, dtype, name=None):
        self._n += 1
        return self.es.enter_context(self.nc.sbuf_tensor(name or "sb%d" % self._n, list(shape), dtype))

    def psum(self, shape, dtype, name=None):
        self._n += 1
        return self.es.enter_context(self.nc.psum_tensor(name or "ps%d" % self._n, list(shape), dtype))

    def _wait(self, E, tok):
        key, sem, val = tok
        if E.waited.get(key, 0) >= val:
            return
        E.waited[key] = val
        E.ops.append(lambda h, sem=sem, val=val: h.wait_ge(sem, val))

    def _deps(self, E, reads, writes, ekey):
        toks = []
        for r in reads:
            if r.w is not None:
                toks.append(r.w)
        for w in writes:
            if w.w is not None:
                toks.append(w.w)
            for k, t in w.r.items():
                if k == ekey:
                    continue
                toks.append(t)
        for t in toks:
            if t[0] == ekey and (E.name == "pe" or not self.self_sync):
                continue
            self._wait(E, t)

    def op(self, eng, fn, reads=(), writes=()):
        E = self.E[eng]
        ekey = "s_" + eng
        ex = [r for r in reads if r.excl]
        if ex:
            writes = list(writes) + ex
            reads = [r for r in reads if not r.excl]
        self._deps(E, reads, writes, ekey)
        E.count += 1
        tok = (ekey, E.sem, E.count)
        rec = _Rec()
        fn(rec)
        E.ops.append(lambda h, c=rec.call, sem=E.sem: getattr(h, c[0])(*c[1], **c[2]).then_inc(sem, 1))
        E.waited[ekey] = max(E.waited.get(ekey, 0), 0)
        for w in writes:
            w.w = tok
            w.r = {}
        for r in reads:
            r.r[ekey] = tok
        return tok

    def dma(self, q, out, in_, reads=(), writes=()):
        E = self.E[q]
        sl = self.slots[q][self.slot_rr[q] % len(self.slots[q])]
        self.slot_rr[q] += 1
        if sl.uses > 0:
            self._wait(E, (sl.key, sl.sem, 16 * sl.uses))
        self._deps(E, reads, writes, None)
        sl.uses += 1
        tok = (sl.key, sl.sem, 16 * sl.uses)
        E.ops.append(lambda h, out=out, in_=in_, sem=sl.sem: h.dma_start(out=out, in_=in_).then_inc(sem, 16))
        for w in writes:
            w.w = tok
            w.r = {}
        for r in reads:
            r.r[sl.key] = tok
        return tok

    def finalize(self):
        for q, sls in self.slots.items():
            for sl in sls:
                if sl.uses > 0:
                    self._wait(self.E[q], (sl.key, sl.sem, 16 * sl.uses))
        nc = self.nc
        with nc.Block() as block:
            decos = {"sp": block.sync, "act": block.scalar, "dve": block.vector,
                     "pool": block.gpsimd, "pe": block.tensor}
            for n in self.ENGS:
                ops = self.E[n].ops
                if not ops:
                    continue

                def body(h, ops=ops):
                    for f in ops:
                        f(h)
                decos[n](body)
        self.es.close()
        return nc

    def ninstr(self):
        return {n: len(self.E[n].ops) for n in self.ENGS}

NT = 352


class LinCtx:
    def __init__(self, P, KCmax_small=16, TSmax=1408, KCmax=44, opnd_elems=44 * 704):
        self.P = P
        self.opnd = P.sbuf([128, opnd_elems], BF16, "opnd")
        self.opnd_r = [Res() for _ in range(KCmax)]
        self.NST = 3
        self.stg = [P.sbuf([128, TSmax + 2], F32, "stg%d" % i) for i in range(self.NST)]
        self.stg_r = [Res() for _ in range(self.NST)]
        self.stg_i = 0
        self.stg2 = [P.sbuf([128, 704], F32, "stgv%d" % i) for i in range(2)]
        self.stg2_r = [Res() for _ in range(2)]
        self.stg2_i = 0
        self.tmp = [P.sbuf([128, TSmax], F32, "tmp%d" % i) for i in range(2)]
        self.tmp_r = [Res() for _ in range(2)]
        self.tmp_i = 0
        self.sq = [P.sbuf([128, TSmax], BF16, "sq%d" % i) for i in range(2)]
        self.sq_r = [Res() for _ in range(2)]
        self.rstd = P.sbuf([128, TSmax], F32, "rstd")
        self.rstd_r = Res()
        self.maskb = P.sbuf([128, TSmax], F32, "maskb")
        self.maskb_r = Res()
        self.wst = [P.sbuf([128, KCmax * 128], F32, "wst%d" % i) for i in range(2)]
        self.wst_r = [Res() for _ in range(2)]
        self.wbf = [P.sbuf([128, KCmax * 128], BF16, "wbf%d" % i) for i in range(2)]
        self.wbf_r = [[Res(), Res()] for _ in range(2)]
        self.w_i = 0
        self.osb = [P.sbuf([128, TSmax], F32, "osb%d" % i) for i in range(2)]
        self.osb_r = [[Res() for _ in range(4)] for _ in range(2)]
        self.rsb = [P.sbuf([128, TSmax], F32, "rsb%d" % i) for i in range(2)]
        self.rsb_r = [Res() for _ in range(2)]
        self.o_i = 0
        self.ps = [P.psum([128, 512], F32, "lps%d" % i) for i in range(8)]
        self.ps_r = [Res(excl=True) for _ in range(8)]
        self.ps_i = 0
        self.ones = P.sbuf([128, 128], BF16, "ones_bf")
        self.ones_r = Res()
        P.op("dve", lambda h: h.memset(self.ones[:], 1.0), writes=[self.ones_r])
        self.small = {}

    def const(self, name, shape, src_ap, q="sp"):
        t = self.P.sbuf(shape, F32, name)
        r = Res()
        self.P.dma(q, t[:], src_ap, writes=[r])
        return t, r


def linear_stage(L, *, mode, K, C, N, W, dst, src=None, src2=None, gamma=None, mask=None,
                 convw=None, convb=None, res=None, eps=1e-6):
    P = L.P
    KC = K // 128
    CB = (C + 127) // 128
    TS = 1408 if (K == 2048 and N % 1408 == 0) else 704
    assert N % TS == 0
    nsub = TS // NT
    opnd = L.opnd
    opv = opnd[:, 0:KC * TS].rearrange("p (k t) -> p k t", k=KC)

    if mode == "norm":
        gam_t, gam_r = L.const("gam%d" % id(gamma), [128, KC], gamma)
    if mode == "convglu":
        cw_t, cw_r = L.const("cw%d" % id(convw), [128, KC * 3], convw)
        cb_t, cb_r = L.const("cb%d" % id(convb), [128, KC], convb)

    for ts in range(N // TS):
        t0 = ts * TS
        if mode == "plain":
            for kc in range(KC):
                i = L.stg_i % L.NST
                L.stg_i += 1
                st, sr = L.stg[i], L.stg_r[i]
                P.dma("sp", st[:, 0:TS], src[kc * 128:(kc + 1) * 128, t0:t0 + TS], writes=[sr])
                if kc % 2 == 0:
                    P.op("dve", lambda h, st=st, kc=kc: h.tensor_copy(opv[:, kc, :], st[:, 0:TS]),
                         reads=[sr], writes=[L.opnd_r[kc]])
                else:
                    P.op("act", lambda h, st=st, kc=kc: h.copy(opv[:, kc, :], st[:, 0:TS]),
                         reads=[sr], writes=[L.opnd_r[kc]])
        elif mode == "norm":
            P.dma("pool", L.maskb[:, 0:TS], mask[0:1, t0:t0 + TS].partition_broadcast(128), writes=[L.maskb_r])
            banks = []
            for s in range(nsub):
                b = L.ps_i % 8
                L.ps_i += 1
                banks.append(b)
            for kc in range(KC):
                i = L.stg_i % L.NST
                L.stg_i += 1
                st, sr = L.stg[i], L.stg_r[i]
                P.dma("sp", st[:, 0:TS], src[kc * 128:(kc + 1) * 128, t0:t0 + TS], writes=[sr])
                sq, sqr = L.sq[kc % 2], L.sq_r[kc % 2]
                P.op("act", lambda h, st=st, sq=sq: h.activation(sq[:, 0:TS], st[:, 0:TS], AF.Square),
                     reads=[sr], writes=[sqr])
                for s in range(nsub):
                    b = banks[s]
                    P.op("pe", lambda h, b=b, sq=sq, s=s, kc=kc: h.matmul(
                        L.ps[b][:, 0:NT], L.ones[:, :], sq[:, s * NT:(s + 1) * NT],
                        start=(kc == 0), stop=(kc == KC - 1)),
                        reads=[sqr, L.ones_r], writes=[L.ps_r[b]])
            for s in range(nsub):
                b = banks[s]
                P.op("dve", lambda h, b=b, s=s: h.tensor_scalar(
                    L.rstd[:, s * NT:(s + 1) * NT], L.ps[b][:, 0:NT], 1.0 / K, eps, ALU.mult, ALU.add),
                    reads=[L.ps_r[b]], writes=[L.rstd_r])
            P.op("act", lambda h: h.activation(L.rstd[:, 0:TS], L.rstd[:, 0:TS], AF.Ln),
                 reads=[L.rstd_r], writes=[L.rstd_r])
            P.op("act", lambda h: h.activation(L.rstd[:, 0:TS], L.rstd[:, 0:TS], AF.Exp, scale=-0.5),
                 reads=[L.rstd_r], writes=[L.rstd_r])
            P.op("dve", lambda h: h.tensor_tensor(L.rstd[:, 0:TS], L.rstd[:, 0:TS], L.maskb[:, 0:TS], ALU.mult),
                 reads=[L.rstd_r, L.maskb_r], writes=[L.rstd_r])
            for kc in range(KC):
                i = L.stg_i % L.NST
                L.stg_i += 1
                st, sr = L.stg[i], L.stg_r[i]
                P.dma("sp", st[:, 0:TS], src[kc * 128:(kc + 1) * 128, t0:t0 + TS], writes=[sr])
                eng = "dve"
                P.op(eng, lambda h, st=st, kc=kc: h.scalar_tensor_tensor(
                    opv[:, kc, :], st[:, 0:TS], gam_t[:, kc:kc + 1], L.rstd[:, 0:TS], ALU.mult, ALU.mult),
                    reads=[sr, gam_r, L.rstd_r], writes=[L.opnd_r[kc]])
        elif mode == "convglu":
            for kc in range(KC):
                i = L.stg_i % L.NST
                L.stg_i += 1
                st, sr = L.stg[i], L.stg_r[i]
                rows = slice(kc * 128, (kc + 1) * 128)
                if t0 == 0:
                    P.op("pool", lambda h, st=st: h.memset(st[:, 0:2], 0.0), writes=[sr])
                    P.dma("sp", st[:, 2:TS + 2], src[rows, 0:TS], writes=[sr])
                else:
                    P.dma("sp", st[:, 0:TS + 2], src[rows, t0 - 2:t0 + TS], writes=[sr])
                j = L.stg2_i % 2
                L.stg2_i += 1
                sv, svr = L.stg2[j], L.stg2_r[j]
                P.dma("pool", sv[:, 0:TS], src2[rows, t0:t0 + TS], writes=[svr])
                k = L.tmp_i % 2
                L.tmp_i += 1
                tm, tmr = L.tmp[k], L.tmp_r[k]
                e1 = "dve"
                P.op(e1, lambda h, st=st, tm=tm, kc=kc: h.tensor_scalar(
                    tm[:, 0:TS], st[:, 2:TS + 2], cw_t[:, kc * 3 + 2:kc * 3 + 3], cb_t[:, kc:kc + 1],
                    ALU.mult, ALU.add), reads=[sr, cw_r, cb_r], writes=[tmr])
                P.op(e1, lambda h, st=st, tm=tm, kc=kc: h.scalar_tensor_tensor(
                    tm[:, 0:TS], st[:, 1:TS + 1], cw_t[:, kc * 3 + 1:kc * 3 + 2], tm[:, 0:TS],
                    ALU.mult, ALU.add), reads=[sr, cw_r, tmr], writes=[tmr])
                P.op(e1, lambda h, st=st, tm=tm, kc=kc: h.scalar_tensor_tensor(
                    tm[:, 0:TS], st[:, 0:TS], cw_t[:, kc * 3:kc * 3 + 1], tm[:, 0:TS],
                    ALU.mult, ALU.add), reads=[sr, cw_r, tmr], writes=[tmr])
                P.op("act", lambda h, tm=tm: h.activation(tm[:, 0:TS], tm[:, 0:TS], AF.Silu),
                     reads=[tmr], writes=[tmr])
                P.op("dve", lambda h, tm=tm, sv=sv, kc=kc: h.tensor_tensor(
                    opv[:, kc, :], tm[:, 0:TS], sv[:, 0:TS], ALU.mult),
                    reads=[tmr, svr], writes=[L.opnd_r[kc]])
        else:
            raise ValueError(mode)

        for cb in range(CB):
            M = min(128, C - cb * 128)
            wi = L.w_i % 2
            L.w_i += 1
            wst, wstr, wbf, wbfr = L.wst[wi], L.wst_r[wi], L.wbf[wi], L.wbf_r[wi]
            if cb == 0:
                P.dma("sp", wst[:, 0:KC * 128], W[0], writes=[wstr])
            if cb + 1 < CB:
                P.dma("sp", L.wst[1 - wi][:, 0:KC * 128], W[cb + 1], writes=[L.wst_r[1 - wi]])
            half = (KC // 2) * 128
            P.op("act", lambda h, wst=wst, wbf=wbf: h.copy(wbf[:, 0:half], wst[:, 0:half]),
                 reads=[wstr], writes=[wbfr[0]])
            P.op("dve", lambda h, wst=wst, wbf=wbf: h.tensor_copy(wbf[:, half:KC * 128], wst[:, half:KC * 128]),
                 reads=[wstr], writes=[wbfr[1]])
            oi = L.o_i % 2
            L.o_i += 1
            osb, osbr, rsb, rsbr = L.osb[oi], L.osb_r[oi], L.rsb[oi], L.rsb_r[oi]
            if res is not None:
                P.dma("pool", rsb[0:M, 0:TS], res[cb * 128:cb * 128 + M, t0:t0 + TS], writes=[rsbr])
            for s in range(nsub):
                b = L.ps_i % 8
                L.ps_i += 1
                for kc in range(KC):
                    P.op("pe", lambda h, b=b, wbf=wbf, kc=kc, s=s, M=M: h.matmul(
                        L.ps[b][0:M, 0:NT], wbf[:, kc * 128:kc * 128 + M], opv[:, kc, s * NT:(s + 1) * NT],
                        start=(kc == 0), stop=(kc == KC - 1)),
                        reads=[wbfr[0], wbfr[1], L.opnd_r[kc]], writes=[L.ps_r[b]])
                if res is not None:
                    P.op("dve", lambda h, b=b, s=s, M=M, osb=osb, rsb=rsb: h.tensor_tensor(
                        osb[0:M, s * NT:(s + 1) * NT], L.ps[b][0:M, 0:NT], rsb[0:M, s * NT:(s + 1) * NT], ALU.add),
                        reads=[L.ps_r[b], rsbr], writes=[osbr[s]])
                else:
                    e = "act" if s % 2 == 0 else "dve"
                    if e == "act":
                        P.op("act", lambda h, b=b, s=s, M=M, osb=osb: h.copy(
                            osb[0:M, s * NT:(s + 1) * NT], L.ps[b][0:M, 0:NT]),
                            reads=[L.ps_r[b]], writes=[osbr[s]])
                    else:
                        P.op("dve", lambda h, b=b, s=s, M=M, osb=osb: h.tensor_copy(
                            osb[0:M, s * NT:(s + 1) * NT], L.ps[b][0:M, 0:NT]),
                            reads=[L.ps_r[b]], writes=[osbr[s]])
            P.dma("sp", dst[cb * 128:cb * 128 + M, t0:t0 + TS], osb[0:M, 0:TS], reads=osbr[0:nsub])


def tile_w(W, dtype=np.float32):
    K, C = W.shape
    CB = (C + 127) // 128
    KC = K // 128
    Wp = np.zeros((K, CB * 128), dtype)
    Wp[:, :C] = W
    return np.ascontiguousarray(Wp.reshape(KC, 128, CB, 128).transpose(2, 1, 0, 3).reshape(CB, 128, KC * 128))

TQ = 384


def attn_stage(P, *, PN, qT, kT, v, qw, kw, sinks, masks, out):
    NB = PN // 128
    NTL = PN // TQ
    assert PN % TQ == 0
    ones64 = P.sbuf([64, 64], F32, "a_ones64")
    ones_r = Res()
    P.op("dve", lambda h: h.memset(ones64[:], 1.0), writes=[ones_r])
    qw_t = P.sbuf([64, 1], F32, "a_qw")
    kw_t = P.sbuf([64, 1], F32, "a_kw")
    cr = Res()
    P.dma("sp", qw_t[:], qw, writes=[cr])
    P.dma("sp", kw_t[:], kw, writes=[cr])
    P.op("dve", lambda h: h.tensor_scalar(qw_t[:], qw_t[:], 0.125, None, ALU.mult), reads=[cr], writes=[cr])
    esink = P.sbuf([128, 4], F32, "a_esink")
    es_r = Res()
    P.dma("sp", esink[:], sinks[0:1, :].partition_broadcast(128), writes=[es_r])
    P.op("act", lambda h: h.activation(esink[:], esink[:], AF.Exp), reads=[es_r], writes=[es_r])
    mstage = P.sbuf([128, 512], F32, "a_mstage")
    ms_r = Res()
    mk = []
    mk_r = []
    for i in range(4):
        m = P.sbuf([128, 512], BF16, "a_mask%d" % i)
        r = Res()
        P.dma("sp", mstage[:], masks[i], writes=[ms_r])
        P.op("dve", lambda h, m=m: h.tensor_copy(m[:], mstage[:]), reads=[ms_r], writes=[r])
        mk.append(m)
        mk_r.append(r)
    M_CUR, M_PREV, M_CUR0, M_PREV1 = 0, 1, 2, 3

    khat = P.sbuf([64, PN], BF16, "a_khat")
    khat_r = [Res() for _ in range(NTL)]
    vaug = P.sbuf([128, NB * 65], BF16, "a_vaug")
    vaug3 = vaug[:, :].rearrange("p (n c) -> p n c", c=65)
    vaug_r = [Res() for _ in range(NTL)]
    vones_r = Res()
    P.op("pool", lambda h: h.memset(vaug[:], 1.0), writes=[vones_r])
    vmeta = P.sbuf([16, 65], BF16, "a_vmeta")
    vmeta_s = P.sbuf([16, 64], F32, "a_vmeta_s")
    vm_r = Res()
    P.op("pool", lambda h: h.memset(vmeta[:], 1.0), writes=[vm_r])
    P.dma("sp", vmeta_s[:], v[112:128, :], writes=[vm_r])
    P.op("dve", lambda h: h.tensor_copy(vmeta[:, 0:64], vmeta_s[:]), reads=[vm_r], writes=[vm_r])

    NBUF = 2
    qst = [P.sbuf([64, 4 * TQ], F32, "a_qst%d" % i) for i in range(NBUF)]
    qst_r = [Res() for _ in range(NBUF)]
    kst = [P.sbuf([64, TQ], F32, "a_kst%d" % i) for i in range(NBUF)]
    kst_r = [Res() for _ in range(NBUF)]
    vst = [P.sbuf([128, 3 * 64], F32, "a_vst%d" % i) for i in range(NBUF)]
    vst_r = [Res() for _ in range(NBUF)]
    sqb = [P.sbuf([64, 5 * TQ], F32, "a_sq%d" % i) for i in range(NBUF)]
    sqb_r = [Res() for _ in range(NBUF)]
    rq = [P.sbuf([64, 5 * TQ], F32, "a_rq%d" % i) for i in range(NBUF)]
    rq_r = [Res() for _ in range(NBUF)]
    qhat = [P.sbuf([64, 4 * TQ], BF16, "a_qhat%d" % i) for i in range(NBUF)]
    qhat_r = [Res() for _ in range(NBUF)]
    osb = [P.sbuf([128, 3 * 256], F32, "a_osb%d" % i) for i in range(NBUF)]
    osb_r = [Res() for _ in range(NBUF)]
    NPT = 3
    pt = [P.sbuf([128, 512], BF16, "a_pt%d" % i) for i in range(NPT * 3)]
    pt_r = [Res() for _ in range(NPT * 3)]
    den = [P.sbuf([128, 8], F32, "a_den%d" % i) for i in range(2)]
    den_r = [Res() for _ in range(2)]
    ps_sq = [P.psum([64, 512], F32, "a_pssq%d" % i) for i in range(2)]
    ps_sq_r = [Res(excl=True) for _ in range(2)]
    ps_sc = [P.psum([128, 512], F32, "a_pssc%d" % i) for i in range(4)]
    ps_sc_r = [Res(excl=True) for _ in range(4)]
    ps_o = [P.psum([128, 4 * 65], F32, "a_pso%d" % i) for i in range(2)]
    ps_o_r = [Res(excl=True) for _ in range(2)]
    cnt = {"sq": 0, "sc": 0, "o": 0, "pt": 0}

    for tl in range(NTL):
        bi = tl % NBUF
        t0 = tl * TQ
        P.dma("sp", qst[bi][:, :].rearrange("p (h t) -> p h t", h=4), qT[:, :, t0:t0 + TQ], writes=[qst_r[bi]])
        P.dma("sp", kst[bi][:, :], kT[:, t0:t0 + TQ], writes=[kst_r[bi]])
        P.dma("pool", vst[bi][:, :].rearrange("p (n c) -> p n c", c=64),
              v[t0:t0 + TQ, :].rearrange("(n p) c -> p n c", p=128), writes=[vst_r[bi]])
        P.op("act", lambda h, bi=bi: h.activation(sqb[bi][:, 0:4 * TQ], qst[bi][:, :], AF.Square),
             reads=[qst_r[bi]], writes=[sqb_r[bi]])
        P.op("act", lambda h, bi=bi: h.activation(sqb[bi][:, 4 * TQ:5 * TQ], kst[bi][:, :], AF.Square),
             reads=[kst_r[bi]], writes=[sqb_r[bi]])
        for j in range(5):
            b = cnt["sq"] % 2
            cnt["sq"] += 1
            P.op("pe", lambda h, b=b, j=j, bi=bi: h.matmul(ps_sq[b][:, 0:TQ], ones64[:, :], sqb[bi][:, j * TQ:(j + 1) * TQ],
                                                           start=True, stop=True),
                 reads=[ones_r, sqb_r[bi]], writes=[ps_sq_r[b]])
            P.op("dve", lambda h, b=b, j=j, bi=bi: h.tensor_scalar(rq[bi][:, j * TQ:(j + 1) * TQ], ps_sq[b][:, 0:TQ],
                                                                   1.0 / 64, 1e-6, ALU.mult, ALU.add),
                 reads=[ps_sq_r[b]], writes=[rq_r[bi]])
        P.op("act", lambda h, bi=bi: h.activation(rq[bi][:, :], rq[bi][:, :], AF.Ln), reads=[rq_r[bi]], writes=[rq_r[bi]])
        P.op("act", lambda h, bi=bi: h.activation(rq[bi][:, :], rq[bi][:, :], AF.Exp, scale=-0.5),
             reads=[rq_r[bi]], writes=[rq_r[bi]])
        P.op("dve", lambda h, bi=bi: h.scalar_tensor_tensor(qhat[bi][:, :], qst[bi][:, :], qw_t[:, 0:1], rq[bi][:, 0:4 * TQ],
                                                            ALU.mult, ALU.mult),
             reads=[qst_r[bi], cr, rq_r[bi]], writes=[qhat_r[bi]])
        P.op("dve", lambda h, bi=bi, t0=t0: h.scalar_tensor_tensor(khat[:, t0:t0 + TQ], kst[bi][:, :], kw_t[:, 0:1],
                                                                    rq[bi][:, 4 * TQ:5 * TQ], ALU.mult, ALU.mult),
             reads=[kst_r[bi], cr, rq_r[bi]], writes=[khat_r[tl]])
        P.op("pool", lambda h, bi=bi, tl=tl: h.tensor_copy(vaug3[:, tl * 3:tl * 3 + 3, 0:64],
                                                           vst[bi][:, :].rearrange("p (n c) -> p n c", c=64)),
             reads=[vst_r[bi], vones_r], writes=[vaug_r[tl]])
        qh3 = qhat[bi][:, :].rearrange("p (h t) -> p h t", h=4)
        for bl in range(3):
            n = tl * 3 + bl
            parts = []
            if n >= 1:
                ptl = (n - 1) // 3
                parts.append(("prev", 128, khat[:, (n - 1) * 128:n * 128], M_PREV1 if n == 1 else M_PREV,
                              vaug3[:, n - 1, :], [khat_r[ptl], vaug_r[ptl]]))
            parts.append(("cur", 128, khat[:, n * 128:(n + 1) * 128], M_CUR0 if n == 0 else M_CUR,
                          vaug3[:, n, :], [khat_r[tl], vaug_r[tl]]))
            if n >= 2:
                parts.append(("meta", 16, khat[:, 112:128], None, vmeta[:, :], [khat_r[0], vm_r]))
            pts = []
            for (kind, kp, lhsT, mi, vr, deps) in parts:
                b = cnt["sc"] % 4
                cnt["sc"] += 1
                P.op("pe", lambda h, b=b, kp=kp, lhsT=lhsT, bl=bl, qh3=qh3: h.matmul(
                    ps_sc[b][0:kp, :], lhsT, qh3[:, :, bl * 128:(bl + 1) * 128], start=True, stop=True),
                    reads=[deps[0], qhat_r[bi]], writes=[ps_sc_r[b]])
                pi = cnt["pt"] % len(pt)
                cnt["pt"] += 1
                P.op("act", lambda h, b=b, kp=kp, pi=pi: h.activation(pt[pi][0:kp, :], ps_sc[b][0:kp, :], AF.Exp),
                     reads=[ps_sc_r[b]], writes=[pt_r[pi]])
                if mi is not None:
                    P.op("dve", lambda h, pi=pi, mi=mi: h.tensor_tensor(pt[pi][:, :], pt[pi][:, :], mk[mi][:, :], ALU.mult),
                         reads=[pt_r[pi], mk_r[mi]], writes=[pt_r[pi]])
                pts.append((pi, kp, vr, deps[1]))
            ob = cnt["o"] % 2
            cnt["o"] += 1
            po3 = ps_o[ob][:, :].rearrange("p (h c) -> p h c", c=65)
            for hh in range(4):
                for idx, (pi, kp, vr, vdep) in enumerate(pts):
                    P.op("pe", lambda h, pi=pi, kp=kp, vr=vr, hh=hh, idx=idx, po3=po3, npt=len(pts): h.matmul(
                        po3[:, hh, :], pt[pi][0:kp, hh * 128:(hh + 1) * 128], vr if kp == 128 else vr[0:kp, :],
                        start=(idx == 0), stop=(idx == npt - 1)),
                        reads=[pt_r[pi], vdep], writes=[ps_o_r[ob]])
            dn, dnr = den[ob], den_r[ob]
            P.op("dve", lambda h, dn=dn, po3=po3: h.tensor_tensor(dn[:, 0:4], po3[:, :, 64], esink[:, :], ALU.add),
                 reads=[ps_o_r[ob], es_r], writes=[dnr])
            P.op("dve", lambda h, dn=dn: h.reciprocal(dn[:, 4:8], dn[:, 0:4]), reads=[dnr], writes=[dnr])
            for hh in range(4):
                if hh % 2 == 0:
                    P.op("dve", lambda h, dn=dn, po3=po3, hh=hh, bl=bl, bi=bi: h.tensor_scalar(
                        osb[bi][:, bl * 256 + hh * 64: bl * 256 + hh * 64 + 64], po3[:, hh, 0:64], dn[:, 4 + hh:5 + hh], None,
                        ALU.mult), reads=[ps_o_r[ob], dnr], writes=[osb_r[bi]])
                else:
                    P.op("act", lambda h, dn=dn, po3=po3, hh=hh, bl=bl, bi=bi: h.activation(
                        osb[bi][:, bl * 256 + hh * 64: bl * 256 + hh * 64 + 64], po3[:, hh, 0:64], AF.Copy,
                        scale=dn[:, 4 + hh:5 + hh]), reads=[ps_o_r[ob], dnr], writes=[osb_r[bi]])
        P.dma("sp", out[t0:t0 + TQ, :].rearrange("(n p) c -> p n c", p=128),
              osb[bi][:, :].rearrange("p (n c) -> p n c", c=256), reads=[osb_r[bi]])


def attn_masks():
    k = np.arange(128)[:, None]
    q = np.arange(128)[None, :]
    cur = (k <= q)
    prev = (k > q)
    cur0 = cur & (k >= 112)
    prev1 = np.broadcast_to(k >= 112, (128, 128))
    return np.stack([np.tile(m.astype(np.float32), (1, 4)) for m in (cur, prev, cur0, prev1)])

LC = 64
WSC = -0.6065306597126334


def rwkv_consts():
    L = LC
    s = np.arange(L)[:, None]
    t = np.arange(L)[None, :]
    tri_incl = (s <= t).astype(np.float32)
    tri_strict = (s < t).astype(np.float32)
    upper = (s > t).astype(np.float32)
    c = {}
    c["TT1"] = np.concatenate([tri_strict, tri_incl], 1) * WSC
    c["TT2"] = np.concatenate([tri_incl, tri_incl], 1) * WSC
    c["TT3"] = np.concatenate([upper, upper], 1) * WSC
    c["negones"] = np.full((64, 64), WSC, np.float32)
    m = np.concatenate([tri_strict, tri_incl], 1)
    c["mask128"] = np.concatenate([m, m], 0)
    c["maskN"] = np.ascontiguousarray(tri_strict.T)
    c["ident"] = np.eye(128, dtype=np.float32)
    return {k: np.ascontiguousarray(v, dtype=np.float32) for k, v in c.items()}


class Rot:
    def __init__(self, items):
        self.items = items
        self.i = 0

    def next(self):
        x = self.items[self.i % len(self.items)]
        self.i += 1
        return x


def rwkv_stage(P, *, PN, rkvp, xwT, xaT, xgT, mu_rkv, mu_w, mu_a, mu_g, rows, w_up, a_up, g_up, consts, out,
               nchunks=None):
    NCH = PN // LC if nchunks is None else nchunks
    sb = lambda shape, name, dt=F32: P.sbuf(shape, dt, "r_" + name)

    cr = Res()
    ct = {}
    for nm, shp in (("TT1", [64, 128]), ("TT2", [64, 128]), ("TT3", [64, 128]), ("negones", [64, 64]),
                    ("mask128", [128, 128]), ("maskN", [64, 64]), ("ident", [128, 128])):
        ct[nm] = sb(shp, nm)
        P.dma("sp", ct[nm][:], consts[nm], writes=[cr])
    mu_b = sb([128, 768], "mu_b")
    P.dma("sp", mu_b[:], mu_rkv[0:1, :].partition_broadcast(128), writes=[cr])
    rowb = sb([128, 7 * 256], "rowb")
    for i in range(7):
        P.dma("sp", rowb[:, i * 256:(i + 1) * 256], rows[i:i + 1, :].partition_broadcast(128), writes=[cr])
    W0, A0, KK, KA, RK, LNW, LNB = [rowb[:, i * 256:(i + 1) * 256] for i in range(7)]
    muw = sb([64, 1], "muw"); mua = sb([64, 1], "mua"); mug0 = sb([128, 1], "mug0"); mug1 = sb([32, 1], "mug1")
    P.dma("sp", muw[:], mu_w, writes=[cr]); P.dma("sp", mua[:], mu_a, writes=[cr])
    P.dma("sp", mug0[:], mu_g[0:128, :], writes=[cr]); P.dma("sp", mug1[:], mu_g[128:160, :], writes=[cr])
    wup = sb([64, 256], "wup"); aup = sb([64, 256], "aup"); gup0 = sb([128, 256], "gup0"); gup1 = sb([32, 256], "gup1")
    P.dma("sp", wup[:], w_up, writes=[cr]); P.dma("sp", aup[:], a_up, writes=[cr])
    P.dma("sp", gup0[:], g_up[0:128, :], writes=[cr]); P.dma("sp", gup1[:], g_up[128:160, :], writes=[cr])

    NB = 2
    def bufs(shape, name, n=NB):
        return Rot([(sb(shape, "%s%d" % (name, i)), Res()) for i in range(n)])
    cur_b = bufs([128, 768], "cur"); prv_b = bufs([128, 768], "prv")
    xw_b = bufs([64, 65], "xw"); xa_b = bufs([64, 65], "xa"); xg0_b = bufs([128, 65], "xg0"); xg1_b = bufs([32, 65], "xg1")
    tw_b = bufs([64, 128], "tw"); ta_b = bufs([64, 128], "ta"); tg0_b = bufs([128, 64], "tg0"); tg1_b = bufs([32, 64], "tg1")
    sig_b = bufs([128, 256], "sig"); a_b = bufs([128, 256], "a"); g_b = bufs([64, 256], "g")
    kk_b = bufs([128, 256], "kk"); sq_b = bufs([128, 256], "sq"); ss_b = bufs([128, 8], "ss")
    t1_b = bufs([128, 256], "t1"); bkr_b = bufs([128, 256], "bkr")
    e1_b = bufs([128, 256], "e1"); e2_b = bufs([128, 256], "e2"); e3_b = bufs([128, 256], "e3")
    glb_b = bufs([64, 256], "glb")
    ar_b = bufs([128, 256], "ar"); bk_b = bufs([128, 256], "bk"); bkb_b = bufs([128, 256], "bkb")
    uv_b = [bufs([128, 64], "uv%d" % u) for u in range(4)]
    art_b = [bufs([64, 128], "art%d" % u) for u in range(4)]
    bkt_b = [bufs([64, 128], "bkt%d" % u) for u in range(4)]
    mts_b = [bufs([128, 128], "mts%d" % u) for u in range(4)]
    p_b = [bufs([64, 64], "p%d" % u, 3) for u in range(4)]
    q_b = [bufs([64, 64], "q%d" % u, 3) for u in range(4)]
    z_b = [bufs([64, 128], "z%d" % u, 3) for u in range(4)]
    ap_b = [bufs([64, 64], "ap%d" % u) for u in range(4)]
    phit_b = [bufs([64, 64], "phit%d" % u) for u in range(4)]
    dgl_b = [bufs([64, 64], "dgl%d" % u) for u in range(4)]
    psi_b = [bufs([64, 64], "psi%d" % u) for u in range(4)]
    rpt_b = [bufs([64, 64], "rpt%d" % u) for u in range(4)]
    h_b = [bufs([64, 64], "h%d" % u, 2) for u in range(4)]
    ysq_b = bufs([64, 256], "ysq"); st_b = bufs([64, 16], "st"); yn_b = bufs([64, 256], "yn")
    rk_b = bufs([64, 256], "rk"); ob_b = bufs([64, 256], "ob")
    pbanks = [P.psum([128, 512], F32, "r_ps%d" % i) for i in range(8)]
    half = Rot([(pbanks[b][:, 0:256], Res(excl=True)) for b in range(2)])
    pso_reg = (pbanks[2][:, 0:256], Res(excl=True))
    quar = Rot([(pbanks[b][:, 0:128], Res(excl=True)) for b in range(3, 8)])

    H = []
    for u in range(4):
        ht, hr = h_b[u].next()
        P.op("pool", lambda h, ht=ht: h.memset(ht[:], 0.0), writes=[hr])
        H.append((ht, hr))

    def mm(ps, psr, lhsT, rhs, reads, start=True, stop=True):
        P.op("pe", lambda h: h.matmul(ps, lhsT, rhs, start=start, stop=stop), reads=reads, writes=[psr])

    def acopy(dst, dstr, src, srcr):
        P.op("act", lambda h: h.copy(dst, src), reads=[srcr], writes=[dstr])

    for c in range(NCH):
        t0 = c * LC
        cur, curr = cur_b.next(); prv, prvr = prv_b.next()
        for hh in range(2):
            P.dma("sp", cur[hh * 64:(hh + 1) * 64, :], rkvp[1 + t0:1 + t0 + 64, :], writes=[curr])
            P.dma("pool", prv[hh * 64:(hh + 1) * 64, :], rkvp[t0:t0 + 64, :], writes=[prvr])
        xw, xwr = xw_b.next(); xa, xar = xa_b.next(); xg0, xg0r = xg0_b.next(); xg1, xg1r = xg1_b.next()
        P.dma("sp", xw[:], xwT[:, t0:t0 + 65], writes=[xwr])
        P.dma("sp", xa[:], xaT[:, t0:t0 + 65], writes=[xar])
        P.dma("pool", xg0[:], xgT[0:128, t0:t0 + 65], writes=[xg0r])
        P.dma("pool", xg1[:], xgT[128:160, t0:t0 + 65], writes=[xg1r])
        P.op("dve", lambda h: h.tensor_tensor(prv[:], prv[:], cur[:], ALU.subtract), reads=[prvr, curr], writes=[prvr])
        P.op("dve", lambda h: h.tensor_tensor(prv[:], prv[:], mu_b[:], ALU.mult), reads=[prvr, cr], writes=[prvr])
        P.op("dve", lambda h: h.tensor_tensor(prv[:], prv[:], cur[:], ALU.add), reads=[prvr, curr], writes=[prvr])
        zs, zsr = prv, prvr
        Rr, Kr, Vr = zs[:, 0:256], zs[:, 256:512], zs[:, 512:768]
        tw, twr = tw_b.next(); ta, tar = ta_b.next(); tg0, tg0r = tg0_b.next(); tg1, tg1r = tg1_b.next()
        for (x, xr, mu, np_, dst, dstr, fn) in ((xw, xwr, muw, 64, tw, twr, AF.Tanh), (xa, xar, mua, 64, ta, tar, AF.Copy),
                                                (xg0, xg0r, mug0, 128, tg0, tg0r, AF.Sigmoid),
                                                (xg1, xg1r, mug1, 32, tg1, tg1r, AF.Sigmoid)):
            tmp, tmpr = (sq_b.next())
            P.op("dve", lambda h, x=x, np_=np_, tmp=tmp: h.tensor_tensor(tmp[0:np_, 0:64], x[0:np_, 0:64], x[0:np_, 1:65],
                                                                         ALU.subtract), reads=[xr], writes=[tmpr])
            P.op("dve", lambda h, x=x, np_=np_, tmp=tmp, mu=mu: h.scalar_tensor_tensor(
                tmp[0:np_, 0:64], tmp[0:np_, 0:64], mu[0:np_, 0:1], x[0:np_, 1:65], ALU.mult, ALU.add),
                reads=[xr, tmpr, cr], writes=[tmpr])
            if fn == AF.Tanh:
                P.op("act", lambda h, dst=dst, np_=np_, tmp=tmp: h.activation(dst[0:np_, 0:64], tmp[0:np_, 0:64], AF.Sigmoid,
                                                                             scale=2.0), reads=[tmpr], writes=[dstr])
                P.op("dve", lambda h, dst=dst, np_=np_: h.tensor_scalar(dst[0:np_, 0:64], dst[0:np_, 0:64], 2.0, -1.0,
                                                                        ALU.mult, ALU.add), reads=[dstr], writes=[dstr])
            else:
                P.op("act", lambda h, dst=dst, np_=np_, tmp=tmp, fn=fn: h.activation(dst[0:np_, 0:64], tmp[0:np_, 0:64], fn),
                     reads=[tmpr], writes=[dstr])
            if dst is tw or dst is ta:
                P.op("act", lambda h, dst=dst: h.copy(dst[:, 64:128], dst[:, 0:64]), reads=[dstr], writes=[dstr])
        sig, sigr = sig_b.next(); a, ar_ = a_b.next(); g, gr = g_b.next()
        ps, psr = half.next()
        mm(ps, psr, tw[:, :], wup[:, :], [twr, cr])
        P.op("dve", lambda h, ps=ps: h.tensor_tensor(sig[:], ps, W0, ALU.add), reads=[psr, cr], writes=[sigr])
        P.op("act", lambda h: h.activation(sig[:], sig[:], AF.Sigmoid), reads=[sigr], writes=[sigr])
        ps, psr = half.next()
        mm(ps, psr, ta[:, :], aup[:, :], [tar, cr])
        P.op("dve", lambda h, ps=ps: h.tensor_tensor(a[:], ps, A0, ALU.add), reads=[psr, cr], writes=[ar_])
        P.op("act", lambda h: h.activation(a[:], a[:], AF.Sigmoid), reads=[ar_], writes=[ar_])
        ps, psr = half.next()
        mm(ps[0:64, :], psr, tg0[:, :], gup0[:, :], [tg0r, cr], True, False)
        mm(ps[0:64, :], psr, tg1[:, :], gup1[:, :], [tg1r, cr], False, True)
        acopy(g[:], gr, ps[0:64, :], psr)
        kk, kkr = kk_b.next(); sq, sqr = sq_b.next(); ss, ssr = ss_b.next()
        P.op("dve", lambda h: h.tensor_tensor(kk[:], Kr, KK, ALU.mult), reads=[zsr, cr], writes=[kkr])
        P.op("dve", lambda h: h.tensor_tensor(sq[:], kk[:], kk[:], ALU.mult), reads=[kkr], writes=[sqr])
        P.op("dve", lambda h: h.tensor_reduce(ss[:, 0:4], sq[:, :].rearrange("p (u d) -> p u d", u=4), AX.X, ALU.add),
             reads=[sqr], writes=[ssr])
        P.op("dve", lambda h: h.tensor_scalar(ss[:, 0:4], ss[:, 0:4], 1e-24, None, ALU.add), reads=[ssr], writes=[ssr])
        P.op("act", lambda h: h.activation(ss[:, 0:4], ss[:, 0:4], AF.Ln), reads=[ssr], writes=[ssr])
        P.op("act", lambda h: h.activation(ss[:, 4:8], ss[:, 0:4], AF.Exp, scale=-0.5), reads=[ssr], writes=[ssr])
        for u in range(4):
            P.op("dve", lambda h, u=u: h.tensor_scalar(kk[:, u * 64:(u + 1) * 64], kk[:, u * 64:(u + 1) * 64],
                                                       ss[:, 4 + u:5 + u], None, ALU.mult), reads=[kkr, ssr], writes=[kkr])
        t1, t1r = t1_b.next(); bkr, bkrr = bkr_b.next()
        P.op("dve", lambda h: h.scalar_tensor_tensor(t1[:], a[:], -1.0, KA, ALU.add, ALU.mult), reads=[ar_, cr], writes=[t1r])
        P.op("dve", lambda h: h.scalar_tensor_tensor(t1[:], t1[:], 1.0, Kr, ALU.add, ALU.mult), reads=[t1r, zsr], writes=[t1r])
        P.op("dve", lambda h: h.tensor_tensor(bkr[0:64, :], kk[0:64, :], a[0:64, :], ALU.mult), reads=[kkr, ar_], writes=[bkrr])
        P.op("act", lambda h: h.copy(bkr[64:128, :], t1[64:128, :]), reads=[t1r], writes=[bkrr])
        e1, e1r = e1_b.next(); e2, e2r = e2_b.next(); e3, e3r = e3_b.next(); glb, glbr = glb_b.next()
        for (TT, e, er, sc) in ((ct["TT1"], e1, e1r, 1.0), (ct["TT2"], e2, e2r, -1.0), (ct["TT3"], e3, e3r, 1.0)):
            ps, psr = half.next()
            mm(ps, psr, TT[:, :], sig[0:64, :], [cr, sigr])
            P.op("act", lambda h, e=e, ps=ps, sc=sc: h.activation(e[:], ps, AF.Exp, scale=sc), reads=[psr], writes=[er])
        ps, psr = half.next()
        for u in range(4):
            mm(ps[0:64, u * 64:(u + 1) * 64], psr, sig[0:64, u * 64:(u + 1) * 64], ct["negones"][:, :], [sigr, cr])
        P.op("act", lambda h, ps=ps: h.activation(glb[:], ps[0:64, :], AF.Exp), reads=[psr], writes=[glbr])
        ar, arr = ar_b.next(); bk, bkr_ = bk_b.next(); bkb, bkbr = bkb_b.next()
        P.op("dve", lambda h: h.scalar_tensor_tensor(ar[0:64, :], kk[0:64, :], -1.0, e1[0:64, :], ALU.mult, ALU.mult),
             reads=[kkr, e1r], writes=[arr])
        P.op("dve", lambda h: h.tensor_tensor(ar[64:128, :], Rr[64:128, :], e1[64:128, :], ALU.mult),
             reads=[zsr, e1r], writes=[arr])
        P.op("dve", lambda h: h.tensor_tensor(bk[:], bkr[:], e2[:], ALU.mult), reads=[bkrr, e2r], writes=[bkr_])
        P.op("dve", lambda h: h.tensor_tensor(bkb[:], bkr[:], e3[:], ALU.mult), reads=[bkrr, e3r], writes=[bkbr])
        UV = []
        for u in range(4):
            uv, uvr = uv_b[u].next()
            P.op("act", lambda h, uv=uv, u=u: h.copy(uv[64:128, :], Vr[64:128, u * 64:(u + 1) * 64]),
                 reads=[zsr], writes=[uvr])
            UV.append((uv, uvr))
        U = [dict() for _ in range(4)]
        for u in range(4):
            cs = slice(u * 64, (u + 1) * 64)
            d = U[u]
            d["art"], d["artr"] = art_b[u].next(); d["bkt"], d["bktr"] = bkt_b[u].next()
            ps, psr = quar.next()
            P.op("pe", lambda h, ps=ps, cs=cs: h.transpose(ps[0:64, :], ar[:, cs], ct["ident"][:, :]), reads=[arr, cr], writes=[psr])
            acopy(d["art"][:], d["artr"], ps[0:64, :], psr)
            ps, psr = quar.next()
            P.op("pe", lambda h, ps=ps, cs=cs: h.transpose(ps[0:64, :], bk[:, cs], ct["ident"][:, :]), reads=[bkr_, cr], writes=[psr])
            P.op("dve", lambda h, ps=ps, d=d: h.tensor_copy(d["bkt"][:], ps[0:64, :]), reads=[psr], writes=[d["bktr"]])
        for u in range(4):
            d = U[u]
            d["mts"], d["mtsr"] = mts_b[u].next()
            ps, psr = quar.next()
            mm(ps, psr, d["bkt"][:, :], d["art"][:, :], [d["bktr"], d["artr"]])
            P.op("dve", lambda h, ps=ps, d=d: h.tensor_tensor(d["mts"][:], ps, ct["mask128"][:, :], ALU.mult),
                 reads=[psr, cr], writes=[d["mtsr"]])
            d["p"], d["pr"] = p_b[u].next()
            ps, psr = quar.next()
            mm(ps[0:64, 0:64], psr, d["art"][:, 0:64], d["bkt"][:, 0:64], [d["bktr"], d["artr"]])
            P.op("dve", lambda h, ps=ps, d=d: h.tensor_tensor(d["p"][:], ps[0:64, 0:64], ct["maskN"][:, :], ALU.mult),
                 reads=[psr, cr], writes=[d["pr"]])
        for u in range(4):
            cs = slice(u * 64, (u + 1) * 64)
            d = U[u]
            uv, uvr = UV[u]
            d["z"], d["zr"] = z_b[u].next()
            ps, psr = quar.next()
            mm(ps[0:64, 0:64], psr, d["mts"][64:128, 0:64], uv[64:128, :], [d["mtsr"], uvr])
            acopy(d["z"][:, 64:128], d["zr"], ps[0:64, 0:64], psr)
            P.op("act", lambda h, d=d, cs=cs: h.copy(d["z"][:, 0:64], ar[0:64, cs]), reads=[arr], writes=[d["zr"]])
            d["q"], d["qr"] = d["mts"][0:64, 0:64], d["mtsr"]
        for lvl in range(6):
            for u in range(4):
                d = U[u]
                uv, uvr = UV[u]
                ps, psr = quar.next()
                mm(ps[0:64, :], psr, d["q"], d["z"][:, :], [d["qr"], d["zr"]])
                if lvl < 5:
                    zn, znr = z_b[u].next()
                    P.op("dve", lambda h, ps=ps, d=d, zn=zn: h.tensor_tensor(zn[:], ps[0:64, :], d["z"][:, :], ALU.add),
                         reads=[psr, d["zr"]], writes=[znr])
                    pn, pnr = p_b[u].next(); qn, qnr = q_b[u].next()
                    ps1, ps1r = quar.next()
                    mm(ps1[0:64, 0:64], ps1r, d["q"], d["p"][:, :], [d["qr"], d["pr"]])
                    acopy(pn[:], pnr, ps1[0:64, 0:64], ps1r)
                    ps2, ps2r = quar.next()
                    mm(ps2[0:64, 0:64], ps2r, d["p"][:, :], d["q"], [d["qr"], d["pr"]])
                    acopy(qn[:], qnr, ps2[0:64, 0:64], ps2r)
                    d["z"], d["zr"] = zn, znr
                    d["p"], d["pr"] = pn, pnr
                    d["q"], d["qr"] = qn[:, :], qnr
                else:
                    d["ap"], d["apr"] = ap_b[u].next()
                    P.op("dve", lambda h, ps=ps, d=d: h.tensor_tensor(d["ap"][:], ps[0:64, 0:64], d["z"][:, 0:64], ALU.add),
                         reads=[psr, d["zr"]], writes=[d["apr"]])
                    P.op("dve", lambda h, ps=ps, d=d, uv=uv: h.tensor_tensor(uv[0:64, :], ps[0:64, 64:128], d["z"][:, 64:128],
                                                                              ALU.add), reads=[psr, d["zr"]], writes=[uvr])
        pso, psor = pso_reg
        for u in range(4):
            cs = slice(u * 64, (u + 1) * 64)
            d = U[u]
            uv, uvr = UV[u]
            phit, phitr = phit_b[u].next(); dgl, dglr = dgl_b[u].next()
            P.op("dve", lambda h, dgl=dgl, cs=cs: h.tensor_tensor(dgl[:], ct["ident"][0:64, 0:64], glb[:, cs], ALU.mult),
                 reads=[cr, glbr], writes=[dglr])
            ps, psr = quar.next()
            mm(ps[0:64, 0:64], psr, d["ap"][:, :], bkb[0:64, cs], [d["apr"], bkbr])
            P.op("dve", lambda h, ps=ps, phit=phit, dgl=dgl: h.tensor_tensor(phit[:], ps[0:64, 0:64], dgl[:], ALU.add),
                 reads=[psr, dglr], writes=[phitr])
            psi, psir = psi_b[u].next()
            ps, psr = quar.next()
            mm(ps[0:64, 0:64], psr, bkb[:, cs], uv[:, :], [bkbr, uvr])
            acopy(psi[:], psir, ps[0:64, 0:64], psr)
            rpt, rptr = rpt_b[u].next()
            ps, psr = quar.next()
            mm(ps[0:64, 0:64], psr, d["ap"][:, :], d["mts"][0:64, 64:128], [d["apr"], d["mtsr"]])
            P.op("dve", lambda h, ps=ps, rpt=rpt, d=d: h.tensor_tensor(rpt[:], ps[0:64, 0:64], d["art"][:, 64:128], ALU.add),
                 reads=[psr, d["artr"]], writes=[rptr])
            ht, hr = H[u]
            mm(pso[0:64, cs], psor, rpt[:, :], ht[:, :], [rptr, hr], True, False)
            mm(pso[0:64, cs], psor, d["mts"][:, 64:128], uv[:, :], [d["mtsr"], uvr], False, True)
            ps, psr = quar.next()
            mm(ps[0:64, 0:64], psr, phit[:, :], ht[:, :], [phitr, hr])
            hn, hnr = h_b[u].next()
            P.op("dve", lambda h, ps=ps, hn=hn, psi=psi: h.tensor_tensor(hn[:], ps[0:64, 0:64], psi[:], ALU.add),
                 reads=[psr, psir], writes=[hnr])
            H[u] = (hn, hnr)
        ysq, ysqr = ysq_b.next(); st, str_ = st_b.next(); yn, ynr = yn_b.next(); rk, rkr = rk_b.next(); ob, obr = ob_b.next()
        y3 = pso[0:64, :].rearrange("p (u d) -> p u d", u=4)
        P.op("act", lambda h: h.activation(ysq[:], pso[0:64, :], AF.Square), reads=[psor], writes=[ysqr])
        P.op("dve", lambda h: h.tensor_reduce(st[:, 0:4], y3, AX.X, ALU.add), reads=[psor], writes=[str_])
        P.op("dve", lambda h: h.tensor_reduce(st[:, 4:8], ysq[:, :].rearrange("p (u d) -> p u d", u=4), AX.X, ALU.add),
             reads=[ysqr], writes=[str_])
        P.op("dve", lambda h: h.tensor_scalar(st[:, 0:8], st[:, 0:8], 1.0 / 64, None, ALU.mult), reads=[str_], writes=[str_])
        P.op("dve", lambda h: h.tensor_tensor(st[:, 8:12], st[:, 0:4], st[:, 0:4], ALU.mult), reads=[str_], writes=[str_])
        P.op("dve", lambda h: h.tensor_tensor(st[:, 8:12], st[:, 4:8], st[:, 8:12], ALU.subtract), reads=[str_], writes=[str_])
        P.op("dve", lambda h: h.tensor_scalar(st[:, 8:12], st[:, 8:12], 64e-5, None, ALU.add), reads=[str_], writes=[str_])
        P.op("act", lambda h: h.activation(st[:, 8:12], st[:, 8:12], AF.Ln), reads=[str_], writes=[str_])
        P.op("act", lambda h: h.activation(st[:, 8:12], st[:, 8:12], AF.Exp, scale=-0.5), reads=[str_], writes=[str_])
        for u in range(4):
            cs = slice(u * 64, (u + 1) * 64)
            P.op("dve", lambda h, u=u, cs=cs: h.tensor_scalar(yn[:, cs], pso[0:64, cs], st[:, u:u + 1], st[:, 8 + u:9 + u],
                                                             ALU.subtract, ALU.mult), reads=[psor, str_], writes=[ynr])
        P.op("dve", lambda h: h.tensor_tensor(yn[:], yn[:], LNW[0:64, :], ALU.mult), reads=[ynr, cr], writes=[ynr])
        P.op("dve", lambda h: h.tensor_tensor(yn[:], yn[:], LNB[0:64, :], ALU.add), reads=[ynr, cr], writes=[ynr])
        P.op("dve", lambda h: h.tensor_tensor(rk[:], Rr[0:64, :], t1[0:64, :], ALU.mult), reads=[zsr, t1r], writes=[rkr])
        P.op("dve", lambda h: h.tensor_tensor(rk[:], rk[:], RK[0:64, :], ALU.mult), reads=[rkr, cr], writes=[rkr])
        P.op("dve", lambda h: h.tensor_reduce(st[:, 12:16], rk[:, :].rearrange("p (u d) -> p u d", u=4), AX.X, ALU.add),
             reads=[rkr], writes=[str_])
        for u in range(4):
            cs = slice(u * 64, (u + 1) * 64)
            P.op("dve", lambda h, u=u, cs=cs: h.scalar_tensor_tensor(yn[:, cs], Vr[0:64, cs], st[:, 12 + u:13 + u], yn[:, cs],
                                                                    ALU.mult, ALU.add), reads=[zsr, str_, ynr], writes=[ynr])
        P.op("dve", lambda h: h.tensor_tensor(ob[:], yn[:], g[:], ALU.mult), reads=[ynr, gr], writes=[obr])
        P.dma("sp", out[t0:t0 + 64, :], ob[:], reads=[obr])

SB3 = 384


def ssd_consts():
    k = np.arange(128)[:, None]
    t = np.arange(128)[None, :]
    c = {"tri": (k <= t).astype(np.float32), "ones": np.ones((128, 128), np.float32),
         "maskneg": np.where(k <= t, 0.0, -30000.0).astype(np.float32), "ident": np.eye(128, dtype=np.float32)}
    return c


def ssd_stage(P, *, PN, xbcT, convw, convb, z, dtT, dtb, alog, dskip, normw, consts, out):
    NSB = PN // SB3
    assert PN % SB3 == 0
    sb = lambda shape, name, dt=F32: P.sbuf(shape, dt, "s_" + name)
    cr = Res()
    ct = {}
    for nm in ("tri", "ones", "maskneg", "ident"):
        ct[nm] = sb([128, 128], nm)
        P.dma("sp", ct[nm][:], consts[nm], writes=[cr])
    cw = sb([128, 48], "cw"); cb = sb([128, 12], "cb")
    P.dma("sp", cw[:], convw, writes=[cr]); P.dma("sp", cb[:], convb, writes=[cr])
    dtb_t = sb([16, 1], "dtb"); acol = sb([16, 1], "acol")
    P.dma("sp", dtb_t[:], dtb, writes=[cr]); P.dma("sp", acol[:], alog, writes=[cr])
    P.op("act", lambda h: h.activation(acol[:], acol[:], AF.Exp), reads=[cr], writes=[cr])
    P.op("dve", lambda h: h.tensor_scalar(acol[:], acol[:], -1.0, None, ALU.mult), reads=[cr], writes=[cr])
    dsk = sb([128, 1024], "dsk"); nw = sb([128, 1024], "nw")
    P.dma("sp", dsk[:], dskip[0:1, :].partition_broadcast(128), writes=[cr])
    P.dma("sp", nw[:], normw[0:1, :].partition_broadcast(128), writes=[cr])

    def bufs(shape, name, n=2, dt=F32):
        return Rot([(sb(shape, "%s%d" % (name, i), dt), Res()) for i in range(n)])
    xin_b = bufs([128, 12 * (SB3 + 3)], "xin")
    cv_b = bufs([128, 12 * SB3], "cv")
    ctmp_b = bufs([128, SB3], "ctmp", 3)
    dtin_b = bufs([16, SB3], "dtin"); dtf_b = bufs([16, 2 * SB3], "dtf")
    xtok_b = Rot([(sb([128, 1024], "xtok%d" % i), [Res() for _ in range(8)]) for i in range(2)]); btok_b = bufs([128, 256], "btok", 2, BF16)
    bct_b = bufs([128, 4 * 128], "bct", 2, BF16)
    dta_b = bufs([128, 64], "dta")
    te_b = bufs([128, 48], "te")
    tmpi_b = bufs([128, 512], "tmpi")
    NSL = 3
    abc_s = [(sb([128, 128], "abc%d" % i), Res()) for i in range(NSL)]
    seg_s = [(sb([128, 128], "seg%d" % i), Res()) for i in range(NSL)]
    wj_s = [(sb([128, 128], "wj%d" % i, BF16), Res()) for i in range(NSL)]
    ebc_s = [(sb([128, 128], "ebc%d" % i), Res()) for i in range(NSL)]
    ctj_s = [(sb([128, 128], "ctj%d" % i, BF16), Res()) for i in range(NSL)]
    cbt_b = bufs([128, 256], "cbt")
    xdt_b = Rot([(sb([128, 1024], "xdt%d" % i, BF16), [Res() for _ in range(16)]) for i in range(2)])
    xs_b = Rot([(sb([128, 1024], "xs%d" % i, BF16), [Res() for _ in range(16)]) for i in range(2)])
    z_b = bufs([128, 1024], "z"); yy_b = bufs([128, 1024], "yy"); sq_b = bufs([128, 1024], "sq", 1)
    st_b = bufs([128, 8], "st")
    state = sb([128, 1024], "state"); state_r = [Res() for _ in range(16)]
    stbf = sb([128, 1024], "statebf", BF16); stbf_r = [Res(), Res()]
    P.op("pool", lambda h: h.memset(state[:], 0.0), writes=state_r)
    P.op("pool", lambda h: h.memset(stbf[:], 0.0), writes=stbf_r)
    pb = [P.psum([128, 512], F32, "s_ps%d" % i) for i in range(8)]
    ybank = [(pb[0], Res(excl=True)), (pb[1], Res(excl=True))]
    cbanks = [(pb[2 + i], Res(excl=True)) for i in range(3)]
    mbank = Rot([(pb[5], Res(excl=True))])
    sbank = Rot([(pb[6], Res(excl=True)), (pb[7], Res(excl=True))])

    def mm(ps, psr, lhsT, rhs, reads, start=True, stop=True):
        P.op("pe", lambda h: h.matmul(ps, lhsT, rhs, start=start, stop=stop), reads=reads, writes=[psr])

    for sbi in range(NSB):
        t0 = sbi * SB3
        xin, xinr = xin_b.next(); cv, cvr = cv_b.next()
        xin3 = xin[:, :].rearrange("p (k t) -> p k t", k=12)
        cv3 = cv[:, :].rearrange("p (k t) -> p k t", k=12)
        for hf in range(2):
            P.dma("sp" if hf == 0 else "pool", xin3[:, hf * 6:(hf + 1) * 6, :],
                  xbcT[hf * 768:(hf + 1) * 768, t0:t0 + SB3 + 3].rearrange("(k p) t -> p k t", p=128), writes=[xinr])
        for kc in range(12):
            tm, tmr = ctmp_b.next()
            P.op("dve", lambda h: h.tensor_scalar(tm[:], xin3[:, kc, 3:SB3 + 3], cw[:, kc * 4 + 3:kc * 4 + 4], cb[:, kc:kc + 1],
                                                  ALU.mult, ALU.add), reads=[xinr, cr], writes=[tmr])
            for j in (2, 1, 0):
                P.op("dve", lambda h: h.scalar_tensor_tensor(tm[:], xin3[:, kc, j:SB3 + j], cw[:, kc * 4 + j:kc * 4 + j + 1], tm[:],
                                                             ALU.mult, ALU.add), reads=[xinr, cr, tmr], writes=[tmr])
            P.op("act", lambda h: h.activation(cv3[:, kc, :], tm[:], AF.Silu), reads=[tmr], writes=[cvr])
        if sbi == 0:
            P.op("pool", lambda h: h.memset(cv3[:, :, 0:112], 0.0), writes=[cvr])
        dtin, dtinr = dtin_b.next(); dtf, dtfr = dtf_b.next()
        P.dma("sp", dtin[:], dtT[:, t0:t0 + SB3], writes=[dtinr])
        P.op("act", lambda h: h.activation(dtf[:, 0:SB3], dtin[:], AF.Exp, bias=dtb_t[:, 0:1]), reads=[dtinr, cr], writes=[dtfr])
        P.op("act", lambda h: h.activation(dtf[:, 0:SB3], dtf[:, 0:SB3], AF.Ln, bias=1.0), reads=[dtfr], writes=[dtfr])
        if sbi == 0:
            P.op("pool", lambda h: h.memset(dtf[:, 0:112], 0.0), writes=[dtfr])
        P.op("dve", lambda h: h.tensor_scalar(dtf[:, SB3:2 * SB3], dtf[:, 0:SB3], acol[:, 0:1], None, ALU.mult),
             reads=[dtfr, cr], writes=[dtfr])
        for ci in range(3):
            c0 = ci * 128
            tc0 = t0 + c0
            xtok, xtokr = xtok_b.next(); btok, btokr = btok_b.next(); bct, bctr = bct_b.next(); dta, dtar = dta_b.next()
            for kc in range(8):
                ps, psr = mbank.next()
                P.op("pe", lambda h: h.transpose(ps[:, 0:128], cv3[:, kc, c0:c0 + 128], ct["ident"][:, :]), reads=[cvr, cr], writes=[psr])
                if kc % 2 == 0:
                    P.op("act", lambda h: h.copy(xtok[:, kc * 128:(kc + 1) * 128], ps[:, 0:128]), reads=[psr], writes=[xtokr[kc]])
                else:
                    P.op("dve", lambda h: h.tensor_copy(xtok[:, kc * 128:(kc + 1) * 128], ps[:, 0:128]), reads=[psr], writes=[xtokr[kc]])
            for g in range(2):
                ps, psr = mbank.next()
                P.op("pe", lambda h: h.transpose(ps[:, 0:128], cv3[:, 8 + g, c0:c0 + 128], ct["ident"][:, :]), reads=[cvr, cr], writes=[psr])
                P.op("act", lambda h: h.copy(btok[:, g * 128:(g + 1) * 128], ps[:, 0:128]), reads=[psr], writes=[btokr])
            P.op("act", lambda h: h.copy(bct[:, :].rearrange("p (k t) -> p k t", k=4), cv3[:, 8:12, c0:c0 + 128]),
                 reads=[cvr], writes=[bctr])
            ps, psr = mbank.next()
            for q in range(2):
                P.op("pe", lambda h: h.transpose(ps[:, q * 16:(q + 1) * 16], dtf[:, q * SB3 + c0:q * SB3 + c0 + 128], ct["ident"][0:16, 0:16]),
                     reads=[dtfr, cr], writes=[psr])
            P.op("dve", lambda h: h.tensor_copy(dta[:, 0:32], ps[:, 0:32]), reads=[psr], writes=[dtar])
            te, ter = te_b.next()
            ps, psr = mbank.next()
            mm(ps[:, 0:16], psr, ct["tri"][:, :], dta[:, 16:32], [cr, dtar])
            mm(ps[:, 16:32], psr, ct["ones"][:, :], dta[:, 16:32], [cr, dtar])
            P.op("dve", lambda h: h.tensor_copy(dta[:, 32:48], ps[:, 0:16]), reads=[psr], writes=[dtar])
            P.op("dve", lambda h: h.tensor_tensor(te[:, 0:16], ps[:, 16:32], dta[:, 32:48], ALU.subtract), reads=[psr, dtar], writes=[ter])
            P.op("act", lambda h: h.activation(te[:, 16:32], ps[:, 16:32], AF.Exp), reads=[psr], writes=[ter])
            P.op("act", lambda h: h.activation(te[:, 0:16], te[:, 0:16], AF.Exp), reads=[ter], writes=[ter])
            P.op("dve", lambda h: h.tensor_tensor(dta[:, 48:64], dta[:, 0:16], te[:, 0:16], ALU.mult), reads=[dtar, ter], writes=[dtar])
            P.op("act", lambda h: h.activation(te[:, 32:48], dta[:, 32:48], AF.Exp), reads=[dtar], writes=[ter])
            xdt, xdtr = xdt_b.next(); xs, xsr = xs_b.next()
            x3 = xtok[:, :].rearrange("p (j d) -> p j d", j=16)
            P.op("dve", lambda h: h.tensor_tensor(xdt[:, :].rearrange("p (j d) -> p j d", j=16), x3,
                                                  dta[:, 0:16].unsqueeze(2).to_broadcast([128, 16, 64]), ALU.mult),
                 reads=xtokr + [dtar], writes=xdtr)
            P.op("dve", lambda h: h.tensor_tensor(xs[:, :].rearrange("p (j d) -> p j d", j=16), x3,
                                                  dta[:, 48:64].unsqueeze(2).to_broadcast([128, 16, 64]), ALU.mult),
                 reads=xtokr + [dtar], writes=xsr)
            cbt, cbtr = cbt_b.next()
            for g in range(2):
                ps, psr = mbank.next()
                mm(ps[:, 0:128], psr, bct[:, g * 128:(g + 1) * 128], bct[:, (2 + g) * 128:(3 + g) * 128], [bctr])
                P.op("act", lambda h: h.copy(cbt[:, g * 128:(g + 1) * 128], ps[:, 0:128]), reads=[psr], writes=[cbtr])
            inter = []
            for g in range(2):
                ps, psr = sbank.next()
                mm(ps[:, :], psr, bct[:, (2 + g) * 128:(3 + g) * 128], stbf[:, g * 512:(g + 1) * 512], [bctr, stbf_r[g]])
                tmpi, tmpir = tmpi_b.next()
                P.op("dve", lambda h: h.tensor_tensor(tmpi[:, :].rearrange("p (j d) -> p j d", j=8),
                                                      ps[:, :].rearrange("p (j d) -> p j d", j=8),
                                                      te[:, 32 + g * 8:40 + g * 8].unsqueeze(2).to_broadcast([128, 8, 64]), ALU.mult),
                     reads=[psr, ter], writes=[tmpir])
                inter.append((tmpi, tmpir))
            def head_gen(j, slot):
                g = j // 8
                yb, ybr = ybank[g]
                seg, segr = seg_s[slot]; wj, wjr = wj_s[slot]
                ps, psr = cbanks[slot]
                mm(ps[:, 0:128], psr, dta[:, 16 + j:17 + j].to_broadcast([128, 128]), ct["tri"][:, :], [dtar, cr])
                yield
                P.op("dve", lambda h: h.scalar_tensor_tensor(seg[:], ps[:, 0:128], dta[:, 32 + j:33 + j], ct["maskneg"][:, :],
                                                             ALU.subtract, ALU.add), reads=[psr, dtar, cr], writes=[segr])
                yield
                P.op("act", lambda h: h.activation(seg[:], seg[:], AF.Exp), reads=[segr], writes=[segr])
                yield
                P.op("dve", lambda h: h.tensor_tensor(wj[:], seg[:], cbt[:, g * 128:(g + 1) * 128], ALU.mult), reads=[segr, cbtr], writes=[wjr])
                yield
                jj = j % 8
                mm(yb[:, jj * 64:(jj + 1) * 64], ybr, wj[:, :], xdt[:, j * 64:(j + 1) * 64], [wjr, xdtr[j]], True, True)

            pending = list(range(16))
            active = {}
            free_slots = list(range(NSL))
            while pending or active:
                if pending and free_slots:
                    slot = free_slots.pop(0)
                    active[slot] = head_gen(pending.pop(0), slot)
                for slot in sorted(active):
                    try:
                        next(active[slot])
                    except StopIteration:
                        del active[slot]
                        free_slots.append(slot)
            for g in range(2):
                ps, psr = sbank.next()
                mm(ps[:, :], psr, btok[:, g * 128:(g + 1) * 128], xs[:, g * 512:(g + 1) * 512], [btokr] + xsr[g * 8:(g + 1) * 8])
                for jj in range(8):
                    j = g * 8 + jj
                    P.op("dve", lambda h: h.scalar_tensor_tensor(state[:, j * 64:(j + 1) * 64], state[:, j * 64:(j + 1) * 64],
                                                                 te[:, 16 + j:17 + j], ps[:, jj * 64:(jj + 1) * 64], ALU.mult, ALU.add),
                         reads=[psr, ter, state_r[j]], writes=[state_r[j]])
                P.op("act", lambda h: h.copy(stbf[:, g * 512:(g + 1) * 512], state[:, g * 512:(g + 1) * 512]),
                     reads=state_r[g * 8:(g + 1) * 8], writes=[stbf_r[g]])
            zt, ztr = z_b.next(); yy, yyr = yy_b.next(); sq, sqr = sq_b.next(); st, str_ = st_b.next()
            P.dma("sp", zt[:], z[tc0:tc0 + 128, :], writes=[ztr])
            P.op("dve", lambda h: h.tensor_tensor(yy[:], xtok[:], dsk[:], ALU.mult), reads=xtokr + [cr], writes=[yyr])
            for g in range(2):
                yb, ybr = ybank[g]
                tmpi, tmpir = inter[g]
                P.op("dve", lambda h: h.tensor_tensor(yy[:, g * 512:(g + 1) * 512], yy[:, g * 512:(g + 1) * 512], tmpi[:, :], ALU.add),
                     reads=[tmpir, yyr], writes=[yyr])
                P.op("dve", lambda h: h.tensor_tensor(yy[:, g * 512:(g + 1) * 512], yy[:, g * 512:(g + 1) * 512], yb[:, :], ALU.add),
                     reads=[ybr, yyr], writes=[yyr])
            P.op("act", lambda h: h.activation(zt[:], zt[:], AF.Silu), reads=[ztr], writes=[ztr])
            P.op("dve", lambda h: h.tensor_tensor(yy[:], yy[:], zt[:], ALU.mult), reads=[yyr, ztr], writes=[yyr])
            P.op("act", lambda h: h.activation(sq[:], yy[:], AF.Square), reads=[yyr], writes=[sqr])
            P.op("dve", lambda h: h.tensor_reduce(st[:, 0:2], sq[:, :].rearrange("p (g c) -> p g c", g=2), AX.X, ALU.add),
                 reads=[sqr], writes=[str_])
            P.op("dve", lambda h: h.tensor_scalar(st[:, 0:2], st[:, 0:2], 1.0 / 512, 1e-5, ALU.mult, ALU.add), reads=[str_], writes=[str_])
            P.op("act", lambda h: h.activation(st[:, 0:2], st[:, 0:2], AF.Ln), reads=[str_], writes=[str_])
            P.op("act", lambda h: h.activation(st[:, 2:4], st[:, 0:2], AF.Exp, scale=-0.5), reads=[str_], writes=[str_])
            for g in range(2):
                P.op("dve", lambda h: h.scalar_tensor_tensor(yy[:, g * 512:(g + 1) * 512], yy[:, g * 512:(g + 1) * 512],
                                                             st[:, 2 + g:3 + g], nw[:, g * 512:(g + 1) * 512], ALU.mult, ALU.mult),
                     reads=[yyr, str_, cr], writes=[yyr])
            P.dma("sp", out[tc0:tc0 + 128, :], yy[:], reads=[yyr])

D_MODEL = 2048
FFN = 5632
_PROGS = {}


def _add_barrier(P):
    toks = []
    for q, sls in P.slots.items():
        for sl in sls:
            if sl.uses > 0:
                toks.append((sl.key, sl.sem, 16 * sl.uses))
    for q in ("sp", "pool", "act"):
        for t in toks:
            P._wait(P.E[q], t)


def _build_tp(N, Kmix, Cnext, first):
    key = ("tp", N, Kmix, Cnext, first)
    if key in _PROGS:
        return _PROGS[key]
    P = Prog()
    d = lambda n, s: P.dram(n, s, F32, "ExternalInput")
    L = LinCtx(P)
    resT = d("resT", [D_MODEL, N])
    mask = d("mask", [1, N])
    if not first:
        mixT = d("mixT", [Kmix, N])
        Wout = d("Wout", [16, 128, Kmix])
        gam_f = d("gam_f", [128, 16])
        Wup = d("Wup", [88, 128, D_MODEL])
        cw = d("cw", [128, 44 * 3])
        cb = d("cb", [128, 44])
        Wdown = d("Wdown", [16, 128, FFN])
        res1 = P.dram("res1", [D_MODEL, N], F32, "Internal")
        up = P.dram("up", [2 * FFN, N], F32, "Internal")
        res2 = P.dram("res2", [D_MODEL, N], F32, "ExternalOutput")
        linear_stage(L, mode="plain", K=Kmix, C=D_MODEL, N=N, W=Wout, dst=res1, src=mixT, res=resT)
        _add_barrier(P)
        linear_stage(L, mode="norm", K=D_MODEL, C=2 * FFN, N=N, W=Wup, dst=up, src=res1, gamma=gam_f, mask=mask)
        _add_barrier(P)
        linear_stage(L, mode="convglu", K=FFN, C=D_MODEL, N=N, W=Wdown, dst=res2, src=up[0:FFN, :], src2=up[FFN:2 * FFN, :],
                     convw=cw, convb=cb, res=res1)
        src_next = res2
    else:
        src_next = resT
    if Cnext:
        gam_n = d("gam_n", [128, 16])
        CBn = (Cnext + 127) // 128
        Win = d("Win", [CBn, 128, D_MODEL])
        znext = P.dram("znext", [Cnext, N], F32, "ExternalOutput")
        if not first:
            _add_barrier(P)
        linear_stage(L, mode="norm", K=D_MODEL, C=Cnext, N=N, W=Win, dst=znext, src=src_next, gamma=gam_n, mask=mask)
    nc = P.finalize()
    _PROGS[key] = nc
    return nc


def _build_attn(PN):
    key = ("attn", PN)
    if key in _PROGS:
        return _PROGS[key]
    P = Prog()
    d = lambda n, s: P.dram(n, s, F32, "ExternalInput")
    qT = d("qT", [64, 4, PN]); kT = d("kT", [64, PN]); vv = d("v", [PN, 64])
    qwd = d("qw", [64, 1]); kwd = d("kw", [64, 1]); sk = d("sinks", [1, 4]); mk = d("masks", [4, 128, 512])
    out = P.dram("out", [PN, 256], F32, "ExternalOutput")
    attn_stage(P, PN=PN, qT=qT, kT=kT, v=vv, qw=qwd, kw=kwd, sinks=sk, masks=mk, out=out)
    nc = P.finalize()
    _PROGS[key] = nc
    return nc


def _build_rwkv(PN):
    key = ("rwkv", PN)
    if key in _PROGS:
        return _PROGS[key]
    P = Prog()
    d = lambda n, s: P.dram(n, s, F32, "ExternalInput")
    C = rwkv_consts()
    rkvp = d("rkvp", [PN + 1, 768]); xwT = d("xwT", [64, PN + 1]); xaT = d("xaT", [64, PN + 1]); xgT = d("xgT", [160, PN + 1])
    mu_rkv = d("mu_rkv", [1, 768]); mu_w = d("mu_w", [64, 1]); mu_a = d("mu_a", [64, 1]); mu_g = d("mu_g", [160, 1])
    rows = d("rows", [7, 256]); wup = d("w_up", [64, 256]); aup = d("a_up", [64, 256]); gup = d("g_up", [160, 256])
    consts = {k: d("c_" + k, list(v.shape)) for k, v in C.items()}
    out = P.dram("out", [PN, 256], F32, "ExternalOutput")
    rwkv_stage(P, PN=PN, rkvp=rkvp, xwT=xwT, xaT=xaT, xgT=xgT, mu_rkv=mu_rkv, mu_w=mu_w, mu_a=mu_a, mu_g=mu_g, rows=rows,
               w_up=wup, a_up=aup, g_up=gup, consts=consts, out=out)
    nc = P.finalize()
    _PROGS[key] = nc
    return nc


def _build_ssd(PN):
    key = ("ssd", PN)
    if key in _PROGS:
        return _PROGS[key]
    P = Prog()
    d = lambda n, s: P.dram(n, s, F32, "ExternalInput")
    C = ssd_consts()
    xbcT = d("xbcT", [1536, PN + 3]); convw = d("convw", [128, 48]); convb = d("convb", [128, 12]); z = d("z", [PN, 1024])
    dtT = d("dtT", [16, PN]); dtb = d("dtb", [16, 1]); alog = d("alog", [16, 1]); dskip = d("dskip", [1, 1024])
    normw = d("normw", [1, 1024])
    consts = {k: d("c_" + k, [128, 128]) for k in C}
    out = P.dram("out", [PN, 1024], F32, "ExternalOutput")
    ssd_stage(P, PN=PN, xbcT=xbcT, convw=convw, convb=convb, z=z, dtT=dtT, dtb=dtb, alog=alog, dskip=dskip, normw=normw,
              consts=consts, out=out)
    nc = P.finalize()
    _PROGS[key] = nc
    return nc


def _c(a):
    return np.ascontiguousarray(a, dtype=np.float32)


def _gam(g):
    return _c(g.reshape(16, 128).T)


def _launch(nc, maps):
    res = run_bass_kernel_spmd(nc, maps, core_ids=list(range(8)))
    return res.results


def kernel(x, meta_tokens, mix_norm_w, ffn_norm_w, ar_w_in, ar_shift_mu, attn_q_norm_w, attn_k_norm_w, attn_sinks,
           rwkv_w0, rwkv_w_up, rwkv_a0, rwkv_a_up, rwkv_g_up, rwkv_k_k, rwkv_k_a, rwkv_r_k, rwkv_ln_w, rwkv_ln_b,
           ar_w_out, ssd_w_in, ssd_conv_w, ssd_conv_b, ssd_dt_bias, ssd_a_log, ssd_d, ssd_norm_w, ssd_w_out,
           ffn_w_up, ffn_conv_w, ffn_conv_b, ffn_w_down):
    f = lambda a: np.asarray(a, dtype=np.float32)
    x = f(x)
    B, SEQ, D = x.shape
    depth = mix_norm_w.shape[0]
    PN = SEQ + 128
    TOT = B * PN
    stride = TOT // 8
    assert stride * 8 == TOT
    N = -(-(stride + 2) // 704) * 704
    H = N - stride

    def windows_T(glob):
        Cc = glob.shape[1]
        outs = []
        for c in range(8):
            lo = c * stride - H
            w = np.zeros((Cc, N), np.float32)
            s0 = max(lo, 0)
            w[:, s0 - lo:] = glob[s0:lo + N].T
            outs.append(w)
        return outs

    def unwindow(outs, name):
        Cc = outs[0][name].shape[0]
        glob = np.empty((TOT, Cc), np.float32)
        for c in range(8):
            glob[c * stride:(c + 1) * stride] = outs[c][name][:, H:].T
        return glob

    res_glob = np.zeros((TOT, D), np.float32)
    valid = np.zeros((TOT, 1), np.float32)
    for b in range(B):
        res_glob[b * PN + 112:b * PN + 128] = f(meta_tokens)
        res_glob[b * PN + 128:(b + 1) * PN] = x[b]
        valid[b * PN + 112:(b + 1) * PN] = 1.0
    mask_w = [_c(w) for w in windows_T(valid)]

    def in_proj_weights(layer):
        i = layer // 2
        if layer % 2 == 0:
            return tile_w(f(ar_w_in[i])), ar_w_in.shape[2]
        return tile_w(f(ssd_w_in[i])), ssd_w_in.shape[2]

    Win, Cn = in_proj_weights(0)
    nc = _build_tp(N, 0, Cn, True)
    res_w = windows_T(res_glob)
    gam_n = _gam(f(mix_norm_w[0]))
    outs = _launch(nc, [{"resT": res_w[c], "mask": mask_w[c], "gam_n": gam_n, "Win": Win} for c in range(8)])
    z_glob = unwindow(outs, "znext")
    del Win

    amasks = attn_masks()
    rconsts = rwkv_consts()
    sconsts = ssd_consts()
    for layer in range(depth):
        i = layer // 2
        if layer % 2 == 0:
            nc = _build_attn(PN)
            maps = []
            for c in range(8):
                b, g = c // 4, c % 4
                zb = z_glob[b * PN:(b + 1) * PN]
                maps.append({"qT": _c(zb[:, g * 256:(g + 1) * 256].reshape(PN, 4, 64).transpose(2, 1, 0)),
                             "kT": _c(zb[:, 1024 + g * 64:1024 + (g + 1) * 64].T),
                             "v": _c(zb[:, 1280 + g * 64:1280 + (g + 1) * 64]),
                             "qw": _c(f(attn_q_norm_w[i])[:, None]), "kw": _c(f(attn_k_norm_w[i])[:, None]),
                             "sinks": _c(f(attn_sinks[i])[None, 4 * g:4 * g + 4]), "masks": amasks})
            outs = _launch(nc, maps)
            mix_glob = np.empty((TOT, 2048), np.float32)
            for c in range(8):
                b, g = c // 4, c % 4
                mix_glob[b * PN:(b + 1) * PN, g * 256:(g + 1) * 256] = outs[c]["out"]
            nc = _build_rwkv(PN)
            mu = f(ar_shift_mu[i])
            w0, a0, k_k, k_a = f(rwkv_w0[i]), f(rwkv_a0[i]), f(rwkv_k_k[i]), f(rwkv_k_a[i])
            r_k, ln_w, ln_b = f(rwkv_r_k[i]).reshape(-1), f(rwkv_ln_w[i]), f(rwkv_ln_b[i])
            w_up, a_up, g_up = f(rwkv_w_up[i]), f(rwkv_a_up[i]), f(rwkv_g_up[i])
            maps = []
            for c in range(8):
                b, u = c // 4, c % 4
                z0 = z_glob[b * PN:(b + 1) * PN, 1536:]
                cs = slice(256 * u, 256 * u + 256)
                rkv = np.concatenate([z0[:, 0:1024][:, cs], z0[:, 1024:2048][:, cs], z0[:, 2048:3072][:, cs]], 1)
                m = {"rkvp": np.concatenate([np.zeros((1, 768), np.float32), rkv], 0),
                     "xwT": np.concatenate([np.zeros((64, 1), np.float32), z0[:, 3072:3136].T], 1),
                     "xaT": np.concatenate([np.zeros((64, 1), np.float32), z0[:, 3136:3200].T], 1),
                     "xgT": np.concatenate([np.zeros((160, 1), np.float32), z0[:, 3200:3360].T], 1),
                     "mu_rkv": np.concatenate([mu[0:1024][cs], mu[1024:2048][cs], mu[2048:3072][cs]])[None, :],
                     "mu_w": mu[3072:3136, None], "mu_a": mu[3136:3200, None], "mu_g": mu[3200:3360, None],
                     "rows": np.stack([w0[cs], a0[cs], k_k[cs], k_a[cs], r_k[cs], ln_w[cs], ln_b[cs]]),
                     "w_up": w_up[:, cs], "a_up": a_up[:, cs], "g_up": g_up[:, cs]}
                for k, v in rconsts.items():
                    m["c_" + k] = v
                maps.append({k: _c(v) for k, v in m.items()})
            outs = _launch(nc, maps)
            for c in range(8):
                b, u = c // 4, c % 4
                mix_glob[b * PN:(b + 1) * PN, 1024 + u * 256:1024 + (u + 1) * 256] = outs[c]["out"]
            Wout = tile_w(f(ar_w_out[i]))
            Kmix = 2048
        else:
            nc = _build_ssd(PN)
            conv_w, conv_b = f(ssd_conv_w[i]), f(ssd_conv_b[i])
            dt_bias, a_log, d_skip, norm_w = f(ssd_dt_bias[i]), f(ssd_a_log[i]), f(ssd_d[i]), f(ssd_norm_w[i])
            maps = []
            for c in range(8):
                b, cidx = c // 4, c % 4
                z0 = z_glob[b * PN:(b + 1) * PN]
                g0 = 2 * cidx
                xs_ = slice(4096 + g0 * 512, 4096 + g0 * 512 + 1024)
                bs_ = slice(8192 + g0 * 128, 8192 + g0 * 128 + 256)
                cs_ = slice(9216 + g0 * 128, 9216 + g0 * 128 + 256)
                chan = np.concatenate([np.arange(4096)[g0 * 512:g0 * 512 + 1024], 4096 + np.arange(1024)[g0 * 128:g0 * 128 + 256],
                                       5120 + np.arange(1024)[g0 * 128:g0 * 128 + 256]])
                xbc = np.concatenate([z0[:, xs_], z0[:, bs_], z0[:, cs_]], 1)
                hs = slice(g0 * 8, g0 * 8 + 16)
                m = {"xbcT": np.concatenate([np.zeros((1536, 3), np.float32), xbc.T], 1),
                     "convw": conv_w[:, chan].T.reshape(12, 128, 4).transpose(1, 0, 2).reshape(128, 48),
                     "convb": conv_b[chan].reshape(12, 128).T,
                     "z": z0[:, g0 * 512:g0 * 512 + 1024],
                     "dtT": z0[:, 10240 + g0 * 8:10240 + g0 * 8 + 16].T, "dtb": dt_bias[hs, None], "alog": a_log[hs, None],
                     "dskip": np.repeat(d_skip[hs], 64)[None, :], "normw": norm_w[g0 * 512:g0 * 512 + 1024][None, :]}
                for k, v in sconsts.items():
                    m["c_" + k] = v
                maps.append({k: _c(v) for k, v in m.items()})
            outs = _launch(nc, maps)
            mix_glob = np.empty((TOT, 4096), np.float32)
            for c in range(8):
                b, cidx = c // 4, c % 4
                mix_glob[b * PN:(b + 1) * PN, cidx * 1024:(cidx + 1) * 1024] = outs[c]["out"]
            Wout = tile_w(f(ssd_w_out[i]))
            Kmix = 4096
        del z_glob, outs
        last = (layer == depth - 1)
        if not last:
            Win, Cn = in_proj_weights(layer + 1)
        else:
            Win, Cn = None, 0
        nc = _build_tp(N, Kmix, Cn, False)
        mix_w = windows_T(mix_glob)
        res_w = windows_T(res_glob)
        del mix_glob
        cwv = f(ffn_conv_w[layer])
        shared = {"Wout": Wout, "gam_f": _gam(f(ffn_norm_w[layer])), "Wup": tile_w(f(ffn_w_up[layer])),
                  "cw": _c(cwv.T.reshape(44, 128, 3).transpose(1, 0, 2).reshape(128, 132)),
                  "cb": _c(f(ffn_conv_b[layer]).reshape(44, 128).T), "Wdown": tile_w(f(ffn_w_down[layer]))}
        if not last:
            shared["gam_n"] = _gam(f(mix_norm_w[layer + 1]))
            shared["Win"] = Win
        maps = []
        for c in range(8):
            m = dict(shared)
            m["mixT"] = mix_w[c]
            m["resT"] = res_w[c]
            m["mask"] = mask_w[c]
            maps.append(m)
        outs = _launch(nc, maps)
        del maps, shared, mix_w, res_w, Wout, Win
        res_glob = unwindow(outs, "res2")
        if not last:
            z_glob = unwindow(outs, "znext")
        del outs
    out = np.empty((B, SEQ, D), np.float32)
    for b in range(B):
        out[b] = res_glob[b * PN + 128:(b + 1) * PN]
    return out
```

```python
from contextlib import ExitStack
import numpy as np
import concourse.bass as bass
import concourse.mybir as mybir
from concourse.bass_utils import run_bass_kernel_spmd

F32 = mybir.dt.float32
BF16 = mybir.dt.bfloat16
AF = mybir.ActivationFunctionType
ALU = mybir.AluOpType
AX = mybir.AxisListType


class Res:
    __slots__ = ("w", "r", "excl")

    def __init__(self, excl=False):
        self.w = None
        self.r = {}
        self.excl = excl


class _Eng:
    def __init__(self, name, sem):
        self.name = name
        self.sem = sem
        self.count = 0
        self.waited = {}
        self.ops = []


class _Rec:
    def __init__(self):
        self.call = None

    def __getattr__(self, name):
        def f(*args, **kwargs):
            self.call = (name, args, kwargs)
            return self
        return f


class _Slot:
    def __init__(self, key, sem):
        self.key = key
        self.sem = sem
        self.uses = 0


class Prog:
    ENGS = ("sp", "act", "dve", "pool", "pe")

    def __init__(self, nslots=6, self_sync=True):
        self.nc = bass.Bass("TRN2", target_bir_lowering=False)
        self.es = ExitStack()
        self.self_sync = self_sync
        self.E = {}
        for n in self.ENGS:
            self.E[n] = _Eng(n, self.es.enter_context(self.nc.semaphore("s_" + n)))
        self.slots = {}
        self.slot_rr = {}
        for q in ("sp", "pool", "act"):
            self.slots[q] = [_Slot("d_%s%d" % (q, i), self.es.enter_context(self.nc.semaphore("d_%s%d" % (q, i))))
                             for i in range(nslots)]
            self.slot_rr[q] = 0
        self._n = 0

    def dram(self, name, shape, dtype, kind):
        return self.nc.dram_tensor(name, list(shape), dtype, kind=kind).ap()

    def sbuf(self, shape, dtype, name=None):
        self._n += 1
        return self.es.enter_context(self.nc.sbuf_tensor(name or "sb%d" % self._n, list(shape), dtype))

    def psum(self, shape, dtype, name=None):
        self._n += 1
        return self.es.enter_context(self.nc.psum_tensor(name or "ps%d" % self._n, list(shape), dtype))

    def _wait(self, E, tok):
        key, sem, val = tok
        if E.waited.get(key, 0) >= val:
            return
        E.waited[key] = val
        E.ops.append(lambda h, sem=sem, val=val: h.wait_ge(sem, val))

    def _deps(self, E, reads, writes, ekey):
        toks = []
        for r in reads:
            if r.w is not None:
                toks.append(r.w)
        for w in writes:
            if w.w is not None:
                toks.append(w.w)
            for k, t in w.r.items():
                if k == ekey:
                    continue
                toks.append(t)
        for t in toks:
            if t[0] == ekey and (E.name == "pe" or not self.self_sync):
                continue
            self._wait(E, t)

    def op(self, eng, fn, reads=(), writes=()):
        E = self.E[eng]
        ekey = "s_" + eng
        ex = [r for r in reads if r.excl]
        if ex:
            writes = list(writes) + ex
            reads = [r for r in reads if not r.excl]
        self._deps(E, reads, writes, ekey)
        E.count += 1
        tok = (ekey, E.sem, E.count)
        rec = _Rec()
        fn(rec)
        E.ops.append(lambda h, c=rec.call, sem=E.sem: getattr(h, c[0])(*c[1], **c[2]).then_inc(sem, 1))
        E.waited[ekey] = max(E.waited.get(ekey, 0), 0)
        for w in writes:
            w.w = tok
            w.r = {}
        for r in reads:
            r.r[ekey] = tok
        return tok

    def dma(self, q, out, in_, reads=(), writes=()):
        E = self.E[q]
        sl = self.slots[q][self.slot_rr[q] % len(self.slots[q])]
        self.slot_rr[q] += 1
        if sl.uses > 0:
            self._wait(E, (sl.key, sl.sem, 16 * sl.uses))
        self._deps(E, reads, writes, None)
        sl.uses += 1
        tok = (sl.key, sl.sem, 16 * sl.uses)
        E.ops.append(lambda h, out=out, in_=in_, sem=sl.sem: h.dma_start(out=out, in_=in_).then_inc(sem, 16))
        for w in writes:
            w.w = tok
            w.r = {}
        for r in reads:
            r.r[sl.key] = tok
        return tok

    def finalize(self):
        for q, sls in self.slots.items():
            for sl in sls:
                if sl.uses > 0:
                    self._wait(self.E[q], (sl.key, sl.sem, 16 * sl.uses))
        nc = self.nc
        with nc.Block() as block:
            decos = {"sp": block.sync, "act": block.scalar, "dve": block.vector,
                     "pool": block.gpsimd, "pe": block.tensor}
            for n in self.ENGS:
                ops = self.E[n].ops
                if not ops:
                    continue

                def body(h, ops=ops):
                    for f in ops:
                        f(h)
                decos[n](body)
        self.es.close()
        return nc

    def ninstr(self):
        return {n: len(self.E[n].ops) for n in self.ENGS}

NT = 352


class LinCtx:
    def __init__(self, P, KCmax_small=16, TSmax=1408, KCmax=44, opnd_elems=44 * 704):
        self.P = P
        self.opnd = P.sbuf([128, opnd_elems], BF16, "opnd")
        self.opnd_r = [Res() for _ in range(KCmax)]
        self.NST = 3
        self.stg = [P.sbuf([128, TSmax + 2], F32, "stg%d" % i) for i in range(self.NST)]
        self.stg_r = [Res() for _ in range(self.NST)]
        self.stg_i = 0
        self.stg2 = [P.sbuf([128, 704], F32, "stgv%d" % i) for i in range(2)]
        self.stg2_r = [Res() for _ in range(2)]
        self.stg2_i = 0
        self.tmp = [P.sbuf([128, TSmax], F32, "tmp%d" % i) for i in range(2)]
        self.tmp_r = [Res() for _ in range(2)]
        self.tmp_i = 0
        self.sq = [P.sbuf([128, TSmax], BF16, "sq%d" % i) for i in range(2)]
        self.sq_r = [Res() for _ in range(2)]
        self.rstd = P.sbuf([128, TSmax], F32, "rstd")
        self.rstd_r = Res()
        self.maskb = P.sbuf([128, TSmax], F32, "maskb")
        self.maskb_r = Res()
        self.NW = 3
        self.wbf = [P.sbuf([128, KCmax * 128], BF16, "wbf%d" % i) for i in range(self.NW)]
        self.wbf_r = [Res() for _ in range(self.NW)]
        self.w_i = 0
        self.osb = [P.sbuf([128, TSmax], F32, "osb%d" % i) for i in range(2)]
        self.osb_r = [[Res() for _ in range(4)] for _ in range(2)]
        self.rsb = [P.sbuf([128, TSmax], F32, "rsb%d" % i) for i in range(2)]
        self.rsb_r = [Res() for _ in range(2)]
        self.o_i = 0
        self.ps = [P.psum([128, 512], F32, "lps%d" % i) for i in range(8)]
        self.ps_r = [Res(excl=True) for _ in range(8)]
        self.ps_i = 0
        self.ones = P.sbuf([128, 128], BF16, "ones_bf")
        self.ones_r = Res()
        P.op("dve", lambda h: h.memset(self.ones[:], 1.0), writes=[self.ones_r])
        self.small = {}

    def const(self, name, shape, src_ap, q="sp"):
        t = self.P.sbuf(shape, F32, name)
        r = Res()
        self.P.dma(q, t[:], src_ap, writes=[r])
        return t, r


def linear_stage(L, *, mode, K, C, N, W, dst, src=None, src2=None, gamma=None, mask=None,
                 convw=None, convb=None, res=None, eps=1e-6):
    P = L.P
    KC = K // 128
    CB = (C + 127) // 128
    TS = 1408 if (K == 2048 and N % 1408 == 0) else 704
    assert N % TS == 0
    nsub = TS // NT
    opnd = L.opnd
    opv = opnd[:, 0:KC * TS].rearrange("p (k t) -> p k t", k=KC)

    if mode == "norm":
        gam_t, gam_r = L.const("gam%d" % id(gamma), [128, KC], gamma)
    if mode == "convglu":
        cw_t, cw_r = L.const("cw%d" % id(convw), [128, KC * 3], convw)
        cb_t, cb_r = L.const("cb%d" % id(convb), [128, KC], convb)

    for ts in range(N // TS):
        t0 = ts * TS
        if mode == "plain":
            for kc in range(KC):
                P.dma("pool", opv[:, kc, :], src[kc * 128:(kc + 1) * 128, t0:t0 + TS], writes=[L.opnd_r[kc]])
        elif mode == "norm":
            P.dma("pool", L.maskb[:, 0:TS], mask[0:1, t0:t0 + TS].partition_broadcast(128), writes=[L.maskb_r])
            banks = []
            for s in range(nsub):
                b = L.ps_i % 8
                L.ps_i += 1
                banks.append(b)
            for kc in range(KC):
                i = L.stg_i % L.NST
                L.stg_i += 1
                st, sr = L.stg[i], L.stg_r[i]
                P.dma("sp", st[:, 0:TS], src[kc * 128:(kc + 1) * 128, t0:t0 + TS], writes=[sr])
                sq, sqr = L.sq[kc % 2], L.sq_r[kc % 2]
                P.op("act", lambda h, st=st, sq=sq: h.activation(sq[:, 0:TS], st[:, 0:TS], AF.Square),
                     reads=[sr], writes=[sqr])
                for s in range(nsub):
                    b = banks[s]
                    P.op("pe", lambda h, b=b, sq=sq, s=s, kc=kc: h.matmul(
                        L.ps[b][:, 0:NT], L.ones[:, :], sq[:, s * NT:(s + 1) * NT],
                        start=(kc == 0), stop=(kc == KC - 1)),
                        reads=[sqr, L.ones_r], writes=[L.ps_r[b]])
            for s in range(nsub):
                b = banks[s]
                P.op("dve", lambda h, b=b, s=s: h.tensor_scalar(
                    L.rstd[:, s * NT:(s + 1) * NT], L.ps[b][:, 0:NT], 1.0 / K, eps, ALU.mult, ALU.add),
                    reads=[L.ps_r[b]], writes=[L.rstd_r])
            P.op("act", lambda h: h.activation(L.rstd[:, 0:TS], L.rstd[:, 0:TS], AF.Ln),
                 reads=[L.rstd_r], writes=[L.rstd_r])
            P.op("act", lambda h: h.activation(L.rstd[:, 0:TS], L.rstd[:, 0:TS], AF.Exp, scale=-0.5),
                 reads=[L.rstd_r], writes=[L.rstd_r])
            P.op("dve", lambda h: h.tensor_tensor(L.rstd[:, 0:TS], L.rstd[:, 0:TS], L.maskb[:, 0:TS], ALU.mult),
                 reads=[L.rstd_r, L.maskb_r], writes=[L.rstd_r])
            for kc in range(KC):
                i = L.stg_i % L.NST
                L.stg_i += 1
                st, sr = L.stg[i], L.stg_r[i]
                P.dma("sp", st[:, 0:TS], src[kc * 128:(kc + 1) * 128, t0:t0 + TS], writes=[sr])
                eng = "dve"
                P.op(eng, lambda h, st=st, kc=kc: h.scalar_tensor_tensor(
                    opv[:, kc, :], st[:, 0:TS], gam_t[:, kc:kc + 1], L.rstd[:, 0:TS], ALU.mult, ALU.mult),
                    reads=[sr, gam_r, L.rstd_r], writes=[L.opnd_r[kc]])
        elif mode == "convglu":
            for kc in range(KC):
                i = L.stg_i % L.NST
                L.stg_i += 1
                st, sr = L.stg[i], L.stg_r[i]
                rows = slice(kc * 128, (kc + 1) * 128)
                if t0 == 0:
                    P.op("pool", lambda h, st=st: h.memset(st[:, 0:2], 0.0), writes=[sr])
                    P.dma("sp", st[:, 2:TS + 2], src[rows, 0:TS], writes=[sr])
                else:
                    P.dma("sp", st[:, 0:TS + 2], src[rows, t0 - 2:t0 + TS], writes=[sr])
                j = L.stg2_i % 2
                L.stg2_i += 1
                sv, svr = L.stg2[j], L.stg2_r[j]
                P.dma("pool", sv[:, 0:TS], src2[rows, t0:t0 + TS], writes=[svr])
                k = L.tmp_i % 2
                L.tmp_i += 1
                tm, tmr = L.tmp[k], L.tmp_r[k]
                e1 = "dve"
                P.op("act", lambda h, st=st, tm=tm, kc=kc: h.activation(
                    tm[:, 0:TS], st[:, 2:TS + 2], AF.Identity, bias=cb_t[:, kc:kc + 1], scale=cw_t[:, kc * 3 + 2:kc * 3 + 3]),
                    reads=[sr, cw_r, cb_r], writes=[tmr])
                P.op(e1, lambda h, st=st, tm=tm, kc=kc: h.scalar_tensor_tensor(
                    tm[:, 0:TS], st[:, 1:TS + 1], cw_t[:, kc * 3 + 1:kc * 3 + 2], tm[:, 0:TS],
                    ALU.mult, ALU.add), reads=[sr, cw_r, tmr], writes=[tmr])
                P.op(e1, lambda h, st=st, tm=tm, kc=kc: h.scalar_tensor_tensor(
                    tm[:, 0:TS], st[:, 0:TS], cw_t[:, kc * 3:kc * 3 + 1], tm[:, 0:TS],
                    ALU.mult, ALU.add), reads=[sr, cw_r, tmr], writes=[tmr])
                P.op("act", lambda h, tm=tm: h.activation(tm[:, 0:TS], tm[:, 0:TS], AF.Silu),
                     reads=[tmr], writes=[tmr])
                P.op("dve", lambda h, tm=tm, sv=sv, kc=kc: h.tensor_tensor(
                    opv[:, kc, :], tm[:, 0:TS], sv[:, 0:TS], ALU.mult),
                    reads=[tmr, svr], writes=[L.opnd_r[kc]])
        else:
            raise ValueError(mode)

        NW = L.NW
        base = L.w_i
        for pre in range(min(NW - 1, CB)):
            wi = (base + pre) % NW
            P.dma("pool", L.wbf[wi][:, 0:KC * 128], W[pre], writes=[L.wbf_r[wi]])
        if res is not None:
            M0 = min(128, C)
            P.dma("sp", L.rsb[L.o_i % 2][0:M0, 0:TS], res[0:M0, t0:t0 + TS], writes=[L.rsb_r[L.o_i % 2]])
        for cb in range(CB):
            M = min(128, C - cb * 128)
            wi = (base + cb) % NW
            wbf, wbfr = L.wbf[wi], L.wbf_r[wi]
            if cb + NW - 1 < CB:
                wj = (base + cb + NW - 1) % NW
                P.dma("pool", L.wbf[wj][:, 0:KC * 128], W[cb + NW - 1], writes=[L.wbf_r[wj]])
            oi = L.o_i % 2
            L.o_i += 1
            osb, osbr, rsb, rsbr = L.osb[oi], L.osb_r[oi], L.rsb[oi], L.rsb_r[oi]
            if res is not None and cb + 1 < CB:
                M1 = min(128, C - (cb + 1) * 128)
                P.dma("sp", L.rsb[1 - oi][0:M1, 0:TS], res[(cb + 1) * 128:(cb + 1) * 128 + M1, t0:t0 + TS], writes=[L.rsb_r[1 - oi]])
            for s in range(nsub):
                b = L.ps_i % 8
                L.ps_i += 1
                for kc in range(KC):
                    P.op("pe", lambda h, b=b, wbf=wbf, kc=kc, s=s, M=M: h.matmul(
                        L.ps[b][0:M, 0:NT], wbf[:, kc * 128:kc * 128 + M], opv[:, kc, s * NT:(s + 1) * NT],
                        start=(kc == 0), stop=(kc == KC - 1)),
                        reads=[wbfr, L.opnd_r[kc]], writes=[L.ps_r[b]])
                if res is not None:
                    P.op("dve", lambda h, b=b, s=s, M=M, osb=osb, rsb=rsb: h.tensor_tensor(
                        osb[0:M, s * NT:(s + 1) * NT], L.ps[b][0:M, 0:NT], rsb[0:M, s * NT:(s + 1) * NT], ALU.add),
                        reads=[L.ps_r[b], rsbr], writes=[osbr[s]])
                else:
                    if s % 2 == 0:
                        P.op("act", lambda h, b=b, s=s, M=M, osb=osb: h.copy(
                            osb[0:M, s * NT:(s + 1) * NT], L.ps[b][0:M, 0:NT]),
                            reads=[L.ps_r[b]], writes=[osbr[s]])
                    else:
                        P.op("dve", lambda h, b=b, s=s, M=M, osb=osb: h.tensor_copy(
                            osb[0:M, s * NT:(s + 1) * NT], L.ps[b][0:M, 0:NT]),
                            reads=[L.ps_r[b]], writes=[osbr[s]])
            P.dma("sp", dst[cb * 128:cb * 128 + M, t0:t0 + TS], osb[0:M, 0:TS], reads=osbr[0:nsub])
        L.w_i = base + CB


def tile_w(W, dtype=np.float32):
    K, C = W.shape
    CB = (C + 127) // 128
    KC = K // 128
    Wp = np.zeros((K, CB * 128), dtype)
    Wp[:, :C] = W
    return np.ascontiguousarray(Wp.reshape(KC, 128, CB, 128).transpose(2, 1, 0, 3).reshape(CB, 128, KC * 128))

TQ = 384


def attn_stage(P, *, PN, qT, kT, v, qw, kw, sinks, masks, out):
    NB = PN // 128
    NTL = PN // TQ
    assert PN % TQ == 0
    ones64 = P.sbuf([64, 64], F32, "a_ones64")
    ones_r = Res()
    P.op("dve", lambda h: h.memset(ones64[:], 1.0), writes=[ones_r])
    qw_t = P.sbuf([64, 1], F32, "a_qw")
    kw_t = P.sbuf([64, 1], F32, "a_kw")
    cr = Res()
    P.dma("sp", qw_t[:], qw, writes=[cr])
    P.dma("sp", kw_t[:], kw, writes=[cr])
    P.op("dve", lambda h: h.tensor_scalar(qw_t[:], qw_t[:], 0.125, None, ALU.mult), reads=[cr], writes=[cr])
    esink = P.sbuf([128, 4], F32, "a_esink")
    es_r = Res()
    P.dma("sp", esink[:], sinks[0:1, :].partition_broadcast(128), writes=[es_r])
    P.op("act", lambda h: h.activation(esink[:], esink[:], AF.Exp), reads=[es_r], writes=[es_r])
    mstage = P.sbuf([128, 512], F32, "a_mstage")
    ms_r = Res()
    mk = []
    mk_r = []
    for i in range(4):
        m = P.sbuf([128, 512], BF16, "a_mask%d" % i)
        r = Res()
        P.dma("sp", mstage[:], masks[i], writes=[ms_r])
        P.op("dve", lambda h, m=m: h.tensor_copy(m[:], mstage[:]), reads=[ms_r], writes=[r])
        mk.append(m)
        mk_r.append(r)
    M_CUR, M_PREV, M_CUR0, M_PREV1 = 0, 1, 2, 3

    khat = P.sbuf([64, PN], BF16, "a_khat")
    khat_r = [Res() for _ in range(NTL)]
    vaug = P.sbuf([128, NB * 65], BF16, "a_vaug")
    vaug3 = vaug[:, :].rearrange("p (n c) -> p n c", c=65)
    vaug_r = [Res() for _ in range(NTL)]
    vones_r = Res()
    P.op("pool", lambda h: h.memset(vaug[:], 1.0), writes=[vones_r])
    vmeta = P.sbuf([16, 65], BF16, "a_vmeta")
    vmeta_s = P.sbuf([16, 64], F32, "a_vmeta_s")
    vm_r = Res()
    P.op("pool", lambda h: h.memset(vmeta[:], 1.0), writes=[vm_r])
    P.dma("sp", vmeta_s[:], v[112:128, :], writes=[vm_r])
    P.op("dve", lambda h: h.tensor_copy(vmeta[:, 0:64], vmeta_s[:]), reads=[vm_r], writes=[vm_r])

    NBUF = 2
    qst = [P.sbuf([64, 4 * TQ], F32, "a_qst%d" % i) for i in range(NBUF)]
    qst_r = [Res() for _ in range(NBUF)]
    kst = [P.sbuf([64, TQ], F32, "a_kst%d" % i) for i in range(NBUF)]
    kst_r = [Res() for _ in range(NBUF)]
    vst = [P.sbuf([128, 3 * 64], F32, "a_vst%d" % i) for i in range(NBUF)]
    vst_r = [Res() for _ in range(NBUF)]
    sqb = [P.sbuf([64, 5 * TQ], F32, "a_sq%d" % i) for i in range(NBUF)]
    sqb_r = [Res() for _ in range(NBUF)]
    rq = [P.sbuf([64, 5 * TQ], F32, "a_rq%d" % i) for i in range(NBUF)]
    rq_r = [Res() for _ in range(NBUF)]
    qhat = [P.sbuf([64, 4 * TQ], BF16, "a_qhat%d" % i) for i in range(NBUF)]
    qhat_r = [Res() for _ in range(NBUF)]
    osb = [P.sbuf([128, 3 * 256], F32, "a_osb%d" % i) for i in range(NBUF)]
    osb_r = [Res() for _ in range(NBUF)]
    NPT = 3
    pt = [P.sbuf([128, 512], BF16, "a_pt%d" % i) for i in range(NPT * 3)]
    pt_r = [Res() for _ in range(NPT * 3)]
    den = [P.sbuf([128, 8], F32, "a_den%d" % i) for i in range(2)]
    den_r = [Res() for _ in range(2)]
    ps_sq = [P.psum([64, 512], F32, "a_pssq%d" % i) for i in range(2)]
    ps_sq_r = [Res(excl=True) for _ in range(2)]
    ps_sc = [P.psum([128, 512], F32, "a_pssc%d" % i) for i in range(4)]
    ps_sc_r = [Res(excl=True) for _ in range(4)]
    ps_o = [P.psum([128, 4 * 65], F32, "a_pso%d" % i) for i in range(2)]
    ps_o_r = [Res(excl=True) for _ in range(2)]
    cnt = {"sq": 0, "sc": 0, "o": 0, "pt": 0}

    for tl in range(NTL):
        bi = tl % NBUF
        t0 = tl * TQ
        P.dma("sp", qst[bi][:, :].rearrange("p (h t) -> p h t", h=4), qT[:, :, t0:t0 + TQ], writes=[qst_r[bi]])
        P.dma("sp", kst[bi][:, :], kT[:, t0:t0 + TQ], writes=[kst_r[bi]])
        P.dma("pool", vst[bi][:, :].rearrange("p (n c) -> p n c", c=64),
              v[t0:t0 + TQ, :].rearrange("(n p) c -> p n c", p=128), writes=[vst_r[bi]])
        P.op("act", lambda h, bi=bi: h.activation(sqb[bi][:, 0:4 * TQ], qst[bi][:, :], AF.Square),
             reads=[qst_r[bi]], writes=[sqb_r[bi]])
        P.op("act", lambda h, bi=bi: h.activation(sqb[bi][:, 4 * TQ:5 * TQ], kst[bi][:, :], AF.Square),
             reads=[kst_r[bi]], writes=[sqb_r[bi]])
        for j in range(5):
            b = cnt["sq"] % 2
            cnt["sq"] += 1
            P.op("pe", lambda h, b=b, j=j, bi=bi: h.matmul(ps_sq[b][:, 0:TQ], ones64[:, :], sqb[bi][:, j * TQ:(j + 1) * TQ],
                                                           start=True, stop=True),
                 reads=[ones_r, sqb_r[bi]], writes=[ps_sq_r[b]])
            P.op("dve", lambda h, b=b, j=j, bi=bi: h.tensor_scalar(rq[bi][:, j * TQ:(j + 1) * TQ], ps_sq[b][:, 0:TQ],
                                                                   1.0 / 64, 1e-6, ALU.mult, ALU.add),
                 reads=[ps_sq_r[b]], writes=[rq_r[bi]])
        P.op("act", lambda h, bi=bi: h.activation(rq[bi][:, :], rq[bi][:, :], AF.Ln), reads=[rq_r[bi]], writes=[rq_r[bi]])
        P.op("act", lambda h, bi=bi: h.activation(rq[bi][:, :], rq[bi][:, :], AF.Exp, scale=-0.5),
             reads=[rq_r[bi]], writes=[rq_r[bi]])
        P.op("dve", lambda h, bi=bi: h.scalar_tensor_tensor(qhat[bi][:, :], qst[bi][:, :], qw_t[:, 0:1], rq[bi][:, 0:4 * TQ],
                                                            ALU.mult, ALU.mult),
             reads=[qst_r[bi], cr, rq_r[bi]], writes=[qhat_r[bi]])
        P.op("dve", lambda h, bi=bi, t0=t0: h.scalar_tensor_tensor(khat[:, t0:t0 + TQ], kst[bi][:, :], kw_t[:, 0:1],
                                                                    rq[bi][:, 4 * TQ:5 * TQ], ALU.mult, ALU.mult),
             reads=[kst_r[bi], cr, rq_r[bi]], writes=[khat_r[tl]])
        P.op("pool", lambda h, bi=bi, tl=tl: h.tensor_copy(vaug3[:, tl * 3:tl * 3 + 3, 0:64],
                                                           vst[bi][:, :].rearrange("p (n c) -> p n c", c=64)),
             reads=[vst_r[bi], vones_r], writes=[vaug_r[tl]])
        qh3 = qhat[bi][:, :].rearrange("p (h t) -> p h t", h=4)
        for bl in range(3):
            n = tl * 3 + bl
            parts = []
            if n >= 1:
                ptl = (n - 1) // 3
                parts.append(("prev", 128, khat[:, (n - 1) * 128:n * 128], M_PREV1 if n == 1 else M_PREV,
                              vaug3[:, n - 1, :], [khat_r[ptl], vaug_r[ptl]]))
            parts.append(("cur", 128, khat[:, n * 128:(n + 1) * 128], M_CUR0 if n == 0 else M_CUR,
                          vaug3[:, n, :], [khat_r[tl], vaug_r[tl]]))
            if n >= 2:
                parts.append(("meta", 16, khat[:, 112:128], None, vmeta[:, :], [khat_r[0], vm_r]))
            pts = []
            for (kind, kp, lhsT, mi, vr, deps) in parts:
                b = cnt["sc"] % 4
                cnt["sc"] += 1
                P.op("pe", lambda h, b=b, kp=kp, lhsT=lhsT, bl=bl, qh3=qh3: h.matmul(
                    ps_sc[b][0:kp, :], lhsT, qh3[:, :, bl * 128:(bl + 1) * 128], start=True, stop=True),
                    reads=[deps[0], qhat_r[bi]], writes=[ps_sc_r[b]])
                pi = cnt["pt"] % len(pt)
                cnt["pt"] += 1
                P.op("act", lambda h, b=b, kp=kp, pi=pi: h.activation(pt[pi][0:kp, :], ps_sc[b][0:kp, :], AF.Exp),
                     reads=[ps_sc_r[b]], writes=[pt_r[pi]])
                if mi is not None:
                    P.op("dve", lambda h, pi=pi, mi=mi: h.tensor_tensor(pt[pi][:, :], pt[pi][:, :], mk[mi][:, :], ALU.mult),
                         reads=[pt_r[pi], mk_r[mi]], writes=[pt_r[pi]])
                pts.append((pi, kp, vr, deps[1]))
            ob = cnt["o"] % 2
            cnt["o"] += 1
            po3 = ps_o[ob][:, :].rearrange("p (h c) -> p h c", c=65)
            for hh in range(4):
                for idx, (pi, kp, vr, vdep) in enumerate(pts):
                    P.op("pe", lambda h, pi=pi, kp=kp, vr=vr, hh=hh, idx=idx, po3=po3, npt=len(pts): h.matmul(
                        po3[:, hh, :], pt[pi][0:kp, hh * 128:(hh + 1) * 128], vr if kp == 128 else vr[0:kp, :],
                        start=(idx == 0), stop=(idx == npt - 1)),
                        reads=[pt_r[pi], vdep], writes=[ps_o_r[ob]])
            dn, dnr = den[ob], den_r[ob]
            P.op("dve", lambda h, dn=dn, po3=po3: h.tensor_tensor(dn[:, 0:4], po3[:, :, 64], esink[:, :], ALU.add),
                 reads=[ps_o_r[ob], es_r], writes=[dnr])
            P.op("dve", lambda h, dn=dn: h.reciprocal(dn[:, 4:8], dn[:, 0:4]), reads=[dnr], writes=[dnr])
            for hh in range(4):
                if hh % 2 == 0:
                    P.op("dve", lambda h, dn=dn, po3=po3, hh=hh, bl=bl, bi=bi: h.tensor_scalar(
                        osb[bi][:, bl * 256 + hh * 64: bl * 256 + hh * 64 + 64], po3[:, hh, 0:64], dn[:, 4 + hh:5 + hh], None,
                        ALU.mult), reads=[ps_o_r[ob], dnr], writes=[osb_r[bi]])
                else:
                    P.op("act", lambda h, dn=dn, po3=po3, hh=hh, bl=bl, bi=bi: h.activation(
                        osb[bi][:, bl * 256 + hh * 64: bl * 256 + hh * 64 + 64], po3[:, hh, 0:64], AF.Copy,
                        scale=dn[:, 4 + hh:5 + hh]), reads=[ps_o_r[ob], dnr], writes=[osb_r[bi]])
        P.dma("sp", out[t0:t0 + TQ, :].rearrange("(n p) c -> p n c", p=128),
              osb[bi][:, :].rearrange("p (n c) -> p n c", c=256), reads=[osb_r[bi]])


def attn_masks():
    k = np.arange(128)[:, None]
    q = np.arange(128)[None, :]
    cur = (k <= q)
    prev = (k > q)
    cur0 = cur & (k >= 112)
    prev1 = np.broadcast_to(k >= 112, (128, 128))
    return np.stack([np.tile(m.astype(np.float32), (1, 4)) for m in (cur, prev, cur0, prev1)])

LC = 64
WSC = -0.6065306597126334


def rwkv_consts():
    L = LC
    s = np.arange(L)[:, None]
    t = np.arange(L)[None, :]
    tri_incl = (s <= t).astype(np.float32)
    tri_strict = (s < t).astype(np.float32)
    upper = (s > t).astype(np.float32)
    c = {}
    c["TT1"] = np.concatenate([tri_strict, tri_incl], 1) * WSC
    c["TT2"] = np.concatenate([tri_incl, tri_incl], 1) * WSC
    c["TT3"] = np.concatenate([upper, upper], 1) * WSC
    c["negones"] = np.full((64, 64), WSC, np.float32)
    m = np.concatenate([tri_strict, tri_incl], 1)
    c["mask128"] = np.concatenate([m, m], 0)
    c["maskN"] = np.ascontiguousarray(tri_strict.T)
    c["ident"] = np.eye(128, dtype=np.float32)
    return {k: np.ascontiguousarray(v, dtype=np.float32) for k, v in c.items()}


class Rot:
    def __init__(self, items):
        self.items = items
        self.i = 0

    def next(self):
        x = self.items[self.i % len(self.items)]
        self.i += 1
        return x


def rwkv_stage(P, *, PN, rkvp, xwT, xaT, xgT, mu_rkv, mu_w, mu_a, mu_g, rows, w_up, a_up, g_up, consts, out,
               nchunks=None):
    NCH = PN // LC if nchunks is None else nchunks
    sb = lambda shape, name, dt=F32: P.sbuf(shape, dt, "r_" + name)

    cr = Res()
    ct = {}
    for nm, shp in (("TT1", [64, 128]), ("TT2", [64, 128]), ("TT3", [64, 128]), ("negones", [64, 64]),
                    ("mask128", [128, 128]), ("maskN", [64, 64]), ("ident", [128, 128])):
        ct[nm] = sb(shp, nm)
        P.dma("sp", ct[nm][:], consts[nm], writes=[cr])
    mu_b = sb([128, 768], "mu_b")
    P.dma("sp", mu_b[:], mu_rkv[0:1, :].partition_broadcast(128), writes=[cr])
    rowb = sb([128, 7 * 256], "rowb")
    for i in range(7):
        P.dma("sp", rowb[:, i * 256:(i + 1) * 256], rows[i:i + 1, :].partition_broadcast(128), writes=[cr])
    W0, A0, KK, KA, RK, LNW, LNB = [rowb[:, i * 256:(i + 1) * 256] for i in range(7)]
    muw = sb([64, 1], "muw"); mua = sb([64, 1], "mua"); mug0 = sb([128, 1], "mug0"); mug1 = sb([32, 1], "mug1")
    P.dma("sp", muw[:], mu_w, writes=[cr]); P.dma("sp", mua[:], mu_a, writes=[cr])
    P.dma("sp", mug0[:], mu_g[0:128, :], writes=[cr]); P.dma("sp", mug1[:], mu_g[128:160, :], writes=[cr])
    wup = sb([64, 256], "wup"); aup = sb([64, 256], "aup"); gup0 = sb([128, 256], "gup0"); gup1 = sb([32, 256], "gup1")
    P.dma("sp", wup[:], w_up, writes=[cr]); P.dma("sp", aup[:], a_up, writes=[cr])
    P.dma("sp", gup0[:], g_up[0:128, :], writes=[cr]); P.dma("sp", gup1[:], g_up[128:160, :], writes=[cr])

    NB = 2
    def bufs(shape, name, n=NB):
        return Rot([(sb(shape, "%s%d" % (name, i)), Res()) for i in range(n)])
    cur_b = bufs([128, 768], "cur"); prv_b = bufs([128, 768], "prv")
    xw_b = bufs([64, 65], "xw"); xa_b = bufs([64, 65], "xa"); xg0_b = bufs([128, 65], "xg0"); xg1_b = bufs([32, 65], "xg1")
    tw_b = bufs([64, 128], "tw"); ta_b = bufs([64, 128], "ta"); tg0_b = bufs([128, 64], "tg0"); tg1_b = bufs([32, 64], "tg1")
    sig_b = bufs([128, 256], "sig"); a_b = bufs([128, 256], "a"); g_b = bufs([64, 256], "g")
    kk_b = bufs([128, 256], "kk"); sq_b = bufs([128, 256], "sq"); ss_b = bufs([128, 8], "ss")
    t1_b = bufs([128, 256], "t1"); bkr_b = bufs([128, 256], "bkr")
    e1_b = bufs([128, 256], "e1"); e2_b = bufs([128, 256], "e2"); e3_b = bufs([128, 256], "e3")
    glb_b = bufs([64, 256], "glb")
    ar_b = bufs([128, 256], "ar"); bk_b = bufs([128, 256], "bk"); bkb_b = bufs([128, 256], "bkb")
    uv_b = [bufs([128, 64], "uv%d" % u) for u in range(4)]
    art_b = [bufs([64, 128], "art%d" % u) for u in range(4)]
    bkt_b = [bufs([64, 128], "bkt%d" % u) for u in range(4)]
    mts_b = [bufs([128, 128], "mts%d" % u) for u in range(4)]
    p_b = [[bufs([64, 64], "p%d_%d" % (pp, u), 3) for u in range(4)] for pp in range(2)]
    q_b = [[bufs([64, 64], "q%d_%d" % (pp, u), 3) for u in range(4)] for pp in range(2)]
    z_b = [[bufs([64, 128], "z%d_%d" % (pp, u), 3) for u in range(4)] for pp in range(2)]
    ap_b = [bufs([64, 64], "ap%d" % u) for u in range(4)]
    phit_b = [bufs([64, 64], "phit%d" % u) for u in range(4)]
    dgl_b = [bufs([64, 64], "dgl%d" % u) for u in range(4)]
    psi_b = [bufs([64, 64], "psi%d" % u) for u in range(4)]
    rpt_b = [bufs([64, 64], "rpt%d" % u) for u in range(4)]
    h_b = [bufs([64, 64], "h%d" % u, 2) for u in range(4)]
    ysq_b = bufs([64, 256], "ysq"); st_b = bufs([64, 16], "st"); yn_b = bufs([64, 256], "yn")
    rk_b = bufs([64, 256], "rk"); ob_b = bufs([64, 256], "ob")
    pbanks = [P.psum([128, 512], F32, "r_ps%d" % i) for i in range(8)]
    half = Rot([(pbanks[b][:, 0:256], Res(excl=True)) for b in range(2)])
    pso_reg = [(pbanks[2][:, 0:256], Res(excl=True)), (pbanks[3][:, 0:256], Res(excl=True))]
    quar = Rot([(pbanks[b][:, 0:128], Res(excl=True)) for b in range(4, 8)])

    H = []
    for u in range(4):
        ht, hr = h_b[u].next()
        P.op("pool", lambda h, ht=ht: h.memset(ht[:], 0.0), writes=[hr])
        H.append((ht, hr))

    def mm(ps, psr, lhsT, rhs, reads, start=True, stop=True):
        P.op("pe", lambda h: h.matmul(ps, lhsT, rhs, start=start, stop=stop), reads=reads, writes=[psr])

    def acopy(dst, dstr, src, srcr):
        P.op("act", lambda h: h.copy(dst, src), reads=[srcr], writes=[dstr])

    prep_done = set()
    chain_done = [0]

    def chunk_gen(c, par):
        t0 = c * LC
        cur, curr = cur_b.next(); prv, prvr = prv_b.next()
        for hh in range(2):
            P.dma("sp", cur[hh * 64:(hh + 1) * 64, :], rkvp[1 + t0:1 + t0 + 64, :], writes=[curr])
            P.dma("pool", prv[hh * 64:(hh + 1) * 64, :], rkvp[t0:t0 + 64, :], writes=[prvr])
        xw, xwr = xw_b.next(); xa, xar = xa_b.next(); xg0, xg0r = xg0_b.next(); xg1, xg1r = xg1_b.next()
        P.dma("sp", xw[:], xwT[:, t0:t0 + 65], writes=[xwr])
        P.dma("sp", xa[:], xaT[:, t0:t0 + 65], writes=[xar])
        P.dma("pool", xg0[:], xgT[0:128, t0:t0 + 65], writes=[xg0r])
        P.dma("pool", xg1[:], xgT[128:160, t0:t0 + 65], writes=[xg1r])
        P.op("dve", lambda h: h.tensor_tensor(prv[:], prv[:], cur[:], ALU.subtract), reads=[prvr, curr], writes=[prvr])
        P.op("dve", lambda h: h.tensor_tensor(prv[:], prv[:], mu_b[:], ALU.mult), reads=[prvr, cr], writes=[prvr])
        P.op("dve", lambda h: h.tensor_tensor(prv[:], prv[:], cur[:], ALU.add), reads=[prvr, curr], writes=[prvr])
        zs, zsr = prv, prvr
        yield
        Rr, Kr, Vr = zs[:, 0:256], zs[:, 256:512], zs[:, 512:768]
        tw, twr = tw_b.next(); ta, tar = ta_b.next(); tg0, tg0r = tg0_b.next(); tg1, tg1r = tg1_b.next()
        for (x, xr, mu, np_, dst, dstr, fn) in ((xw, xwr, muw, 64, tw, twr, AF.Tanh), (xa, xar, mua, 64, ta, tar, AF.Copy),
                                                (xg0, xg0r, mug0, 128, tg0, tg0r, AF.Sigmoid),
                                                (xg1, xg1r, mug1, 32, tg1, tg1r, AF.Sigmoid)):
            tmp, tmpr = (sq_b.next())
            P.op("dve", lambda h, x=x, np_=np_, tmp=tmp: h.tensor_tensor(tmp[0:np_, 0:64], x[0:np_, 0:64], x[0:np_, 1:65],
                                                                         ALU.subtract), reads=[xr], writes=[tmpr])
            P.op("dve", lambda h, x=x, np_=np_, tmp=tmp, mu=mu: h.scalar_tensor_tensor(
                tmp[0:np_, 0:64], tmp[0:np_, 0:64], mu[0:np_, 0:1], x[0:np_, 1:65], ALU.mult, ALU.add),
                reads=[xr, tmpr, cr], writes=[tmpr])
            if fn == AF.Tanh:
                P.op("act", lambda h, dst=dst, np_=np_, tmp=tmp: h.activation(dst[0:np_, 0:64], tmp[0:np_, 0:64], AF.Sigmoid,
                                                                             scale=2.0), reads=[tmpr], writes=[dstr])
                P.op("dve", lambda h, dst=dst, np_=np_: h.tensor_scalar(dst[0:np_, 0:64], dst[0:np_, 0:64], 2.0, -1.0,
                                                                        ALU.mult, ALU.add), reads=[dstr], writes=[dstr])
            else:
                P.op("act", lambda h, dst=dst, np_=np_, tmp=tmp, fn=fn: h.activation(dst[0:np_, 0:64], tmp[0:np_, 0:64], fn),
                     reads=[tmpr], writes=[dstr])
            if dst is tw or dst is ta:
                P.op("act", lambda h, dst=dst: h.copy(dst[:, 64:128], dst[:, 0:64]), reads=[dstr], writes=[dstr])
            yield
        sig, sigr = sig_b.next(); a, ar_ = a_b.next(); g, gr = g_b.next()
        ps, psr = half.next()
        mm(ps, psr, tw[:, :], wup[:, :], [twr, cr])
        P.op("dve", lambda h, ps=ps: h.tensor_tensor(sig[:], ps, W0, ALU.add), reads=[psr, cr], writes=[sigr])
        P.op("act", lambda h: h.activation(sig[:], sig[:], AF.Sigmoid), reads=[sigr], writes=[sigr])
        yield
        ps, psr = half.next()
        mm(ps, psr, ta[:, :], aup[:, :], [tar, cr])
        P.op("dve", lambda h, ps=ps: h.tensor_tensor(a[:], ps, A0, ALU.add), reads=[psr, cr], writes=[ar_])
        P.op("act", lambda h: h.activation(a[:], a[:], AF.Sigmoid), reads=[ar_], writes=[ar_])
        yield
        ps, psr = half.next()
        mm(ps[0:64, :], psr, tg0[:, :], gup0[:, :], [tg0r, cr], True, False)
        mm(ps[0:64, :], psr, tg1[:, :], gup1[:, :], [tg1r, cr], False, True)
        acopy(g[:], gr, ps[0:64, :], psr)
        yield
        kk, kkr = kk_b.next(); sq, sqr = sq_b.next(); ss, ssr = ss_b.next()
        P.op("dve", lambda h: h.tensor_tensor(kk[:], Kr, KK, ALU.mult), reads=[zsr, cr], writes=[kkr])
        P.op("dve", lambda h: h.tensor_tensor(sq[:], kk[:], kk[:], ALU.mult), reads=[kkr], writes=[sqr])
        P.op("dve", lambda h: h.tensor_reduce(ss[:, 0:4], sq[:, :].rearrange("p (u d) -> p u d", u=4), AX.X, ALU.add),
             reads=[sqr], writes=[ssr])
        P.op("dve", lambda h: h.tensor_scalar(ss[:, 0:4], ss[:, 0:4], 1e-24, None, ALU.add), reads=[ssr], writes=[ssr])
        P.op("act", lambda h: h.activation(ss[:, 0:4], ss[:, 0:4], AF.Ln), reads=[ssr], writes=[ssr])
        P.op("act", lambda h: h.activation(ss[:, 4:8], ss[:, 0:4], AF.Exp, scale=-0.5), reads=[ssr], writes=[ssr])
        for u in range(4):
            P.op("dve", lambda h, u=u: h.tensor_scalar(kk[:, u * 64:(u + 1) * 64], kk[:, u * 64:(u + 1) * 64],
                                                       ss[:, 4 + u:5 + u], None, ALU.mult), reads=[kkr, ssr], writes=[kkr])
        yield
        t1, t1r = t1_b.next(); bkr, bkrr = bkr_b.next()
        P.op("dve", lambda h: h.scalar_tensor_tensor(t1[:], a[:], -1.0, KA, ALU.add, ALU.mult), reads=[ar_, cr], writes=[t1r])
        P.op("dve", lambda h: h.scalar_tensor_tensor(t1[:], t1[:], 1.0, Kr, ALU.add, ALU.mult), reads=[t1r, zsr], writes=[t1r])
        P.op("dve", lambda h: h.tensor_tensor(bkr[0:64, :], kk[0:64, :], a[0:64, :], ALU.mult), reads=[kkr, ar_], writes=[bkrr])
        P.op("act", lambda h: h.copy(bkr[64:128, :], t1[64:128, :]), reads=[t1r], writes=[bkrr])
        yield
        e1, e1r = e1_b.next(); e2, e2r = e2_b.next(); e3, e3r = e3_b.next(); glb, glbr = glb_b.next()
        for (TT, e, er, sc) in ((ct["TT1"], e1, e1r, 1.0), (ct["TT2"], e2, e2r, -1.0), (ct["TT3"], e3, e3r, 1.0)):
            ps, psr = half.next()
            mm(ps, psr, TT[:, :], sig[0:64, :], [cr, sigr])
            P.op("act", lambda h, e=e, ps=ps, sc=sc: h.activation(e[:], ps, AF.Exp, scale=sc), reads=[psr], writes=[er])
            yield
        ps, psr = half.next()
        for u in range(4):
            mm(ps[0:64, u * 64:(u + 1) * 64], psr, sig[0:64, u * 64:(u + 1) * 64], ct["negones"][:, :], [sigr, cr])
        P.op("act", lambda h, ps=ps: h.activation(glb[:], ps[0:64, :], AF.Exp), reads=[psr], writes=[glbr])
        yield
        ar, arr = ar_b.next(); bk, bkr_ = bk_b.next(); bkb, bkbr = bkb_b.next()
        P.op("dve", lambda h: h.scalar_tensor_tensor(ar[0:64, :], kk[0:64, :], -1.0, e1[0:64, :], ALU.mult, ALU.mult),
             reads=[kkr, e1r], writes=[arr])
        P.op("dve", lambda h: h.tensor_tensor(ar[64:128, :], Rr[64:128, :], e1[64:128, :], ALU.mult),
             reads=[zsr, e1r], writes=[arr])
        P.op("dve", lambda h: h.tensor_tensor(bk[:], bkr[:], e2[:], ALU.mult), reads=[bkrr, e2r], writes=[bkr_])
        P.op("dve", lambda h: h.tensor_tensor(bkb[:], bkr[:], e3[:], ALU.mult), reads=[bkrr, e3r], writes=[bkbr])
        UV = []
        for u in range(4):
            uv, uvr = uv_b[u].next()
            P.op("act", lambda h, uv=uv, u=u: h.copy(uv[64:128, :], Vr[64:128, u * 64:(u + 1) * 64]),
                 reads=[zsr], writes=[uvr])
            UV.append((uv, uvr))
        prep_done.add(c)
        yield
        U = [dict() for _ in range(4)]
        for u in range(4):
            cs = slice(u * 64, (u + 1) * 64)
            d = U[u]
            d["art"], d["artr"] = art_b[u].next(); d["bkt"], d["bktr"] = bkt_b[u].next()
            ps, psr = quar.next()
            P.op("pe", lambda h, ps=ps, cs=cs: h.transpose(ps[0:64, :], ar[:, cs], ct["ident"][:, :]), reads=[arr, cr], writes=[psr])
            acopy(d["art"][:], d["artr"], ps[0:64, :], psr)
            ps, psr = quar.next()
            P.op("pe", lambda h, ps=ps, cs=cs: h.transpose(ps[0:64, :], bk[:, cs], ct["ident"][:, :]), reads=[bkr_, cr], writes=[psr])
            P.op("dve", lambda h, ps=ps, d=d: h.tensor_copy(d["bkt"][:], ps[0:64, :]), reads=[psr], writes=[d["bktr"]])
            yield
        for u in range(4):
            d = U[u]
            d["mts"], d["mtsr"] = mts_b[u].next()
            ps, psr = quar.next()
            mm(ps, psr, d["bkt"][:, :], d["art"][:, :], [d["bktr"], d["artr"]])
            P.op("dve", lambda h, ps=ps, d=d: h.tensor_tensor(d["mts"][:], ps, ct["mask128"][:, :], ALU.mult),
                 reads=[psr, cr], writes=[d["mtsr"]])
            d["p"], d["pr"] = p_b[par][u].next()
            ps, psr = quar.next()
            mm(ps[0:64, 0:64], psr, d["art"][:, 0:64], d["bkt"][:, 0:64], [d["bktr"], d["artr"]])
            P.op("dve", lambda h, ps=ps, d=d: h.tensor_tensor(d["p"][:], ps[0:64, 0:64], ct["maskN"][:, :], ALU.mult),
                 reads=[psr, cr], writes=[d["pr"]])
            yield
        for u in range(4):
            cs = slice(u * 64, (u + 1) * 64)
            d = U[u]
            uv, uvr = UV[u]
            d["z"], d["zr"] = z_b[par][u].next()
            ps, psr = quar.next()
            mm(ps[0:64, 0:64], psr, d["mts"][64:128, 0:64], uv[64:128, :], [d["mtsr"], uvr])
            acopy(d["z"][:, 64:128], d["zr"], ps[0:64, 0:64], psr)
            P.op("act", lambda h, d=d, cs=cs: h.copy(d["z"][:, 0:64], ar[0:64, cs]), reads=[arr], writes=[d["zr"]])
            d["q"], d["qr"] = d["mts"][0:64, 0:64], d["mtsr"]
            yield
        for lvl in range(6):
            for u in range(4):
                d = U[u]
                uv, uvr = UV[u]
                ps, psr = quar.next()
                mm(ps[0:64, :], psr, d["q"], d["z"][:, :], [d["qr"], d["zr"]])
                if lvl < 5:
                    zn, znr = z_b[par][u].next()
                    P.op("dve", lambda h, ps=ps, d=d, zn=zn: h.tensor_tensor(zn[:], ps[0:64, :], d["z"][:, :], ALU.add),
                         reads=[psr, d["zr"]], writes=[znr])
                    pn, pnr = p_b[par][u].next(); qn, qnr = q_b[par][u].next()
                    ps1, ps1r = quar.next()
                    mm(ps1[0:64, 0:64], ps1r, d["q"], d["p"][:, :], [d["qr"], d["pr"]])
                    acopy(pn[:], pnr, ps1[0:64, 0:64], ps1r)
                    ps2, ps2r = quar.next()
                    mm(ps2[0:64, 0:64], ps2r, d["p"][:, :], d["q"], [d["qr"], d["pr"]])
                    acopy(qn[:], qnr, ps2[0:64, 0:64], ps2r)
                    d["z"], d["zr"] = zn, znr
                    d["p"], d["pr"] = pn, pnr
                    d["q"], d["qr"] = qn[:, :], qnr
                else:
                    d["ap"], d["apr"] = ap_b[u].next()
                    P.op("dve", lambda h, ps=ps, d=d: h.tensor_tensor(d["ap"][:], ps[0:64, 0:64], d["z"][:, 0:64], ALU.add),
                         reads=[psr, d["zr"]], writes=[d["apr"]])
                    P.op("dve", lambda h, ps=ps, d=d, uv=uv: h.tensor_tensor(uv[0:64, :], ps[0:64, 64:128], d["z"][:, 64:128],
                                                                              ALU.add), reads=[psr, d["zr"]], writes=[uvr])
                yield
        pso, psor = pso_reg[par]
        for u in range(4):
            cs = slice(u * 64, (u + 1) * 64)
            d = U[u]
            uv, uvr = UV[u]
            phit, phitr = phit_b[u].next(); dgl, dglr = dgl_b[u].next()
            P.op("dve", lambda h, dgl=dgl, cs=cs: h.tensor_tensor(dgl[:], ct["ident"][0:64, 0:64], glb[:, cs], ALU.mult),
                 reads=[cr, glbr], writes=[dglr])
            ps, psr = quar.next()
            mm(ps[0:64, 0:64], psr, d["ap"][:, :], bkb[0:64, cs], [d["apr"], bkbr])
            P.op("dve", lambda h, ps=ps, phit=phit, dgl=dgl: h.tensor_tensor(phit[:], ps[0:64, 0:64], dgl[:], ALU.add),
                 reads=[psr, dglr], writes=[phitr])
            psi, psir = psi_b[u].next()
            ps, psr = quar.next()
            mm(ps[0:64, 0:64], psr, bkb[:, cs], uv[:, :], [bkbr, uvr])
            acopy(psi[:], psir, ps[0:64, 0:64], psr)
            rpt, rptr = rpt_b[u].next()
            ps, psr = quar.next()
            mm(ps[0:64, 0:64], psr, d["ap"][:, :], d["mts"][0:64, 64:128], [d["apr"], d["mtsr"]])
            P.op("dve", lambda h, ps=ps, rpt=rpt, d=d: h.tensor_tensor(rpt[:], ps[0:64, 0:64], d["art"][:, 64:128], ALU.add),
                 reads=[psr, d["artr"]], writes=[rptr])
            d["phit"], d["phitr"], d["psi"], d["psir"], d["rpt"], d["rptr"] = phit, phitr, psi, psir, rpt, rptr
            yield
        while chain_done[0] < c:
            yield
        for u in range(4):
            cs = slice(u * 64, (u + 1) * 64)
            d = U[u]
            uv, uvr = UV[u]
            phit, phitr, psi, psir, rpt, rptr = d["phit"], d["phitr"], d["psi"], d["psir"], d["rpt"], d["rptr"]
            ht, hr = H[u]
            mm(pso[0:64, cs], psor, rpt[:, :], ht[:, :], [rptr, hr], True, False)
            mm(pso[0:64, cs], psor, d["mts"][:, 64:128], uv[:, :], [d["mtsr"], uvr], False, True)
            ps, psr = quar.next()
            mm(ps[0:64, 0:64], psr, phit[:, :], ht[:, :], [phitr, hr])
            hn, hnr = h_b[u].next()
            P.op("dve", lambda h, ps=ps, hn=hn, psi=psi: h.tensor_tensor(hn[:], ps[0:64, 0:64], psi[:], ALU.add),
                 reads=[psr, psir], writes=[hnr])
            H[u] = (hn, hnr)
        chain_done[0] = c + 1
        yield
        ysq, ysqr = ysq_b.next(); st, str_ = st_b.next(); yn, ynr = yn_b.next(); rk, rkr = rk_b.next(); ob, obr = ob_b.next()
        y3 = pso[0:64, :].rearrange("p (u d) -> p u d", u=4)
        P.op("act", lambda h: h.activation(ysq[:], pso[0:64, :], AF.Square), reads=[psor], writes=[ysqr])
        P.op("dve", lambda h: h.tensor_reduce(st[:, 0:4], y3, AX.X, ALU.add), reads=[psor], writes=[str_])
        P.op("dve", lambda h: h.tensor_reduce(st[:, 4:8], ysq[:, :].rearrange("p (u d) -> p u d", u=4), AX.X, ALU.add),
             reads=[ysqr], writes=[str_])
        P.op("dve", lambda h: h.tensor_scalar(st[:, 0:8], st[:, 0:8], 1.0 / 64, None, ALU.mult), reads=[str_], writes=[str_])
        P.op("dve", lambda h: h.tensor_tensor(st[:, 8:12], st[:, 0:4], st[:, 0:4], ALU.mult), reads=[str_], writes=[str_])
        P.op("dve", lambda h: h.tensor_tensor(st[:, 8:12], st[:, 4:8], st[:, 8:12], ALU.subtract), reads=[str_], writes=[str_])
        P.op("dve", lambda h: h.tensor_scalar(st[:, 8:12], st[:, 8:12], 64e-5, None, ALU.add), reads=[str_], writes=[str_])
        P.op("act", lambda h: h.activation(st[:, 8:12], st[:, 8:12], AF.Ln), reads=[str_], writes=[str_])
        P.op("act", lambda h: h.activation(st[:, 8:12], st[:, 8:12], AF.Exp, scale=-0.5), reads=[str_], writes=[str_])
        yield
        for u in range(4):
            cs = slice(u * 64, (u + 1) * 64)
            P.op("dve", lambda h, u=u, cs=cs: h.tensor_scalar(yn[:, cs], pso[0:64, cs], st[:, u:u + 1], st[:, 8 + u:9 + u],
                                                             ALU.subtract, ALU.mult), reads=[psor, str_], writes=[ynr])
        P.op("dve", lambda h: h.tensor_tensor(yn[:], yn[:], LNW[0:64, :], ALU.mult), reads=[ynr, cr], writes=[ynr])
        P.op("dve", lambda h: h.tensor_tensor(yn[:], yn[:], LNB[0:64, :], ALU.add), reads=[ynr, cr], writes=[ynr])
        yield
        P.op("dve", lambda h: h.tensor_tensor(rk[:], Rr[0:64, :], t1[0:64, :], ALU.mult), reads=[zsr, t1r], writes=[rkr])
        P.op("dve", lambda h: h.tensor_tensor(rk[:], rk[:], RK[0:64, :], ALU.mult), reads=[rkr, cr], writes=[rkr])
        P.op("dve", lambda h: h.tensor_reduce(st[:, 12:16], rk[:, :].rearrange("p (u d) -> p u d", u=4), AX.X, ALU.add),
             reads=[rkr], writes=[str_])
        for u in range(4):
            cs = slice(u * 64, (u + 1) * 64)
            P.op("dve", lambda h, u=u, cs=cs: h.scalar_tensor_tensor(yn[:, cs], Vr[0:64, cs], st[:, 12 + u:13 + u], yn[:, cs],
                                                                    ALU.mult, ALU.add), reads=[zsr, str_, ynr], writes=[ynr])
        P.op("dve", lambda h: h.tensor_tensor(ob[:], yn[:], g[:], ALU.mult), reads=[ynr, gr], writes=[obr])
        P.dma("sp", out[t0:t0 + 64, :], ob[:], reads=[obr])

    active = []
    next_c = 0
    while active or next_c < NCH:
        if next_c < NCH and len(active) < 2 and (next_c == 0 or (next_c - 1) in prep_done):
            active.append([next_c, chunk_gen(next_c, next_c % 2)])
            next_c += 1
        for item in list(active):
            try:
                next(item[1])
            except StopIteration:
                active.remove(item)

SB3 = 384


def ssd_consts():
    k = np.arange(128)[:, None]
    t = np.arange(128)[None, :]
    c = {"tri": (k <= t).astype(np.float32), "ones": np.ones((128, 128), np.float32),
         "maskneg": np.where(k <= t, 0.0, -30000.0).astype(np.float32), "ident": np.eye(128, dtype=np.float32)}
    return c


def ssd_stage(P, *, PN, xbcT, convw, convb, z, dtT, dtb, alog, dskip, normw, consts, out):
    NSB = PN // SB3
    assert PN % SB3 == 0
    sb = lambda shape, name, dt=F32: P.sbuf(shape, dt, "s_" + name)
    cr = Res()
    ct = {}
    for nm in ("tri", "ones", "maskneg", "ident"):
        ct[nm] = sb([128, 128], nm)
        P.dma("sp", ct[nm][:], consts[nm], writes=[cr])
    cw = sb([128, 48], "cw"); cb = sb([128, 12], "cb")
    P.dma("sp", cw[:], convw, writes=[cr]); P.dma("sp", cb[:], convb, writes=[cr])
    dtb_t = sb([16, 1], "dtb"); acol = sb([16, 1], "acol")
    P.dma("sp", dtb_t[:], dtb, writes=[cr]); P.dma("sp", acol[:], alog, writes=[cr])
    P.op("act", lambda h: h.activation(acol[:], acol[:], AF.Exp), reads=[cr], writes=[cr])
    P.op("dve", lambda h: h.tensor_scalar(acol[:], acol[:], -1.0, None, ALU.mult), reads=[cr], writes=[cr])
    dsk = sb([128, 1024], "dsk"); nw = sb([128, 1024], "nw")
    P.dma("sp", dsk[:], dskip[0:1, :].partition_broadcast(128), writes=[cr])
    P.dma("sp", nw[:], normw[0:1, :].partition_broadcast(128), writes=[cr])

    def bufs(shape, name, n=2, dt=F32):
        return Rot([(sb(shape, "%s%d" % (name, i), dt), Res()) for i in range(n)])
    xin_b = bufs([128, 12 * (SB3 + 3)], "xin")
    cv_b = bufs([128, 12 * SB3], "cv")
    ctmp_b = bufs([128, SB3], "ctmp", 3)
    dtin_b = bufs([16, SB3], "dtin"); dtf_b = bufs([16, 2 * SB3], "dtf")
    xtok_b = Rot([(sb([128, 1024], "xtok%d" % i), [Res() for _ in range(8)]) for i in range(2)]); btok_b = bufs([128, 256], "btok", 2, BF16)
    bct_b = bufs([128, 4 * 128], "bct", 2, BF16)
    dta_b = bufs([128, 64], "dta")
    te_b = bufs([128, 48], "te")
    tmpi_b = bufs([128, 512], "tmpi")
    NSL = 3
    abc_s = [(sb([128, 128], "abc%d" % i), Res()) for i in range(NSL)]
    seg_s = [(sb([128, 128], "seg%d" % i), Res()) for i in range(NSL)]
    wj_s = [(sb([128, 128], "wj%d" % i, BF16), Res()) for i in range(NSL)]
    ebc_s = [(sb([128, 128], "ebc%d" % i), Res()) for i in range(NSL)]
    ctj_s = [(sb([128, 128], "ctj%d" % i, BF16), Res()) for i in range(NSL)]
    cbt_b = bufs([128, 256], "cbt")
    xdt_b = Rot([(sb([128, 1024], "xdt%d" % i, BF16), [Res() for _ in range(16)]) for i in range(2)])
    xs_b = Rot([(sb([128, 1024], "xs%d" % i, BF16), [Res() for _ in range(16)]) for i in range(2)])
    z_b = bufs([128, 1024], "z"); yy_b = bufs([128, 1024], "yy"); sq_b = bufs([128, 1024], "sq", 1)
    st_b = bufs([128, 8], "st")
    state = sb([128, 1024], "state"); state_r = [Res() for _ in range(16)]
    stbf = sb([128, 1024], "statebf", BF16); stbf_r = [Res(), Res()]
    P.op("pool", lambda h: h.memset(state[:], 0.0), writes=state_r)
    P.op("pool", lambda h: h.memset(stbf[:], 0.0), writes=stbf_r)
    pb = [P.psum([128, 512], F32, "s_ps%d" % i) for i in range(8)]
    ybank = [(pb[0], Res(excl=True)), (pb[1], Res(excl=True))]
    cbanks = [(pb[2 + i], Res(excl=True)) for i in range(3)]
    mbank = Rot([(pb[5], Res(excl=True))])
    sbank = Rot([(pb[6], Res(excl=True)), (pb[7], Res(excl=True))])

    def mm(ps, psr, lhsT, rhs, reads, start=True, stop=True):
        P.op("pe", lambda h: h.matmul(ps, lhsT, rhs, start=start, stop=stop), reads=reads, writes=[psr])

    for sbi in range(NSB):
        t0 = sbi * SB3
        xin, xinr = xin_b.next(); cv, cvr = cv_b.next()
        xin3 = xin[:, :].rearrange("p (k t) -> p k t", k=12)
        cv3 = cv[:, :].rearrange("p (k t) -> p k t", k=12)
        for hf in range(2):
            P.dma("sp" if hf == 0 else "pool", xin3[:, hf * 6:(hf + 1) * 6, :],
                  xbcT[hf * 768:(hf + 1) * 768, t0:t0 + SB3 + 3].rearrange("(k p) t -> p k t", p=128), writes=[xinr])
        for kc in range(12):
            tm, tmr = ctmp_b.next()
            P.op("dve", lambda h: h.tensor_scalar(tm[:], xin3[:, kc, 3:SB3 + 3], cw[:, kc * 4 + 3:kc * 4 + 4], cb[:, kc:kc + 1],
                                                  ALU.mult, ALU.add), reads=[xinr, cr], writes=[tmr])
            for j in (2, 1, 0):
                P.op("dve", lambda h: h.scalar_tensor_tensor(tm[:], xin3[:, kc, j:SB3 + j], cw[:, kc * 4 + j:kc * 4 + j + 1], tm[:],
                                                             ALU.mult, ALU.add), reads=[xinr, cr, tmr], writes=[tmr])
            P.op("act", lambda h: h.activation(cv3[:, kc, :], tm[:], AF.Silu), reads=[tmr], writes=[cvr])
        if sbi == 0:
            P.op("pool", lambda h: h.memset(cv3[:, :, 0:112], 0.0), writes=[cvr])
        dtin, dtinr = dtin_b.next(); dtf, dtfr = dtf_b.next()
        P.dma("sp", dtin[:], dtT[:, t0:t0 + SB3], writes=[dtinr])
        P.op("act", lambda h: h.activation(dtf[:, 0:SB3], dtin[:], AF.Exp, bias=dtb_t[:, 0:1]), reads=[dtinr, cr], writes=[dtfr])
        P.op("act", lambda h: h.activation(dtf[:, 0:SB3], dtf[:, 0:SB3], AF.Ln, bias=1.0), reads=[dtfr], writes=[dtfr])
        if sbi == 0:
            P.op("pool", lambda h: h.memset(dtf[:, 0:112], 0.0), writes=[dtfr])
        P.op("dve", lambda h: h.tensor_scalar(dtf[:, SB3:2 * SB3], dtf[:, 0:SB3], acol[:, 0:1], None, ALU.mult),
             reads=[dtfr, cr], writes=[dtfr])
        for ci in range(3):
            c0 = ci * 128
            tc0 = t0 + c0
            xtok, xtokr = xtok_b.next(); btok, btokr = btok_b.next(); bct, bctr = bct_b.next(); dta, dtar = dta_b.next()
            for kc in range(8):
                ps, psr = mbank.next()
                P.op("pe", lambda h: h.transpose(ps[:, 0:128], cv3[:, kc, c0:c0 + 128], ct["ident"][:, :]), reads=[cvr, cr], writes=[psr])
                if kc % 2 == 0:
                    P.op("act", lambda h: h.copy(xtok[:, kc * 128:(kc + 1) * 128], ps[:, 0:128]), reads=[psr], writes=[xtokr[kc]])
                else:
                    P.op("dve", lambda h: h.tensor_copy(xtok[:, kc * 128:(kc + 1) * 128], ps[:, 0:128]), reads=[psr], writes=[xtokr[kc]])
            for g in range(2):
                ps, psr = mbank.next()
                P.op("pe", lambda h: h.transpose(ps[:, 0:128], cv3[:, 8 + g, c0:c0 + 128], ct["ident"][:, :]), reads=[cvr, cr], writes=[psr])
                P.op("act", lambda h: h.copy(btok[:, g * 128:(g + 1) * 128], ps[:, 0:128]), reads=[psr], writes=[btokr])
            P.op("act", lambda h: h.copy(bct[:, :].rearrange("p (k t) -> p k t", k=4), cv3[:, 8:12, c0:c0 + 128]),
                 reads=[cvr], writes=[bctr])
            ps, psr = mbank.next()
            for q in range(2):
                P.op("pe", lambda h: h.transpose(ps[:, q * 16:(q + 1) * 16], dtf[:, q * SB3 + c0:q * SB3 + c0 + 128], ct["ident"][0:16, 0:16]),
                     reads=[dtfr, cr], writes=[psr])
            P.op("dve", lambda h: h.tensor_copy(dta[:, 0:32], ps[:, 0:32]), reads=[psr], writes=[dtar])
            te, ter = te_b.next()
            ps, psr = mbank.next()
            mm(ps[:, 0:16], psr, ct["tri"][:, :], dta[:, 16:32], [cr, dtar])
            mm(ps[:, 16:32], psr, ct["ones"][:, :], dta[:, 16:32], [cr, dtar])
            P.op("dve", lambda h: h.tensor_copy(dta[:, 32:48], ps[:, 0:16]), reads=[psr], writes=[dtar])
            P.op("dve", lambda h: h.tensor_tensor(te[:, 0:16], ps[:, 16:32], dta[:, 32:48], ALU.subtract), reads=[psr, dtar], writes=[ter])
            P.op("act", lambda h: h.activation(te[:, 16:32], ps[:, 16:32], AF.Exp), reads=[psr], writes=[ter])
            P.op("act", lambda h: h.activation(te[:, 0:16], te[:, 0:16], AF.Exp), reads=[ter], writes=[ter])
            P.op("dve", lambda h: h.tensor_tensor(dta[:, 48:64], dta[:, 0:16], te[:, 0:16], ALU.mult), reads=[dtar, ter], writes=[dtar])
            P.op("act", lambda h: h.activation(te[:, 32:48], dta[:, 32:48], AF.Exp), reads=[dtar], writes=[ter])
            xdt, xdtr = xdt_b.next(); xs, xsr = xs_b.next()
            x3 = xtok[:, :].rearrange("p (j d) -> p j d", j=16)
            P.op("dve", lambda h: h.tensor_tensor(xdt[:, :].rearrange("p (j d) -> p j d", j=16), x3,
                                                  dta[:, 0:16].unsqueeze(2).to_broadcast([128, 16, 64]), ALU.mult),
                 reads=xtokr + [dtar], writes=xdtr)
            P.op("dve", lambda h: h.tensor_tensor(xs[:, :].rearrange("p (j d) -> p j d", j=16), x3,
                                                  dta[:, 48:64].unsqueeze(2).to_broadcast([128, 16, 64]), ALU.mult),
                 reads=xtokr + [dtar], writes=xsr)
            cbt, cbtr = cbt_b.next()
            for g in range(2):
                ps, psr = mbank.next()
                mm(ps[:, 0:128], psr, bct[:, g * 128:(g + 1) * 128], bct[:, (2 + g) * 128:(3 + g) * 128], [bctr])
                P.op("act", lambda h: h.copy(cbt[:, g * 128:(g + 1) * 128], ps[:, 0:128]), reads=[psr], writes=[cbtr])
            inter = []
            for g in range(2):
                ps, psr = sbank.next()
                mm(ps[:, :], psr, bct[:, (2 + g) * 128:(3 + g) * 128], stbf[:, g * 512:(g + 1) * 512], [bctr, stbf_r[g]])
                tmpi, tmpir = tmpi_b.next()
                P.op("dve", lambda h: h.tensor_tensor(tmpi[:, :].rearrange("p (j d) -> p j d", j=8),
                                                      ps[:, :].rearrange("p (j d) -> p j d", j=8),
                                                      te[:, 32 + g * 8:40 + g * 8].unsqueeze(2).to_broadcast([128, 8, 64]), ALU.mult),
                     reads=[psr, ter], writes=[tmpir])
                inter.append((tmpi, tmpir))
            def head_gen(j, slot):
                g = j // 8
                yb, ybr = ybank[g]
                seg, segr = seg_s[slot]; wj, wjr = wj_s[slot]
                ps, psr = cbanks[slot]
                mm(ps[:, 0:128], psr, dta[:, 16 + j:17 + j].to_broadcast([128, 128]), ct["tri"][:, :], [dtar, cr])
                yield
                P.op("dve", lambda h: h.scalar_tensor_tensor(seg[:], ps[:, 0:128], dta[:, 32 + j:33 + j], ct["maskneg"][:, :],
                                                             ALU.subtract, ALU.add), reads=[psr, dtar, cr], writes=[segr])
                yield
                P.op("act", lambda h: h.activation(seg[:], seg[:], AF.Exp), reads=[segr], writes=[segr])
                yield
                P.op("dve", lambda h: h.tensor_tensor(wj[:], seg[:], cbt[:, g * 128:(g + 1) * 128], ALU.mult), reads=[segr, cbtr], writes=[wjr])
                yield
                jj = j % 8
                mm(yb[:, jj * 64:(jj + 1) * 64], ybr, wj[:, :], xdt[:, j * 64:(j + 1) * 64], [wjr, xdtr[j]], True, True)

            pending = list(range(16))
            active = {}
            free_slots = list(range(NSL))
            while pending or active:
                if pending and free_slots:
                    slot = free_slots.pop(0)
                    active[slot] = head_gen(pending.pop(0), slot)
                for slot in sorted(active):
                    try:
                        next(active[slot])
                    except StopIteration:
                        del active[slot]
                        free_slots.append(slot)
            for g in range(2):
                ps, psr = sbank.next()
                mm(ps[:, :], psr, btok[:, g * 128:(g + 1) * 128], xs[:, g * 512:(g + 1) * 512], [btokr] + xsr[g * 8:(g + 1) * 8])
                for jj in range(8):
                    j = g * 8 + jj
                    P.op("dve", lambda h: h.scalar_tensor_tensor(state[:, j * 64:(j + 1) * 64], state[:, j * 64:(j + 1) * 64],
                                                                 te[:, 16 + j:17 + j], ps[:, jj * 64:(jj + 1) * 64], ALU.mult, ALU.add),
                         reads=[psr, ter, state_r[j]], writes=[state_r[j]])
                P.op("act", lambda h: h.copy(stbf[:, g * 512:(g + 1) * 512], state[:, g * 512:(g + 1) * 512]),
                     reads=state_r[g * 8:(g + 1) * 8], writes=[stbf_r[g]])
            zt, ztr = z_b.next(); yy, yyr = yy_b.next(); sq, sqr = sq_b.next(); st, str_ = st_b.next()
            P.dma("sp", zt[:], z[tc0:tc0 + 128, :], writes=[ztr])
            P.op("dve", lambda h: h.tensor_tensor(yy[:], xtok[:], dsk[:], ALU.mult), reads=xtokr + [cr], writes=[yyr])
            for g in range(2):
                yb, ybr = ybank[g]
                tmpi, tmpir = inter[g]
                P.op("dve", lambda h: h.tensor_tensor(yy[:, g * 512:(g + 1) * 512], yy[:, g * 512:(g + 1) * 512], tmpi[:, :], ALU.add),
                     reads=[tmpir, yyr], writes=[yyr])
                P.op("dve", lambda h: h.tensor_tensor(yy[:, g * 512:(g + 1) * 512], yy[:, g * 512:(g + 1) * 512], yb[:, :], ALU.add),
                     reads=[ybr, yyr], writes=[yyr])
            P.op("act", lambda h: h.activation(zt[:], zt[:], AF.Silu), reads=[ztr], writes=[ztr])
            P.op("dve", lambda h: h.tensor_tensor(yy[:], yy[:], zt[:], ALU.mult), reads=[yyr, ztr], writes=[yyr])
            P.op("act", lambda h: h.activation(sq[:], yy[:], AF.Square), reads=[yyr], writes=[sqr])
            P.op("dve", lambda h: h.tensor_reduce(st[:, 0:2], sq[:, :].rearrange("p (g c) -> p g c", g=2), AX.X, ALU.add),
                 reads=[sqr], writes=[str_])
            P.op("dve", lambda h: h.tensor_scalar(st[:, 0:2], st[:, 0:2], 1.0 / 512, 1e-5, ALU.mult, ALU.add), reads=[str_], writes=[str_])
            P.op("act", lambda h: h.activation(st[:, 0:2], st[:, 0:2], AF.Ln), reads=[str_], writes=[str_])
            P.op("act", lambda h: h.activation(st[:, 2:4], st[:, 0:2], AF.Exp, scale=-0.5), reads=[str_], writes=[str_])
            for g in range(2):
                P.op("dve", lambda h: h.scalar_tensor_tensor(yy[:, g * 512:(g + 1) * 512], yy[:, g * 512:(g + 1) * 512],
                                                             st[:, 2 + g:3 + g], nw[:, g * 512:(g + 1) * 512], ALU.mult, ALU.mult),
                     reads=[yyr, str_, cr], writes=[yyr])
            P.dma("sp", out[tc0:tc0 + 128, :], yy[:], reads=[yyr])

D_MODEL = 2048
FFN = 5632
_PROGS = {}


def _add_barrier(P):
    toks = []
    for q, sls in P.slots.items():
        for sl in sls:
            if sl.uses > 0:
                toks.append((sl.key, sl.sem, 16 * sl.uses))
    for q in ("sp", "pool", "act"):
        for t in toks:
            P._wait(P.E[q], t)


def _build_tp(N, Kmix, Cnext, first):
    key = ("tp", N, Kmix, Cnext, first)
    if key in _PROGS:
        return _PROGS[key]
    P = Prog()
    d = lambda n, s: P.dram(n, s, F32, "ExternalInput")
    L = LinCtx(P)
    resT = d("resT", [D_MODEL, N])
    mask = d("mask", [1, N])
    if not first:
        mixT = d("mixT", [Kmix, N])
        Wout = d("Wout", [16, 128, Kmix])
        gam_f = d("gam_f", [128, 16])
        Wup = d("Wup", [88, 128, D_MODEL])
        cw = d("cw", [128, 44 * 3])
        cb = d("cb", [128, 44])
        Wdown = d("Wdown", [16, 128, FFN])
        res1 = P.dram("res1", [D_MODEL, N], F32, "Internal")
        up = P.dram("up", [2 * FFN, N], F32, "Internal")
        res2 = P.dram("res2", [D_MODEL, N], F32, "ExternalOutput")
        linear_stage(L, mode="plain", K=Kmix, C=D_MODEL, N=N, W=Wout, dst=res1, src=mixT, res=resT)
        _add_barrier(P)
        linear_stage(L, mode="norm", K=D_MODEL, C=2 * FFN, N=N, W=Wup, dst=up, src=res1, gamma=gam_f, mask=mask)
        _add_barrier(P)
        linear_stage(L, mode="convglu", K=FFN, C=D_MODEL, N=N, W=Wdown, dst=res2, src=up[0:FFN, :], src2=up[FFN:2 * FFN, :],
                     convw=cw, convb=cb, res=res1)
        src_next = res2
    else:
        src_next = resT
    if Cnext:
        gam_n = d("gam_n", [128, 16])
        CBn = (Cnext + 127) // 128
        Win = d("Win", [CBn, 128, D_MODEL])
        znext = P.dram("znext", [Cnext, N], F32, "ExternalOutput")
        if not first:
            _add_barrier(P)
        linear_stage(L, mode="norm", K=D_MODEL, C=Cnext, N=N, W=Win, dst=znext, src=src_next, gamma=gam_n, mask=mask)
    nc = P.finalize()
    _PROGS[key] = nc
    return nc


def _build_attn(PN):
    key = ("attn", PN)
    if key in _PROGS:
        return _PROGS[key]
    P = Prog()
    d = lambda n, s: P.dram(n, s, F32, "ExternalInput")
    qT = d("qT", [64, 4, PN]); kT = d("kT", [64, PN]); vv = d("v", [PN, 64])
    qwd = d("qw", [64, 1]); kwd = d("kw", [64, 1]); sk = d("sinks", [1, 4]); mk = d("masks", [4, 128, 512])
    out = P.dram("out", [PN, 256], F32, "ExternalOutput")
    attn_stage(P, PN=PN, qT=qT, kT=kT, v=vv, qw=qwd, kw=kwd, sinks=sk, masks=mk, out=out)
    nc = P.finalize()
    _PROGS[key] = nc
    return nc


def _build_rwkv(PN):
    key = ("rwkv", PN)
    if key in _PROGS:
        return _PROGS[key]
    P = Prog()
    d = lambda n, s: P.dram(n, s, F32, "ExternalInput")
    C = rwkv_consts()
    rkvp = d("rkvp", [PN + 1, 768]); xwT = d("xwT", [64, PN + 1]); xaT = d("xaT", [64, PN + 1]); xgT = d("xgT", [160, PN + 1])
    mu_rkv = d("mu_rkv", [1, 768]); mu_w = d("mu_w", [64, 1]); mu_a = d("mu_a", [64, 1]); mu_g = d("mu_g", [160, 1])
    rows = d("rows", [7, 256]); wup = d("w_up", [64, 256]); aup = d("a_up", [64, 256]); gup = d("g_up", [160, 256])
    consts = {k: d("c_" + k, list(v.shape)) for k, v in C.items()}
    out = P.dram("out", [PN, 256], F32, "ExternalOutput")
    rwkv_stage(P, PN=PN, rkvp=rkvp, xwT=xwT, xaT=xaT, xgT=xgT, mu_rkv=mu_rkv, mu_w=mu_w, mu_a=mu_a, mu_g=mu_g, rows=rows,
               w_up=wup, a_up=aup, g_up=gup, consts=consts, out=out)
    nc = P.finalize()
    _PROGS[key] = nc
    return nc


def _build_ssd(PN):
    key = ("ssd", PN)
    if key in _PROGS:
        return _PROGS[key]
    P = Prog()
    d = lambda n, s: P.dram(n, s, F32, "ExternalInput")
    C = ssd_consts()
    xbcT = d("xbcT", [1536, PN + 3]); convw = d("convw", [128, 48]); convb = d("convb", [128, 12]); z = d("z", [PN, 1024])
    dtT = d("dtT", [16, PN]); dtb = d("dtb", [16, 1]); alog = d("alog", [16, 1]); dskip = d("dskip", [1, 1024])
    normw = d("normw", [1, 1024])
    consts = {k: d("c_" + k, [128, 128]) for k in C}
    out = P.dram("out", [PN, 1024], F32, "ExternalOutput")
    ssd_stage(P, PN=PN, xbcT=xbcT, convw=convw, convb=convb, z=z, dtT=dtT, dtb=dtb, alog=alog, dskip=dskip, normw=normw,
              consts=consts, out=out)
    nc = P.finalize()
    _PROGS[key] = nc
    return nc


def _c(a):
    return np.ascontiguousarray(a, dtype=np.float32)


def _gam(g):
    return _c(g.reshape(16, 128).T)


def _launch(nc, maps):
    res = run_bass_kernel_spmd(nc, maps, core_ids=list(range(8)))
    return res.results


def kernel(x, meta_tokens, mix_norm_w, ffn_norm_w, ar_w_in, ar_shift_mu, attn_q_norm_w, attn_k_norm_w, attn_sinks,
           rwkv_w0, rwkv_w_up, rwkv_a0, rwkv_a_up, rwkv_g_up, rwkv_k_k, rwkv_k_a, rwkv_r_k, rwkv_ln_w, rwkv_ln_b,
           ar_w_out, ssd_w_in, ssd_conv_w, ssd_conv_b, ssd_dt_bias, ssd_a_log, ssd_d, ssd_norm_w, ssd_w_out,
           ffn_w_up, ffn_conv_w, ffn_conv_b, ffn_w_down):
    f = lambda a: np.asarray(a, dtype=np.float32)
    x = f(x)
    B, SEQ, D = x.shape
    depth = mix_norm_w.shape[0]
    PN = SEQ + 128
    TOT = B * PN
    stride = TOT // 8
    assert stride * 8 == TOT
    N = -(-(stride + 2) // 704) * 704
    H = N - stride

    def windows_T(glob):
        Cc = glob.shape[1]
        outs = []
        for c in range(8):
            lo = c * stride - H
            w = np.zeros((Cc, N), np.float32)
            s0 = max(lo, 0)
            w[:, s0 - lo:] = glob[s0:lo + N].T
            outs.append(w)
        return outs

    def unwindow(outs, name):
        Cc = outs[0][name].shape[0]
        glob = np.empty((TOT, Cc), np.float32)
        for c in range(8):
            glob[c * stride:(c + 1) * stride] = outs[c][name][:, H:].T
        return glob

    res_glob = np.zeros((TOT, D), np.float32)
    valid = np.zeros((TOT, 1), np.float32)
    for b in range(B):
        res_glob[b * PN + 112:b * PN + 128] = f(meta_tokens)
        res_glob[b * PN + 128:(b + 1) * PN] = x[b]
        valid[b * PN + 112:(b + 1) * PN] = 1.0
    mask_w = [_c(w) for w in windows_T(valid)]

    def in_proj_weights(layer):
        i = layer // 2
        if layer % 2 == 0:
            return tile_w(f(ar_w_in[i])), ar_w_in.shape[2]
        return tile_w(f(ssd_w_in[i])), ssd_w_in.shape[2]

    Win, Cn = in_proj_weights(0)
    nc = _build_tp(N, 0, Cn, True)
    res_w = windows_T(res_glob)
    gam_n = _gam(f(mix_norm_w[0]))
    outs = _launch(nc, [{"resT": res_w[c], "mask": mask_w[c], "gam_n": gam_n, "Win": Win} for c in range(8)])
    z_glob = unwindow(outs, "znext")
    del Win

    amasks = attn_masks()
    rconsts = rwkv_consts()
    sconsts = ssd_consts()
    for layer in range(depth):
        i = layer // 2
        if layer % 2 == 0:
            nc = _build_attn(PN)
            maps = []
            for c in range(8):
                b, g = c // 4, c % 4
                zb = z_glob[b * PN:(b + 1) * PN]
                maps.append({"qT": _c(zb[:, g * 256:(g + 1) * 256].reshape(PN, 4, 64).transpose(2, 1, 0)),
                             "kT": _c(zb[:, 1024 + g * 64:1024 + (g + 1) * 64].T),
                             "v": _c(zb[:, 1280 + g * 64:1280 + (g + 1) * 64]),
                             "qw": _c(f(attn_q_norm_w[i])[:, None]), "kw": _c(f(attn_k_norm_w[i])[:, None]),
                             "sinks": _c(f(attn_sinks[i])[None, 4 * g:4 * g + 4]), "masks": amasks})
            outs = _launch(nc, maps)
            mix_glob = np.empty((TOT, 2048), np.float32)
            for c in range(8):
                b, g = c // 4, c % 4
                mix_glob[b * PN:(b + 1) * PN, g * 256:(g + 1) * 256] = outs[c]["out"]
            nc = _build_rwkv(PN)
            mu = f(ar_shift_mu[i])
            w0, a0, k_k, k_a = f(rwkv_w0[i]), f(rwkv_a0[i]), f(rwkv_k_k[i]), f(rwkv_k_a[i])
            r_k, ln_w, ln_b = f(rwkv_r_k[i]).reshape(-1), f(rwkv_ln_w[i]), f(rwkv_ln_b[i])
            w_up, a_up, g_up = f(rwkv_w_up[i]), f(rwkv_a_up[i]), f(rwkv_g_up[i])
            maps = []
            for c in range(8):
                b, u = c // 4, c % 4
                z0 = z_glob[b * PN:(b + 1) * PN, 1536:]
                cs = slice(256 * u, 256 * u + 256)
                rkv = np.concatenate([z0[:, 0:1024][:, cs], z0[:, 1024:2048][:, cs], z0[:, 2048:3072][:, cs]], 1)
                m = {"rkvp": np.concatenate([np.zeros((1, 768), np.float32), rkv], 0),
                     "xwT": np.concatenate([np.zeros((64, 1), np.float32), z0[:, 3072:3136].T], 1),
                     "xaT": np.concatenate([np.zeros((64, 1), np.float32), z0[:, 3136:3200].T], 1),
                     "xgT": np.concatenate([np.zeros((160, 1), np.float32), z0[:, 3200:3360].T], 1),
                     "mu_rkv": np.concatenate([mu[0:1024][cs], mu[1024:2048][cs], mu[2048:3072][cs]])[None, :],
                     "mu_w": mu[3072:3136, None], "mu_a": mu[3136:3200, None], "mu_g": mu[3200:3360, None],
                     "rows": np.stack([w0[cs], a0[cs], k_k[cs], k_a[cs], r_k[cs], ln_w[cs], ln_b[cs]]),
                     "w_up": w_up[:, cs], "a_up": a_up[:, cs], "g_up": g_up[:, cs]}
                for k, v in rconsts.items():
                    m["c_" + k] = v
                maps.append({k: _c(v) for k, v in m.items()})
            outs = _launch(nc, maps)
            for c in range(8):
                b, u = c // 4, c % 4
                mix_glob[b * PN:(b + 1) * PN, 1024 + u * 256:1024 + (u + 1) * 256] = outs[c]["out"]
            Wout = tile_w(f(ar_w_out[i]))
            Kmix = 2048
        else:
            nc = _build_ssd(PN)
            conv_w, conv_b = f(ssd_conv_w[i]), f(ssd_conv_b[i])
            dt_bias, a_log, d_skip, norm_w = f(ssd_dt_bias[i]), f(ssd_a_log[i]), f(ssd_d[i]), f(ssd_norm_w[i])
            maps = []
            for c in range(8):
                b, cidx = c // 4, c % 4
                z0 = z_glob[b * PN:(b + 1) * PN]
                g0 = 2 * cidx
                xs_ = slice(4096 + g0 * 512, 4096 + g0 * 512 + 1024)
                bs_ = slice(8192 + g0 * 128, 8192 + g0 * 128 + 256)
                cs_ = slice(9216 + g0 * 128, 9216 + g0 * 128 + 256)
                chan = np.concatenate([np.arange(4096)[g0 * 512:g0 * 512 + 1024], 4096 + np.arange(1024)[g0 * 128:g0 * 128 + 256],
                                       5120 + np.arange(1024)[g0 * 128:g0 * 128 + 256]])
                xbc = np.concatenate([z0[:, xs_], z0[:, bs_], z0[:, cs_]], 1)
                hs = slice(g0 * 8, g0 * 8 + 16)
                m = {"xbcT": np.concatenate([np.zeros((1536, 3), np.float32), xbc.T], 1),
                     "convw": conv_w[:, chan].T.reshape(12, 128, 4).transpose(1, 0, 2).reshape(128, 48),
                     "convb": conv_b[chan].reshape(12, 128).T,
                     "z": z0[:, g0 * 512:g0 * 512 + 1024],
                     "dtT": z0[:, 10240 + g0 * 8:10240 + g0 * 8 + 16].T, "dtb": dt_bias[hs, None], "alog": a_log[hs, None],
                     "dskip": np.repeat(d_skip[hs], 64)[None, :], "normw": norm_w[g0 * 512:g0 * 512 + 1024][None, :]}
                for k, v in sconsts.items():
                    m["c_" + k] = v
                maps.append({k: _c(v) for k, v in m.items()})
            outs = _launch(nc, maps)
            mix_glob = np.empty((TOT, 4096), np.float32)
            for c in range(8):
                b, cidx = c // 4, c % 4
                mix_glob[b * PN:(b + 1) * PN, cidx * 1024:(cidx + 1) * 1024] = outs[c]["out"]
            Wout = tile_w(f(ssd_w_out[i]))
            Kmix = 4096
        del z_glob, outs
        last = (layer == depth - 1)
        if not last:
            Win, Cn = in_proj_weights(layer + 1)
        else:
            Win, Cn = None, 0
        nc = _build_tp(N, Kmix, Cn, False)
        mix_w = windows_T(mix_glob)
        res_w = windows_T(res_glob)
        del mix_glob
        cwv = f(ffn_conv_w[layer])
        shared = {"Wout": Wout, "gam_f": _gam(f(ffn_norm_w[layer])), "Wup": tile_w(f(ffn_w_up[layer])),
                  "cw": _c(cwv.T.reshape(44, 128, 3).transpose(1, 0, 2).reshape(128, 132)),
                  "cb": _c(f(ffn_conv_b[layer]).reshape(44, 128).T), "Wdown": tile_w(f(ffn_w_down[layer]))}
        if not last:
            shared["gam_n"] = _gam(f(mix_norm_w[layer + 1]))
            shared["Win"] = Win
        maps = []
        for c in range(8):
            m = dict(shared)
            m["mixT"] = mix_w[c]
            m["resT"] = res_w[c]
            m["mask"] = mask_w[c]
            maps.append(m)
        outs = _launch(nc, maps)
        del maps, shared, mix_w, res_w, Wout, Win
        res_glob = unwindow(outs, "res2")
        if not last:
            z_glob = unwindow(outs, "znext")
        del outs
    out = np.empty((B, SEQ, D), np.float32)
    for b in range(B):
        out[b] = res_glob[b * PN + 128:(b + 1) * PN]
    return out
```

```python
from contextlib import ExitStack
import numpy as np
import concourse.bass as bass
import concourse.mybir as mybir
from concourse.bass_utils import run_bass_kernel_spmd

F32 = mybir.dt.float32
BF16 = mybir.dt.bfloat16
AF = mybir.ActivationFunctionType
ALU = mybir.AluOpType
AX = mybir.AxisListType


class Res:
    __slots__ = ("w", "r", "excl")

    def __init__(self, excl=False):
        self.w = None
        self.r = {}
        self.excl = excl


class _Eng:
    def __init__(self, name, sem):
        self.name = name
        self.sem = sem
        self.count = 0
        self.waited = {}
        self.ops = []


class _Rec:
    def __init__(self):
        self.call = None

    def __getattr__(self, name):
        def f(*args, **kwargs):
            self.call = (name, args, kwargs)
            return self
        return f


class _Slot:
    def __init__(self, key, sem):
        self.key = key
        self.sem = sem
        self.uses = 0


class Prog:
    ENGS = ("sp", "act", "dve", "pool", "pe")

    def __init__(self, nslots=6, self_sync=True):
        self.nc = bass.Bass("TRN2", target_bir_lowering=False)
        self.es = ExitStack()
        self.self_sync = self_sync
        self.E = {}
        for n in self.ENGS:
            self.E[n] = _Eng(n, self.es.enter_context(self.nc.semaphore("s_" + n)))
        self.slots = {}
        self.slot_rr = {}
        for q in ("sp", "pool", "act"):
            self.slots[q] = [_Slot("d_%s%d" % (q, i), self.es.enter_context(self.nc.semaphore("d_%s%d" % (q, i))))
                             for i in range(nslots)]
            self.slot_rr[q] = 0
        self._n = 0

    def dram(self, name, shape, dtype, kind):
        return self.nc.dram_tensor(name, list(shape), dtype, kind=kind).ap()

    def sbuf(self, shape, dtype, name=None):
        self._n += 1
        return self.es.enter_context(self.nc.sbuf_tensor(name or "sb%d" % self._n, list(shape), dtype))

    def psum(self, shape, dtype, name=None):
        self._n += 1
        return self.es.enter_context(self.nc.psum_tensor(name or "ps%d" % self._n, list(shape), dtype))

    def _wait(self, E, tok):
        key, sem, val = tok
        if E.waited.get(key, 0) >= val:
            return
        E.waited[key] = val
        E.ops.append(lambda h, sem=sem, val=val: h.wait_ge(sem, val))

    def _deps(self, E, reads, writes, ekey):
        toks = []
        for r in reads:
            if r.w is not None:
                toks.append(r.w)
        for w in writes:
            if w.w is not None:
                toks.append(w.w)
            for k, t in w.r.items():
                if k == ekey:
                    continue
                toks.append(t)
        for t in toks:
            if t[0] == ekey and (E.name == "pe" or not self.self_sync):
                continue
            self._wait(E, t)

    def op(self, eng, fn, reads=(), writes=()):
        E = self.E[eng]
        ekey = "s_" + eng
        ex = [r for r in reads if r.excl]
        if ex:
            writes = list(writes) + ex
            reads = [r for r in reads if not r.excl]
        self._deps(E, reads, writes, ekey)
        E.count += 1
        tok = (ekey, E.sem, E.count)
        rec = _Rec()
        fn(rec)
        E.ops.append(lambda h, c=rec.call, sem=E.sem: getattr(h, c[0])(*c[1], **c[2]).then_inc(sem, 1))
        E.waited[ekey] = max(E.waited.get(ekey, 0), 0)
        for w in writes:
            w.w = tok
            w.r = {}
        for r in reads:
            r.r[ekey] = tok
        return tok

    def dma(self, q, out, in_, reads=(), writes=()):
        E = self.E[q]
        sl = self.slots[q][self.slot_rr[q] % len(self.slots[q])]
        self.slot_rr[q] += 1
        if sl.uses > 0:
            self._wait(E, (sl.key, sl.sem, 16 * sl.uses))
        self._deps(E, reads, writes, None)
        sl.uses += 1
        tok = (sl.key, sl.sem, 16 * sl.uses)
        E.ops.append(lambda h, out=out, in_=in_, sem=sl.sem: h.dma_start(out=out, in_=in_).then_inc(sem, 16))
        for w in writes:
            w.w = tok
            w.r = {}
        for r in reads:
            r.r[sl.key] = tok
        return tok

    def finalize(self):
        for q, sls in self.slots.items():
            for sl in sls:
                if sl.uses > 0:
                    self._wait(self.E[q], (sl.key, sl.sem, 16 * sl.uses))
        nc = self.nc
        with nc.Block() as block:
            decos = {"sp": block.sync, "act": block.scalar, "dve": block.vector,
                     "pool": block.gpsimd, "pe": block.tensor}
            for n in self.ENGS:
                ops = self.E[n].ops
                if not ops:
                    continue

                def body(h, ops=ops):
                    for f in ops:
                        f(h)
                decos[n](body)
        self.es.close()
        return nc

    def ninstr(self):
        return {n: len(self.E[n].ops) for n in self.ENGS}

NT = 352


class LinCtx:
    def __init__(self, P, KCmax_small=16, TSmax=1408, KCmax=44, opnd_elems=44 * 704):
        self.P = P
        self.opnd = P.sbuf([128, opnd_elems], BF16, "opnd")
        self.opnd_r = [Res() for _ in range(KCmax)]
        self.NST = 3
        self.stg = [P.sbuf([128, TSmax + 2], F32, "stg%d" % i) for i in range(self.NST)]
        self.stg_r = [Res() for _ in range(self.NST)]
        self.stg_i = 0
        self.stg2 = [P.sbuf([128, 704], F32, "stgv%d" % i) for i in range(2)]
        self.stg2_r = [Res() for _ in range(2)]
        self.stg2_i = 0
        self.tmp = [P.sbuf([128, TSmax], F32, "tmp%d" % i) for i in range(2)]
        self.tmp_r = [Res() for _ in range(2)]
        self.tmp_i = 0
        self.sq = [P.sbuf([128, TSmax], BF16, "sq%d" % i) for i in range(2)]
        self.sq_r = [Res() for _ in range(2)]
        self.rstd = P.sbuf([128, TSmax], F32, "rstd")
        self.rstd_r = Res()
        self.maskb = P.sbuf([128, TSmax], F32, "maskb")
        self.maskb_r = Res()
        self.NW = 3
        self.wbf = [P.sbuf([128, KCmax * 128], BF16, "wbf%d" % i) for i in range(self.NW)]
        self.wbf_r = [Res() for _ in range(self.NW)]
        self.w_i = 0
        self.osb = [P.sbuf([128, TSmax], F32, "osb%d" % i) for i in range(2)]
        self.osb_r = [[Res() for _ in range(4)] for _ in range(2)]
        self.rsb = [P.sbuf([128, TSmax], F32, "rsb%d" % i) for i in range(2)]
        self.rsb_r = [Res() for _ in range(2)]
        self.o_i = 0
        self.ps = [P.psum([128, 512], F32, "lps%d" % i) for i in range(8)]
        self.ps_r = [Res(excl=True) for _ in range(8)]
        self.ps_i = 0
        self.ones = P.sbuf([128, 128], BF16, "ones_bf")
        self.ones_r = Res()
        P.op("dve", lambda h: h.memset(self.ones[:], 1.0), writes=[self.ones_r])
        self.small = {}

    def const(self, name, shape, src_ap, q="sp"):
        t = self.P.sbuf(shape, F32, name)
        r = Res()
        self.P.dma(q, t[:], src_ap, writes=[r])
        return t, r


def linear_stage(L, *, mode, K, C, N, W, dst, src=None, src2=None, gamma=None, mask=None,
                 convw=None, convb=None, res=None, eps=1e-6):
    P = L.P
    KC = K // 128
    CB = (C + 127) // 128
    TS = 1408 if (K == 2048 and N % 1408 == 0) else 704
    assert N % TS == 0
    nsub = TS // NT
    opnd = L.opnd
    opv = opnd[:, 0:KC * TS].rearrange("p (k t) -> p k t", k=KC)

    if mode == "norm":
        gam_t, gam_r = L.const("gam%d" % id(gamma), [128, KC], gamma)
    if mode == "convglu":
        cw_t, cw_r = L.const("cw%d" % id(convw), [128, KC * 3], convw)
        cb_t, cb_r = L.const("cb%d" % id(convb), [128, KC], convb)

    for ts in range(N // TS):
        t0 = ts * TS
        if mode == "plain":
            for kc in range(KC):
                P.dma("pool", opv[:, kc, :], src[kc * 128:(kc + 1) * 128, t0:t0 + TS], writes=[L.opnd_r[kc]])
        elif mode == "norm":
            P.dma("pool", L.maskb[:, 0:TS], mask[0:1, t0:t0 + TS].partition_broadcast(128), writes=[L.maskb_r])
            banks = []
            for s in range(nsub):
                b = L.ps_i % 8
                L.ps_i += 1
                banks.append(b)
            for kc in range(KC):
                i = L.stg_i % L.NST
                L.stg_i += 1
                st, sr = L.stg[i], L.stg_r[i]
                P.dma("sp", st[:, 0:TS], src[kc * 128:(kc + 1) * 128, t0:t0 + TS], writes=[sr])
                sq, sqr = L.sq[kc % 2], L.sq_r[kc % 2]
                P.op("act", lambda h, st=st, sq=sq: h.activation(sq[:, 0:TS], st[:, 0:TS], AF.Square),
                     reads=[sr], writes=[sqr])
                for s in range(nsub):
                    b = banks[s]
                    P.op("pe", lambda h, b=b, sq=sq, s=s, kc=kc: h.matmul(
                        L.ps[b][:, 0:NT], L.ones[:, :], sq[:, s * NT:(s + 1) * NT],
                        start=(kc == 0), stop=(kc == KC - 1)),
                        reads=[sqr, L.ones_r], writes=[L.ps_r[b]])
            for s in range(nsub):
                b = banks[s]
                P.op("dve", lambda h, b=b, s=s: h.tensor_scalar(
                    L.rstd[:, s * NT:(s + 1) * NT], L.ps[b][:, 0:NT], 1.0 / K, eps, ALU.mult, ALU.add),
                    reads=[L.ps_r[b]], writes=[L.rstd_r])
            P.op("act", lambda h: h.activation(L.rstd[:, 0:TS], L.rstd[:, 0:TS], AF.Ln),
                 reads=[L.rstd_r], writes=[L.rstd_r])
            P.op("act", lambda h: h.activation(L.rstd[:, 0:TS], L.rstd[:, 0:TS], AF.Exp, scale=-0.5),
                 reads=[L.rstd_r], writes=[L.rstd_r])
            P.op("dve", lambda h: h.tensor_tensor(L.rstd[:, 0:TS], L.rstd[:, 0:TS], L.maskb[:, 0:TS], ALU.mult),
                 reads=[L.rstd_r, L.maskb_r], writes=[L.rstd_r])
            for kc in range(KC):
                i = L.stg_i % L.NST
                L.stg_i += 1
                st, sr = L.stg[i], L.stg_r[i]
                P.dma("sp", st[:, 0:TS], src[kc * 128:(kc + 1) * 128, t0:t0 + TS], writes=[sr])
                eng = "dve"
                P.op(eng, lambda h, st=st, kc=kc: h.scalar_tensor_tensor(
                    opv[:, kc, :], st[:, 0:TS], gam_t[:, kc:kc + 1], L.rstd[:, 0:TS], ALU.mult, ALU.mult),
                    reads=[sr, gam_r, L.rstd_r], writes=[L.opnd_r[kc]])
        elif mode == "convglu":
            for kc in range(KC):
                i = L.stg_i % L.NST
                L.stg_i += 1
                st, sr = L.stg[i], L.stg_r[i]
                rows = slice(kc * 128, (kc + 1) * 128)
                if t0 == 0:
                    P.op("pool", lambda h, st=st: h.memset(st[:, 0:2], 0.0), writes=[sr])
                    P.dma("sp", st[:, 2:TS + 2], src[rows, 0:TS], writes=[sr])
                else:
                    P.dma("sp", st[:, 0:TS + 2], src[rows, t0 - 2:t0 + TS], writes=[sr])
                j = L.stg2_i % 2
                L.stg2_i += 1
                sv, svr = L.stg2[j], L.stg2_r[j]
                P.dma("pool", sv[:, 0:TS], src2[rows, t0:t0 + TS], writes=[svr])
                k = L.tmp_i % 2
                L.tmp_i += 1
                tm, tmr = L.tmp[k], L.tmp_r[k]
                e1 = "dve"
                P.op("act", lambda h, st=st, tm=tm, kc=kc: h.activation(
                    tm[:, 0:TS], st[:, 2:TS + 2], AF.Identity, bias=cb_t[:, kc:kc + 1], scale=cw_t[:, kc * 3 + 2:kc * 3 + 3]),
                    reads=[sr, cw_r, cb_r], writes=[tmr])
                P.op(e1, lambda h, st=st, tm=tm, kc=kc: h.scalar_tensor_tensor(
                    tm[:, 0:TS], st[:, 1:TS + 1], cw_t[:, kc * 3 + 1:kc * 3 + 2], tm[:, 0:TS],
                    ALU.mult, ALU.add), reads=[sr, cw_r, tmr], writes=[tmr])
                P.op(e1, lambda h, st=st, tm=tm, kc=kc: h.scalar_tensor_tensor(
                    tm[:, 0:TS], st[:, 0:TS], cw_t[:, kc * 3:kc * 3 + 1], tm[:, 0:TS],
                    ALU.mult, ALU.add), reads=[sr, cw_r, tmr], writes=[tmr])
                P.op("act", lambda h, tm=tm: h.activation(tm[:, 0:TS], tm[:, 0:TS], AF.Silu),
                     reads=[tmr], writes=[tmr])
                P.op("dve", lambda h, tm=tm, sv=sv, kc=kc: h.tensor_tensor(
                    opv[:, kc, :], tm[:, 0:TS], sv[:, 0:TS], ALU.mult),
                    reads=[tmr, svr], writes=[L.opnd_r[kc]])
        else:
            raise ValueError(mode)

        NW = L.NW
        base = L.w_i
        for pre in range(min(NW - 1, CB)):
            wi = (base + pre) % NW
            P.dma("pool", L.wbf[wi][:, 0:KC * 128], W[pre], writes=[L.wbf_r[wi]])
        if res is not None:
            M0 = min(128, C)
            P.dma("sp", L.rsb[L.o_i % 2][0:M0, 0:TS], res[0:M0, t0:t0 + TS], writes=[L.rsb_r[L.o_i % 2]])
        for cb in range(CB):
            M = min(128, C - cb * 128)
            wi = (base + cb) % NW
            wbf, wbfr = L.wbf[wi], L.wbf_r[wi]
            if cb + NW - 1 < CB:
                wj = (base + cb + NW - 1) % NW
                P.dma("pool", L.wbf[wj][:, 0:KC * 128], W[cb + NW - 1], writes=[L.wbf_r[wj]])
            oi = L.o_i % 2
            L.o_i += 1
            osb, osbr, rsb, rsbr = L.osb[oi], L.osb_r[oi], L.rsb[oi], L.rsb_r[oi]
            if res is not None and cb + 1 < CB:
                M1 = min(128, C - (cb + 1) * 128)
                P.dma("sp", L.rsb[1 - oi][0:M1, 0:TS], res[(cb + 1) * 128:(cb + 1) * 128 + M1, t0:t0 + TS], writes=[L.rsb_r[1 - oi]])
            for s in range(nsub):
                b = L.ps_i % 8
                L.ps_i += 1
                for kc in range(KC):
                    P.op("pe", lambda h, b=b, wbf=wbf, kc=kc, s=s, M=M: h.matmul(
                        L.ps[b][0:M, 0:NT], wbf[:, kc * 128:kc * 128 + M], opv[:, kc, s * NT:(s + 1) * NT],
                        start=(kc == 0), stop=(kc == KC - 1)),
                        reads=[wbfr, L.opnd_r[kc]], writes=[L.ps_r[b]])
                if res is not None:
                    P.op("dve", lambda h, b=b, s=s, M=M, osb=osb, rsb=rsb: h.tensor_tensor(
                        osb[0:M, s * NT:(s + 1) * NT], L.ps[b][0:M, 0:NT], rsb[0:M, s * NT:(s + 1) * NT], ALU.add),
                        reads=[L.ps_r[b], rsbr], writes=[osbr[s]])
                else:
                    if s % 2 == 0:
                        P.op("act", lambda h, b=b, s=s, M=M, osb=osb: h.copy(
                            osb[0:M, s * NT:(s + 1) * NT], L.ps[b][0:M, 0:NT]),
                            reads=[L.ps_r[b]], writes=[osbr[s]])
                    else:
                        P.op("dve", lambda h, b=b, s=s, M=M, osb=osb: h.tensor_copy(
                            osb[0:M, s * NT:(s + 1) * NT], L.ps[b][0:M, 0:NT]),
                            reads=[L.ps_r[b]], writes=[osbr[s]])
            P.dma("sp", dst[cb * 128:cb * 128 + M, t0:t0 + TS], osb[0:M, 0:TS], reads=osbr[0:nsub])
        L.w_i = base + CB


def tile_w(W, dtype=np.float32):
    K, C = W.shape
    CB = (C + 127) // 128
    KC = K // 128
    Wp = np.zeros((K, CB * 128), dtype)
    Wp[:, :C] = W
    return np.ascontiguousarray(Wp.reshape(KC, 128, CB, 128).transpose(2, 1, 0, 3).reshape(CB, 128, KC * 128))

TQ = 384


def attn_stage(P, *, PN, qT, kT, v, qw, kw, sinks, masks, out):
    NB = PN // 128
    NTL = PN // TQ
    assert PN % TQ == 0
    ones64 = P.sbuf([64, 64], F32, "a_ones64")
    ones_r = Res()
    P.op("dve", lambda h: h.memset(ones64[:], 1.0), writes=[ones_r])
    qw_t = P.sbuf([64, 1], F32, "a_qw")
    kw_t = P.sbuf([64, 1], F32, "a_kw")
    cr = Res()
    P.dma("sp", qw_t[:], qw, writes=[cr])
    P.dma("sp", kw_t[:], kw, writes=[cr])
    P.op("dve", lambda h: h.tensor_scalar(qw_t[:], qw_t[:], 0.125, None, ALU.mult), reads=[cr], writes=[cr])
    esink = P.sbuf([128, 4], F32, "a_esink")
    es_r = Res()
    P.dma("sp", esink[:], sinks[0:1, :].partition_broadcast(128), writes=[es_r])
    P.op("act", lambda h: h.activation(esink[:], esink[:], AF.Exp), reads=[es_r], writes=[es_r])
    mstage = P.sbuf([128, 512], F32, "a_mstage")
    ms_r = Res()
    mk = []
    mk_r = []
    for i in range(4):
        m = P.sbuf([128, 512], BF16, "a_mask%d" % i)
        r = Res()
        P.dma("sp", mstage[:], masks[i], writes=[ms_r])
        P.op("dve", lambda h, m=m: h.tensor_copy(m[:], mstage[:]), reads=[ms_r], writes=[r])
        mk.append(m)
        mk_r.append(r)
    M_CUR, M_PREV, M_CUR0, M_PREV1 = 0, 1, 2, 3

    khat = P.sbuf([64, PN], BF16, "a_khat")
    khat_r = [Res() for _ in range(NTL)]
    vaug = P.sbuf([128, NB * 65], BF16, "a_vaug")
    vaug3 = vaug[:, :].rearrange("p (n c) -> p n c", c=65)
    vaug_r = [Res() for _ in range(NTL)]
    vones_r = Res()
    P.op("pool", lambda h: h.memset(vaug[:], 1.0), writes=[vones_r])
    vmeta = P.sbuf([16, 65], BF16, "a_vmeta")
    vmeta_s = P.sbuf([16, 64], F32, "a_vmeta_s")
    vm_r = Res()
    P.op("pool", lambda h: h.memset(vmeta[:], 1.0), writes=[vm_r])
    P.dma("sp", vmeta_s[:], v[112:128, :], writes=[vm_r])
    P.op("dve", lambda h: h.tensor_copy(vmeta[:, 0:64], vmeta_s[:]), reads=[vm_r], writes=[vm_r])

    NBUF = 2
    qst = [P.sbuf([64, 4 * TQ], F32, "a_qst%d" % i) for i in range(NBUF)]
    qst_r = [Res() for _ in range(NBUF)]
    kst = [P.sbuf([64, TQ], F32, "a_kst%d" % i) for i in range(NBUF)]
    kst_r = [Res() for _ in range(NBUF)]
    vst = [P.sbuf([128, 3 * 64], F32, "a_vst%d" % i) for i in range(NBUF)]
    vst_r = [Res() for _ in range(NBUF)]
    sqb = [P.sbuf([64, 5 * TQ], F32, "a_sq%d" % i) for i in range(NBUF)]
    sqb_r = [Res() for _ in range(NBUF)]
    rq = [P.sbuf([64, 5 * TQ], F32, "a_rq%d" % i) for i in range(NBUF)]
    rq_r = [Res() for _ in range(NBUF)]
    qhat = [P.sbuf([64, 4 * TQ], BF16, "a_qhat%d" % i) for i in range(NBUF)]
    qhat_r = [Res() for _ in range(NBUF)]
    osb = [P.sbuf([128, 3 * 256], F32, "a_osb%d" % i) for i in range(NBUF)]
    osb_r = [Res() for _ in range(NBUF)]
    NPT = 3
    pt = [P.sbuf([128, 512], BF16, "a_pt%d" % i) for i in range(NPT * 3)]
    pt_r = [Res() for _ in range(NPT * 3)]
    den = [P.sbuf([128, 8], F32, "a_den%d" % i) for i in range(2)]
    den_r = [Res() for _ in range(2)]
    ps_sq = [P.psum([64, 512], F32, "a_pssq%d" % i) for i in range(2)]
    ps_sq_r = [Res(excl=True) for _ in range(2)]
    ps_sc = [P.psum([128, 512], F32, "a_pssc%d" % i) for i in range(4)]
    ps_sc_r = [Res(excl=True) for _ in range(4)]
    ps_o = [P.psum([128, 4 * 65], F32, "a_pso%d" % i) for i in range(2)]
    ps_o_r = [Res(excl=True) for _ in range(2)]
    cnt = {"sq": 0, "sc": 0, "o": 0, "pt": 0}

    for tl in range(NTL):
        bi = tl % NBUF
        t0 = tl * TQ
        P.dma("sp", qst[bi][:, :].rearrange("p (h t) -> p h t", h=4), qT[:, :, t0:t0 + TQ], writes=[qst_r[bi]])
        P.dma("sp", kst[bi][:, :], kT[:, t0:t0 + TQ], writes=[kst_r[bi]])
        P.dma("pool", vst[bi][:, :].rearrange("p (n c) -> p n c", c=64),
              v[t0:t0 + TQ, :].rearrange("(n p) c -> p n c", p=128), writes=[vst_r[bi]])
        P.op("act", lambda h, bi=bi: h.activation(sqb[bi][:, 0:4 * TQ], qst[bi][:, :], AF.Square),
             reads=[qst_r[bi]], writes=[sqb_r[bi]])
        P.op("act", lambda h, bi=bi: h.activation(sqb[bi][:, 4 * TQ:5 * TQ], kst[bi][:, :], AF.Square),
             reads=[kst_r[bi]], writes=[sqb_r[bi]])
        for j in range(5):
            b = cnt["sq"] % 2
            cnt["sq"] += 1
            P.op("pe", lambda h, b=b, j=j, bi=bi: h.matmul(ps_sq[b][:, 0:TQ], ones64[:, :], sqb[bi][:, j * TQ:(j + 1) * TQ],
                                                           start=True, stop=True),
                 reads=[ones_r, sqb_r[bi]], writes=[ps_sq_r[b]])
            P.op("dve", lambda h, b=b, j=j, bi=bi: h.tensor_scalar(rq[bi][:, j * TQ:(j + 1) * TQ], ps_sq[b][:, 0:TQ],
                                                                   1.0 / 64, 1e-6, ALU.mult, ALU.add),
                 reads=[ps_sq_r[b]], writes=[rq_r[bi]])
        P.op("act", lambda h, bi=bi: h.activation(rq[bi][:, :], rq[bi][:, :], AF.Ln), reads=[rq_r[bi]], writes=[rq_r[bi]])
        P.op("act", lambda h, bi=bi: h.activation(rq[bi][:, :], rq[bi][:, :], AF.Exp, scale=-0.5),
             reads=[rq_r[bi]], writes=[rq_r[bi]])
        P.op("dve", lambda h, bi=bi: h.scalar_tensor_tensor(qhat[bi][:, :], qst[bi][:, :], qw_t[:, 0:1], rq[bi][:, 0:4 * TQ],
                                                            ALU.mult, ALU.mult),
             reads=[qst_r[bi], cr, rq_r[bi]], writes=[qhat_r[bi]])
        P.op("dve", lambda h, bi=bi, t0=t0: h.scalar_tensor_tensor(khat[:, t0:t0 + TQ], kst[bi][:, :], kw_t[:, 0:1],
                                                                    rq[bi][:, 4 * TQ:5 * TQ], ALU.mult, ALU.mult),
             reads=[kst_r[bi], cr, rq_r[bi]], writes=[khat_r[tl]])
        P.op("pool", lambda h, bi=bi, tl=tl: h.tensor_copy(vaug3[:, tl * 3:tl * 3 + 3, 0:64],
                                                           vst[bi][:, :].rearrange("p (n c) -> p n c", c=64)),
             reads=[vst_r[bi], vones_r], writes=[vaug_r[tl]])
        qh3 = qhat[bi][:, :].rearrange("p (h t) -> p h t", h=4)
        for bl in range(3):
            n = tl * 3 + bl
            parts = []
            if n >= 1:
                ptl = (n - 1) // 3
                parts.append(("prev", 128, khat[:, (n - 1) * 128:n * 128], M_PREV1 if n == 1 else M_PREV,
                              vaug3[:, n - 1, :], [khat_r[ptl], vaug_r[ptl]]))
            parts.append(("cur", 128, khat[:, n * 128:(n + 1) * 128], M_CUR0 if n == 0 else M_CUR,
                          vaug3[:, n, :], [khat_r[tl], vaug_r[tl]]))
            if n >= 2:
                parts.append(("meta", 16, khat[:, 112:128], None, vmeta[:, :], [khat_r[0], vm_r]))
            pts = []
            for (kind, kp, lhsT, mi, vr, deps) in parts:
                b = cnt["sc"] % 4
                cnt["sc"] += 1
                P.op("pe", lambda h, b=b, kp=kp, lhsT=lhsT, bl=bl, qh3=qh3: h.matmul(
                    ps_sc[b][0:kp, :], lhsT, qh3[:, :, bl * 128:(bl + 1) * 128], start=True, stop=True),
                    reads=[deps[0], qhat_r[bi]], writes=[ps_sc_r[b]])
                pi = cnt["pt"] % len(pt)
                cnt["pt"] += 1
                P.op("act", lambda h, b=b, kp=kp, pi=pi: h.activation(pt[pi][0:kp, :], ps_sc[b][0:kp, :], AF.Exp),
                     reads=[ps_sc_r[b]], writes=[pt_r[pi]])
                if mi is not None:
                    P.op("dve", lambda h, pi=pi, mi=mi: h.tensor_tensor(pt[pi][:, :], pt[pi][:, :], mk[mi][:, :], ALU.mult),
                         reads=[pt_r[pi], mk_r[mi]], writes=[pt_r[pi]])
                pts.append((pi, kp, vr, deps[1]))
            ob = cnt["o"] % 2
            cnt["o"] += 1
            po3 = ps_o[ob][:, :].rearrange("p (h c) -> p h c", c=65)
            for hh in range(4):
                for idx, (pi, kp, vr, vdep) in enumerate(pts):
                    P.op("pe", lambda h, pi=pi, kp=kp, vr=vr, hh=hh, idx=idx, po3=po3, npt=len(pts): h.matmul(
                        po3[:, hh, :], pt[pi][0:kp, hh * 128:(hh + 1) * 128], vr if kp == 128 else vr[0:kp, :],
                        start=(idx == 0), stop=(idx == npt - 1)),
                        reads=[pt_r[pi], vdep], writes=[ps_o_r[ob]])
            dn, dnr = den[ob], den_r[ob]
            P.op("dve", lambda h, dn=dn, po3=po3: h.tensor_tensor(dn[:, 0:4], po3[:, :, 64], esink[:, :], ALU.add),
                 reads=[ps_o_r[ob], es_r], writes=[dnr])
            P.op("dve", lambda h, dn=dn: h.reciprocal(dn[:, 4:8], dn[:, 0:4]), reads=[dnr], writes=[dnr])
            for hh in range(4):
                if hh % 2 == 0:
                    P.op("dve", lambda h, dn=dn, po3=po3, hh=hh, bl=bl, bi=bi: h.tensor_scalar(
                        osb[bi][:, bl * 256 + hh * 64: bl * 256 + hh * 64 + 64], po3[:, hh, 0:64], dn[:, 4 + hh:5 + hh], None,
                        ALU.mult), reads=[ps_o_r[ob], dnr], writes=[osb_r[bi]])
                else:
                    P.op("act", lambda h, dn=dn, po3=po3, hh=hh, bl=bl, bi=bi: h.activation(
                        osb[bi][:, bl * 256 + hh * 64: bl * 256 + hh * 64 + 64], po3[:, hh, 0:64], AF.Copy,
                        scale=dn[:, 4 + hh:5 + hh]), reads=[ps_o_r[ob], dnr], writes=[osb_r[bi]])
        P.dma("sp", out[t0:t0 + TQ, :].rearrange("(n p) c -> p n c", p=128),
              osb[bi][:, :].rearrange("p (n c) -> p n c", c=256), reads=[osb_r[bi]])


def attn_masks():
    k = np.arange(128)[:, None]
    q = np.arange(128)[None, :]
    cur = (k <= q)
    prev = (k > q)
    cur0 = cur & (k >= 112)
    prev1 = np.broadcast_to(k >= 112, (128, 128))
    return np.stack([np.tile(m.astype(np.float32), (1, 4)) for m in (cur, prev, cur0, prev1)])

LC = 64
WSC = -0.6065306597126334


def rwkv_consts():
    L = LC
    s = np.arange(L)[:, None]
    t = np.arange(L)[None, :]
    tri_incl = (s <= t).astype(np.float32)
    tri_strict = (s < t).astype(np.float32)
    upper = (s > t).astype(np.float32)
    c = {}
    c["TT1"] = np.concatenate([tri_strict, tri_incl], 1) * WSC
    c["TT2"] = np.concatenate([tri_incl, tri_incl], 1) * WSC
    c["TT3"] = np.concatenate([upper, upper], 1) * WSC
    c["negones"] = np.full((64, 64), WSC, np.float32)
    m = np.concatenate([tri_strict, tri_incl], 1)
    c["mask128"] = np.concatenate([m, m], 0)
    c["maskN"] = np.ascontiguousarray(tri_strict.T)
    c["ident"] = np.eye(128, dtype=np.float32)
    return {k: np.ascontiguousarray(v, dtype=np.float32) for k, v in c.items()}


class Rot:
    def __init__(self, items):
        self.items = items
        self.i = 0

    def next(self):
        x = self.items[self.i % len(self.items)]
        self.i += 1
        return x


def rwkv_stage(P, *, PN, rkvp, xwT, xaT, xgT, mu_rkv, mu_w, mu_a, mu_g, rows, w_up, a_up, g_up, consts, out,
               nchunks=None):
    NCH = PN // LC if nchunks is None else nchunks
    sb = lambda shape, name, dt=F32: P.sbuf(shape, dt, "r_" + name)

    cr = Res()
    ct = {}
    for nm, shp in (("TT1", [64, 128]), ("TT2", [64, 128]), ("TT3", [64, 128]), ("negones", [64, 64]),
                    ("mask128", [128, 128]), ("maskN", [64, 64]), ("ident", [128, 128])):
        ct[nm] = sb(shp, nm)
        P.dma("sp", ct[nm][:], consts[nm], writes=[cr])
    mu_b = sb([128, 768], "mu_b")
    P.dma("sp", mu_b[:], mu_rkv[0:1, :].partition_broadcast(128), writes=[cr])
    rowb = sb([128, 7 * 256], "rowb")
    for i in range(7):
        P.dma("sp", rowb[:, i * 256:(i + 1) * 256], rows[i:i + 1, :].partition_broadcast(128), writes=[cr])
    W0, A0, KK, KA, RK, LNW, LNB = [rowb[:, i * 256:(i + 1) * 256] for i in range(7)]
    muw = sb([64, 1], "muw"); mua = sb([64, 1], "mua"); mug0 = sb([128, 1], "mug0"); mug1 = sb([32, 1], "mug1")
    P.dma("sp", muw[:], mu_w, writes=[cr]); P.dma("sp", mua[:], mu_a, writes=[cr])
    P.dma("sp", mug0[:], mu_g[0:128, :], writes=[cr]); P.dma("sp", mug1[:], mu_g[128:160, :], writes=[cr])
    wup = sb([64, 256], "wup"); aup = sb([64, 256], "aup"); gup0 = sb([128, 256], "gup0"); gup1 = sb([32, 256], "gup1")
    P.dma("sp", wup[:], w_up, writes=[cr]); P.dma("sp", aup[:], a_up, writes=[cr])
    P.dma("sp", gup0[:], g_up[0:128, :], writes=[cr]); P.dma("sp", gup1[:], g_up[128:160, :], writes=[cr])

    NB = 2
    def bufs(shape, name, n=NB, dt=F32):
        return Rot([(sb(shape, "%s%d" % (name, i), dt), Res()) for i in range(n)])
    cur_b = bufs([128, 768], "cur"); prv_b = bufs([128, 768], "prv")
    xw_b = bufs([64, 65], "xw"); xa_b = bufs([64, 65], "xa"); xg0_b = bufs([128, 65], "xg0"); xg1_b = bufs([32, 65], "xg1")
    tw_b = bufs([64, 128], "tw"); ta_b = bufs([64, 128], "ta"); tg0_b = bufs([128, 64], "tg0"); tg1_b = bufs([32, 64], "tg1")
    sig_b = bufs([128, 256], "sig"); a_b = bufs([128, 256], "a"); g_b = bufs([64, 256], "g")
    kk_b = bufs([128, 256], "kk"); sq_b = bufs([128, 256], "sq"); ss_b = bufs([128, 8], "ss")
    t1_b = bufs([128, 256], "t1"); bkr_b = bufs([128, 256], "bkr")
    e1_b = bufs([128, 256], "e1"); e2_b = bufs([128, 256], "e2"); e3_b = bufs([128, 256], "e3")
    glb_b = bufs([64, 256], "glb")
    ar_b = bufs([128, 256], "ar"); bk_b = bufs([128, 256], "bk"); bkb_b = bufs([128, 256], "bkb")
    uv_b = [bufs([128, 64], "uv%d" % u) for u in range(4)]
    art_b = [bufs([64, 128], "art%d" % u) for u in range(4)]
    bkt_b = [bufs([64, 128], "bkt%d" % u) for u in range(4)]
    mts_b = [bufs([128, 128], "mts%d" % u) for u in range(4)]
    p_b = [[bufs([64, 64], "p%d_%d" % (pp, u), 3, BF16) for u in range(4)] for pp in range(2)]
    q_b = [[bufs([64, 64], "q%d_%d" % (pp, u), 3, BF16) for u in range(4)] for pp in range(2)]
    z_b = [[bufs([64, 128], "z%d_%d" % (pp, u), 3, BF16) for u in range(4)] for pp in range(2)]
    ap_b = [bufs([64, 64], "ap%d" % u) for u in range(4)]
    phit_b = [bufs([64, 64], "phit%d" % u) for u in range(4)]
    dgl_b = [bufs([64, 64], "dgl%d" % u) for u in range(4)]
    psi_b = [bufs([64, 64], "psi%d" % u) for u in range(4)]
    rpt_b = [bufs([64, 64], "rpt%d" % u) for u in range(4)]
    h_b = [bufs([64, 64], "h%d" % u, 2) for u in range(4)]
    ysq_b = bufs([64, 256], "ysq"); st_b = bufs([64, 16], "st"); yn_b = bufs([64, 256], "yn")
    rk_b = bufs([64, 256], "rk"); ob_b = bufs([64, 256], "ob")
    pbanks = [P.psum([128, 512], F32, "r_ps%d" % i) for i in range(8)]
    half = Rot([(pbanks[b][:, 0:256], Res(excl=True)) for b in range(2)])
    pso_reg = [(pbanks[2][:, 0:256], Res(excl=True)), (pbanks[3][:, 0:256], Res(excl=True))]
    quar = Rot([(pbanks[b][:, 0:128], Res(excl=True)) for b in range(4, 8)])

    H = []
    for u in range(4):
        ht, hr = h_b[u].next()
        P.op("pool", lambda h, ht=ht: h.memset(ht[:], 0.0), writes=[hr])
        H.append((ht, hr))

    def mm(ps, psr, lhsT, rhs, reads, start=True, stop=True):
        P.op("pe", lambda h: h.matmul(ps, lhsT, rhs, start=start, stop=stop), reads=reads, writes=[psr])

    def acopy(dst, dstr, src, srcr):
        P.op("act", lambda h: h.copy(dst, src), reads=[srcr], writes=[dstr])

    prep_done = set()
    chain_done = [0]

    def chunk_gen(c, par):
        t0 = c * LC
        cur, curr = cur_b.next(); prv, prvr = prv_b.next()
        for hh in range(2):
            P.dma("sp", cur[hh * 64:(hh + 1) * 64, :], rkvp[1 + t0:1 + t0 + 64, :], writes=[curr])
            P.dma("pool", prv[hh * 64:(hh + 1) * 64, :], rkvp[t0:t0 + 64, :], writes=[prvr])
        xw, xwr = xw_b.next(); xa, xar = xa_b.next(); xg0, xg0r = xg0_b.next(); xg1, xg1r = xg1_b.next()
        P.dma("sp", xw[:], xwT[:, t0:t0 + 65], writes=[xwr])
        P.dma("sp", xa[:], xaT[:, t0:t0 + 65], writes=[xar])
        P.dma("pool", xg0[:], xgT[0:128, t0:t0 + 65], writes=[xg0r])
        P.dma("pool", xg1[:], xgT[128:160, t0:t0 + 65], writes=[xg1r])
        P.op("dve", lambda h: h.tensor_tensor(prv[:], prv[:], cur[:], ALU.subtract), reads=[prvr, curr], writes=[prvr])
        P.op("dve", lambda h: h.tensor_tensor(prv[:], prv[:], mu_b[:], ALU.mult), reads=[prvr, cr], writes=[prvr])
        P.op("dve", lambda h: h.tensor_tensor(prv[:], prv[:], cur[:], ALU.add), reads=[prvr, curr], writes=[prvr])
        zs, zsr = prv, prvr
        yield
        Rr, Kr, Vr = zs[:, 0:256], zs[:, 256:512], zs[:, 512:768]
        tw, twr = tw_b.next(); ta, tar = ta_b.next(); tg0, tg0r = tg0_b.next(); tg1, tg1r = tg1_b.next()
        for (x, xr, mu, np_, dst, dstr, fn) in ((xw, xwr, muw, 64, tw, twr, AF.Tanh), (xa, xar, mua, 64, ta, tar, AF.Copy),
                                                (xg0, xg0r, mug0, 128, tg0, tg0r, AF.Sigmoid),
                                                (xg1, xg1r, mug1, 32, tg1, tg1r, AF.Sigmoid)):
            tmp, tmpr = (sq_b.next())
            P.op("dve", lambda h, x=x, np_=np_, tmp=tmp: h.tensor_tensor(tmp[0:np_, 0:64], x[0:np_, 0:64], x[0:np_, 1:65],
                                                                         ALU.subtract), reads=[xr], writes=[tmpr])
            P.op("dve", lambda h, x=x, np_=np_, tmp=tmp, mu=mu: h.scalar_tensor_tensor(
                tmp[0:np_, 0:64], tmp[0:np_, 0:64], mu[0:np_, 0:1], x[0:np_, 1:65], ALU.mult, ALU.add),
                reads=[xr, tmpr, cr], writes=[tmpr])
            if fn == AF.Tanh:
                P.op("act", lambda h, dst=dst, np_=np_, tmp=tmp: h.activation(dst[0:np_, 0:64], tmp[0:np_, 0:64], AF.Sigmoid,
                                                                             scale=2.0), reads=[tmpr], writes=[dstr])
                P.op("dve", lambda h, dst=dst, np_=np_: h.tensor_scalar(dst[0:np_, 0:64], dst[0:np_, 0:64], 2.0, -1.0,
                                                                        ALU.mult, ALU.add), reads=[dstr], writes=[dstr])
            else:
                P.op("act", lambda h, dst=dst, np_=np_, tmp=tmp, fn=fn: h.activation(dst[0:np_, 0:64], tmp[0:np_, 0:64], fn),
                     reads=[tmpr], writes=[dstr])
            if dst is tw or dst is ta:
                P.op("act", lambda h, dst=dst: h.copy(dst[:, 64:128], dst[:, 0:64]), reads=[dstr], writes=[dstr])
            yield
        sig, sigr = sig_b.next(); a, ar_ = a_b.next(); g, gr = g_b.next()
        ps, psr = half.next()
        mm(ps, psr, tw[:, :], wup[:, :], [twr, cr])
        P.op("dve", lambda h, ps=ps: h.tensor_tensor(sig[:], ps, W0, ALU.add), reads=[psr, cr], writes=[sigr])
        P.op("act", lambda h: h.activation(sig[:], sig[:], AF.Sigmoid), reads=[sigr], writes=[sigr])
        yield
        ps, psr = half.next()
        mm(ps, psr, ta[:, :], aup[:, :], [tar, cr])
        P.op("dve", lambda h, ps=ps: h.tensor_tensor(a[:], ps, A0, ALU.add), reads=[psr, cr], writes=[ar_])
        P.op("act", lambda h: h.activation(a[:], a[:], AF.Sigmoid), reads=[ar_], writes=[ar_])
        yield
        ps, psr = half.next()
        mm(ps[0:64, :], psr, tg0[:, :], gup0[:, :], [tg0r, cr], True, False)
        mm(ps[0:64, :], psr, tg1[:, :], gup1[:, :], [tg1r, cr], False, True)
        acopy(g[:], gr, ps[0:64, :], psr)
        yield
        kk, kkr = kk_b.next(); sq, sqr = sq_b.next(); ss, ssr = ss_b.next()
        P.op("dve", lambda h: h.tensor_tensor(kk[:], Kr, KK, ALU.mult), reads=[zsr, cr], writes=[kkr])
        P.op("dve", lambda h: h.tensor_tensor(sq[:], kk[:], kk[:], ALU.mult), reads=[kkr], writes=[sqr])
        P.op("dve", lambda h: h.tensor_reduce(ss[:, 0:4], sq[:, :].rearrange("p (u d) -> p u d", u=4), AX.X, ALU.add),
             reads=[sqr], writes=[ssr])
        P.op("dve", lambda h: h.tensor_scalar(ss[:, 0:4], ss[:, 0:4], 1e-24, None, ALU.add), reads=[ssr], writes=[ssr])
        P.op("act", lambda h: h.activation(ss[:, 0:4], ss[:, 0:4], AF.Ln), reads=[ssr], writes=[ssr])
        P.op("act", lambda h: h.activation(ss[:, 4:8], ss[:, 0:4], AF.Exp, scale=-0.5), reads=[ssr], writes=[ssr])
        for u in range(4):
            P.op("dve", lambda h, u=u: h.tensor_scalar(kk[:, u * 64:(u + 1) * 64], kk[:, u * 64:(u + 1) * 64],
                                                       ss[:, 4 + u:5 + u], None, ALU.mult), reads=[kkr, ssr], writes=[kkr])
        yield
        t1, t1r = t1_b.next(); bkr, bkrr = bkr_b.next()
        P.op("dve", lambda h: h.scalar_tensor_tensor(t1[:], a[:], -1.0, KA, ALU.add, ALU.mult), reads=[ar_, cr], writes=[t1r])
        P.op("dve", lambda h: h.scalar_tensor_tensor(t1[:], t1[:], 1.0, Kr, ALU.add, ALU.mult), reads=[t1r, zsr], writes=[t1r])
        P.op("dve", lambda h: h.tensor_tensor(bkr[0:64, :], kk[0:64, :], a[0:64, :], ALU.mult), reads=[kkr, ar_], writes=[bkrr])
        P.op("act", lambda h: h.copy(bkr[64:128, :], t1[64:128, :]), reads=[t1r], writes=[bkrr])
        yield
        e1, e1r = e1_b.next(); e2, e2r = e2_b.next(); e3, e3r = e3_b.next(); glb, glbr = glb_b.next()
        for (TT, e, er, sc) in ((ct["TT1"], e1, e1r, 1.0), (ct["TT2"], e2, e2r, -1.0), (ct["TT3"], e3, e3r, 1.0)):
            ps, psr = half.next()
            mm(ps, psr, TT[:, :], sig[0:64, :], [cr, sigr])
            P.op("act", lambda h, e=e, ps=ps, sc=sc: h.activation(e[:], ps, AF.Exp, scale=sc), reads=[psr], writes=[er])
            yield
        ps, psr = half.next()
        for u in range(4):
            mm(ps[0:64, u * 64:(u + 1) * 64], psr, sig[0:64, u * 64:(u + 1) * 64], ct["negones"][:, :], [sigr, cr])
        P.op("act", lambda h, ps=ps: h.activation(glb[:], ps[0:64, :], AF.Exp), reads=[psr], writes=[glbr])
        yield
        ar, arr = ar_b.next(); bk, bkr_ = bk_b.next(); bkb, bkbr = bkb_b.next()
        P.op("dve", lambda h: h.scalar_tensor_tensor(ar[0:64, :], kk[0:64, :], -1.0, e1[0:64, :], ALU.mult, ALU.mult),
             reads=[kkr, e1r], writes=[arr])
        P.op("dve", lambda h: h.tensor_tensor(ar[64:128, :], Rr[64:128, :], e1[64:128, :], ALU.mult),
             reads=[zsr, e1r], writes=[arr])
        P.op("dve", lambda h: h.tensor_tensor(bk[:], bkr[:], e2[:], ALU.mult), reads=[bkrr, e2r], writes=[bkr_])
        P.op("dve", lambda h: h.tensor_tensor(bkb[:], bkr[:], e3[:], ALU.mult), reads=[bkrr, e3r], writes=[bkbr])
        UV = []
        for u in range(4):
            uv, uvr = uv_b[u].next()
            P.op("act", lambda h, uv=uv, u=u: h.copy(uv[64:128, :], Vr[64:128, u * 64:(u + 1) * 64]),
                 reads=[zsr], writes=[uvr])
            UV.append((uv, uvr))
        prep_done.add(c)
        yield
        U = [dict() for _ in range(4)]
        for u in range(4):
            cs = slice(u * 64, (u + 1) * 64)
            d = U[u]
            d["art"], d["artr"] = art_b[u].next(); d["bkt"], d["bktr"] = bkt_b[u].next()
            ps, psr = quar.next()
            P.op("pe", lambda h, ps=ps, cs=cs: h.transpose(ps[0:64, :], ar[:, cs], ct["ident"][:, :]), reads=[arr, cr], writes=[psr])
            acopy(d["art"][:], d["artr"], ps[0:64, :], psr)
            ps, psr = quar.next()
            P.op("pe", lambda h, ps=ps, cs=cs: h.transpose(ps[0:64, :], bk[:, cs], ct["ident"][:, :]), reads=[bkr_, cr], writes=[psr])
            P.op("dve", lambda h, ps=ps, d=d: h.tensor_copy(d["bkt"][:], ps[0:64, :]), reads=[psr], writes=[d["bktr"]])
            yield
        for u in range(4):
            d = U[u]
            d["mts"], d["mtsr"] = mts_b[u].next()
            ps, psr = quar.next()
            mm(ps, psr, d["bkt"][:, :], d["art"][:, :], [d["bktr"], d["artr"]])
            P.op("dve", lambda h, ps=ps, d=d: h.tensor_tensor(d["mts"][:], ps, ct["mask128"][:, :], ALU.mult),
                 reads=[psr, cr], writes=[d["mtsr"]])
            d["p"], d["pr"] = p_b[par][u].next()
            ps, psr = quar.next()
            mm(ps[0:64, 0:64], psr, d["art"][:, 0:64], d["bkt"][:, 0:64], [d["bktr"], d["artr"]])
            P.op("dve", lambda h, ps=ps, d=d: h.tensor_tensor(d["p"][:], ps[0:64, 0:64], ct["maskN"][:, :], ALU.mult),
                 reads=[psr, cr], writes=[d["pr"]])
            yield
        for u in range(4):
            cs = slice(u * 64, (u + 1) * 64)
            d = U[u]
            uv, uvr = UV[u]
            d["z"], d["zr"] = z_b[par][u].next()
            ps, psr = quar.next()
            mm(ps[0:64, 0:64], psr, d["mts"][64:128, 0:64], uv[64:128, :], [d["mtsr"], uvr])
            acopy(d["z"][:, 64:128], d["zr"], ps[0:64, 0:64], psr)
            P.op("act", lambda h, d=d, cs=cs: h.copy(d["z"][:, 0:64], ar[0:64, cs]), reads=[arr], writes=[d["zr"]])
            q0, q0r = q_b[par][u].next()
            P.op("dve", lambda h, q0=q0, d=d: h.tensor_copy(q0[:], d["mts"][0:64, 0:64]), reads=[d["mtsr"]], writes=[q0r])
            d["q"], d["qr"] = q0[:, :], q0r
            yield
        for lvl in range(6):
            for u in range(4):
                d = U[u]
                uv, uvr = UV[u]
                ps, psr = quar.next()
                mm(ps[0:64, :], psr, d["q"], d["z"][:, :], [d["qr"], d["zr"]])
                if lvl < 5:
                    zn, znr = z_b[par][u].next()
                    P.op("dve", lambda h, ps=ps, d=d, zn=zn: h.tensor_tensor(zn[:], ps[0:64, :], d["z"][:, :], ALU.add),
                         reads=[psr, d["zr"]], writes=[znr])
                    pn, pnr = p_b[par][u].next(); qn, qnr = q_b[par][u].next()
                    ps1, ps1r = quar.next()
                    mm(ps1[0:64, 0:64], ps1r, d["q"], d["p"][:, :], [d["qr"], d["pr"]])
                    acopy(pn[:], pnr, ps1[0:64, 0:64], ps1r)
                    ps2, ps2r = quar.next()
                    mm(ps2[0:64, 0:64], ps2r, d["p"][:, :], d["q"], [d["qr"], d["pr"]])
                    acopy(qn[:], qnr, ps2[0:64, 0:64], ps2r)
                    d["z"], d["zr"] = zn, znr
                    d["p"], d["pr"] = pn, pnr
                    d["q"], d["qr"] = qn[:, :], qnr
                else:
                    d["ap"], d["apr"] = ap_b[u].next()
                    P.op("dve", lambda h, ps=ps, d=d: h.tensor_tensor(d["ap"][:], ps[0:64, 0:64], d["z"][:, 0:64], ALU.add),
                         reads=[psr, d["zr"]], writes=[d["apr"]])
                    P.op("dve", lambda h, ps=ps, d=d, uv=uv: h.tensor_tensor(uv[0:64, :], ps[0:64, 64:128], d["z"][:, 64:128],
                                                                              ALU.add), reads=[psr, d["zr"]], writes=[uvr])
                yield
        pso, psor = pso_reg[par]
        for u in range(4):
            cs = slice(u * 64, (u + 1) * 64)
            d = U[u]
            uv, uvr = UV[u]
            phit, phitr = phit_b[u].next(); dgl, dglr = dgl_b[u].next()
            P.op("dve", lambda h, dgl=dgl, cs=cs: h.tensor_tensor(dgl[:], ct["ident"][0:64, 0:64], glb[:, cs], ALU.mult),
                 reads=[cr, glbr], writes=[dglr])
            ps, psr = quar.next()
            mm(ps[0:64, 0:64], psr, d["ap"][:, :], bkb[0:64, cs], [d["apr"], bkbr])
            P.op("dve", lambda h, ps=ps, phit=phit, dgl=dgl: h.tensor_tensor(phit[:], ps[0:64, 0:64], dgl[:], ALU.add),
                 reads=[psr, dglr], writes=[phitr])
            psi, psir = psi_b[u].next()
            ps, psr = quar.next()
            mm(ps[0:64, 0:64], psr, bkb[:, cs], uv[:, :], [bkbr, uvr])
            acopy(psi[:], psir, ps[0:64, 0:64], psr)
            rpt, rptr = rpt_b[u].next()
            ps, psr = quar.next()
            mm(ps[0:64, 0:64], psr, d["ap"][:, :], d["mts"][0:64, 64:128], [d["apr"], d["mtsr"]])
            P.op("dve", lambda h, ps=ps, rpt=rpt, d=d: h.tensor_tensor(rpt[:], ps[0:64, 0:64], d["art"][:, 64:128], ALU.add),
                 reads=[psr, d["artr"]], writes=[rptr])
            d["phit"], d["phitr"], d["psi"], d["psir"], d["rpt"], d["rptr"] = phit, phitr, psi, psir, rpt, rptr
            yield
        while chain_done[0] < c:
            yield
        for u in range(4):
            cs = slice(u * 64, (u + 1) * 64)
            d = U[u]
            uv, uvr = UV[u]
            phit, phitr, psi, psir, rpt, rptr = d["phit"], d["phitr"], d["psi"], d["psir"], d["rpt"], d["rptr"]
            ht, hr = H[u]
            mm(pso[0:64, cs], psor, rpt[:, :], ht[:, :], [rptr, hr], True, False)
            mm(pso[0:64, cs], psor, d["mts"][:, 64:128], uv[:, :], [d["mtsr"], uvr], False, True)
            ps, psr = quar.next()
            mm(ps[0:64, 0:64], psr, phit[:, :], ht[:, :], [phitr, hr])
            hn, hnr = h_b[u].next()
            P.op("dve", lambda h, ps=ps, hn=hn, psi=psi: h.tensor_tensor(hn[:], ps[0:64, 0:64], psi[:], ALU.add),
                 reads=[psr, psir], writes=[hnr])
            H[u] = (hn, hnr)
        chain_done[0] = c + 1
        yield
        ysq, ysqr = ysq_b.next(); st, str_ = st_b.next(); yn, ynr = yn_b.next(); rk, rkr = rk_b.next(); ob, obr = ob_b.next()
        y3 = pso[0:64, :].rearrange("p (u d) -> p u d", u=4)
        P.op("act", lambda h: h.activation(ysq[:], pso[0:64, :], AF.Square), reads=[psor], writes=[ysqr])
        P.op("dve", lambda h: h.tensor_reduce(st[:, 0:4], y3, AX.X, ALU.add), reads=[psor], writes=[str_])
        P.op("dve", lambda h: h.tensor_reduce(st[:, 4:8], ysq[:, :].rearrange("p (u d) -> p u d", u=4), AX.X, ALU.add),
             reads=[ysqr], writes=[str_])
        P.op("dve", lambda h: h.tensor_scalar(st[:, 0:8], st[:, 0:8], 1.0 / 64, None, ALU.mult), reads=[str_], writes=[str_])
        P.op("dve", lambda h: h.tensor_tensor(st[:, 8:12], st[:, 0:4], st[:, 0:4], ALU.mult), reads=[str_], writes=[str_])
        P.op("dve", lambda h: h.tensor_tensor(st[:, 8:12], st[:, 4:8], st[:, 8:12], ALU.subtract), reads=[str_], writes=[str_])
        P.op("dve", lambda h: h.tensor_scalar(st[:, 8:12], st[:, 8:12], 64e-5, None, ALU.add), reads=[str_], writes=[str_])
        P.op("act", lambda h: h.activation(st[:, 8:12], st[:, 8:12], AF.Ln), reads=[str_], writes=[str_])
        P.op("act", lambda h: h.activation(st[:, 8:12], st[:, 8:12], AF.Exp, scale=-0.5), reads=[str_], writes=[str_])
        yield
        for u in range(4):
            cs = slice(u * 64, (u + 1) * 64)
            P.op("dve", lambda h, u=u, cs=cs: h.tensor_scalar(yn[:, cs], pso[0:64, cs], st[:, u:u + 1], st[:, 8 + u:9 + u],
                                                             ALU.subtract, ALU.mult), reads=[psor, str_], writes=[ynr])
        P.op("dve", lambda h: h.tensor_tensor(yn[:], yn[:], LNW[0:64, :], ALU.mult), reads=[ynr, cr], writes=[ynr])
        P.op("dve", lambda h: h.tensor_tensor(yn[:], yn[:], LNB[0:64, :], ALU.add), reads=[ynr, cr], writes=[ynr])
        yield
        P.op("dve", lambda h: h.tensor_tensor(rk[:], Rr[0:64, :], t1[0:64, :], ALU.mult), reads=[zsr, t1r], writes=[rkr])
        P.op("dve", lambda h: h.tensor_tensor(rk[:], rk[:], RK[0:64, :], ALU.mult), reads=[rkr, cr], writes=[rkr])
        P.op("dve", lambda h: h.tensor_reduce(st[:, 12:16], rk[:, :].rearrange("p (u d) -> p u d", u=4), AX.X, ALU.add),
             reads=[rkr], writes=[str_])
        for u in range(4):
            cs = slice(u * 64, (u + 1) * 64)
            P.op("dve", lambda h, u=u, cs=cs: h.scalar_tensor_tensor(yn[:, cs], Vr[0:64, cs], st[:, 12 + u:13 + u], yn[:, cs],
                                                                    ALU.mult, ALU.add), reads=[zsr, str_, ynr], writes=[ynr])
        P.op("dve", lambda h: h.tensor_tensor(ob[:], yn[:], g[:], ALU.mult), reads=[ynr, gr], writes=[obr])
        P.dma("sp", out[t0:t0 + 64, :], ob[:], reads=[obr])

    active = []
    next_c = 0
    while active or next_c < NCH:
        if next_c < NCH and len(active) < 2 and (next_c == 0 or (next_c - 1) in prep_done):
            active.append([next_c, chunk_gen(next_c, next_c % 2)])
            next_c += 1
        for item in list(active):
            try:
                next(item[1])
            except StopIteration:
                active.remove(item)

SB3 = 384


def ssd_consts():
    k = np.arange(128)[:, None]
    t = np.arange(128)[None, :]
    c = {"tri": (k <= t).astype(np.float32), "ones": np.ones((128, 128), np.float32),
         "maskneg": np.where(k <= t, 0.0, -30000.0).astype(np.float32), "ident": np.eye(128, dtype=np.float32)}
    return c


def ssd_stage(P, *, PN, xbcT, convw, convb, z, dtT, dtb, alog, dskip, normw, consts, out):
    NSB = PN // SB3
    assert PN % SB3 == 0
    sb = lambda shape, name, dt=F32: P.sbuf(shape, dt, "s_" + name)
    cr = Res()
    ct = {}
    for nm in ("tri", "ones", "maskneg", "ident"):
        ct[nm] = sb([128, 128], nm)
        P.dma("sp", ct[nm][:], consts[nm], writes=[cr])
    cw = sb([128, 48], "cw"); cb = sb([128, 12], "cb")
    P.dma("sp", cw[:], convw, writes=[cr]); P.dma("sp", cb[:], convb, writes=[cr])
    dtb_t = sb([16, 1], "dtb"); acol = sb([16, 1], "acol")
    P.dma("sp", dtb_t[:], dtb, writes=[cr]); P.dma("sp", acol[:], alog, writes=[cr])
    P.op("act", lambda h: h.activation(acol[:], acol[:], AF.Exp), reads=[cr], writes=[cr])
    P.op("dve", lambda h: h.tensor_scalar(acol[:], acol[:], -1.0, None, ALU.mult), reads=[cr], writes=[cr])
    dsk = sb([128, 1024], "dsk"); nw = sb([128, 1024], "nw")
    P.dma("sp", dsk[:], dskip[0:1, :].partition_broadcast(128), writes=[cr])
    P.dma("sp", nw[:], normw[0:1, :].partition_broadcast(128), writes=[cr])

    def bufs(shape, name, n=2, dt=F32):
        return Rot([(sb(shape, "%s%d" % (name, i), dt), Res()) for i in range(n)])
    xin_b = bufs([128, 12 * (SB3 + 3)], "xin")
    cv_b = bufs([128, 12 * SB3], "cv")
    ctmp_b = bufs([128, SB3], "ctmp", 3)
    dtin_b = bufs([16, SB3], "dtin"); dtf_b = bufs([16, 2 * SB3], "dtf")
    xtok_b = Rot([(sb([128, 1024], "xtok%d" % i), [Res() for _ in range(8)]) for i in range(2)]); btok_b = bufs([128, 256], "btok", 2, BF16)
    bct_b = bufs([128, 4 * 128], "bct", 2, BF16)
    dta_b = bufs([128, 64], "dta")
    te_b = bufs([128, 48], "te")
    tmpi_b = bufs([128, 512], "tmpi")
    NSL = 3
    abc_s = [(sb([128, 128], "abc%d" % i), Res()) for i in range(NSL)]
    seg_s = [(sb([128, 128], "seg%d" % i), Res()) for i in range(NSL)]
    wj_s = [(sb([128, 128], "wj%d" % i, BF16), Res()) for i in range(NSL)]
    ebc_s = [(sb([128, 128], "ebc%d" % i), Res()) for i in range(NSL)]
    ctj_s = [(sb([128, 128], "ctj%d" % i, BF16), Res()) for i in range(NSL)]
    cbt_b = bufs([128, 256], "cbt")
    xdt_b = Rot([(sb([128, 1024], "xdt%d" % i, BF16), [Res() for _ in range(16)]) for i in range(2)])
    xs_b = Rot([(sb([128, 1024], "xs%d" % i, BF16), [Res() for _ in range(16)]) for i in range(2)])
    z_b = bufs([128, 1024], "z"); yy_b = bufs([128, 1024], "yy"); sq_b = bufs([128, 1024], "sq", 1)
    st_b = bufs([128, 8], "st")
    state = sb([128, 1024], "state"); state_r = [Res() for _ in range(16)]
    stbf = sb([128, 1024], "statebf", BF16); stbf_r = [Res(), Res()]
    P.op("pool", lambda h: h.memset(state[:], 0.0), writes=state_r)
    P.op("pool", lambda h: h.memset(stbf[:], 0.0), writes=stbf_r)
    pb = [P.psum([128, 512], F32, "s_ps%d" % i) for i in range(8)]
    ybank = [(pb[0], Res(excl=True)), (pb[1], Res(excl=True))]
    cbanks = [(pb[2 + i], Res(excl=True)) for i in range(3)]
    mbank = Rot([(pb[5], Res(excl=True))])
    sbank = Rot([(pb[6], Res(excl=True)), (pb[7], Res(excl=True))])

    def mm(ps, psr, lhsT, rhs, reads, start=True, stop=True):
        P.op("pe", lambda h: h.matmul(ps, lhsT, rhs, start=start, stop=stop), reads=reads, writes=[psr])

    for sbi in range(NSB):
        t0 = sbi * SB3
        xin, xinr = xin_b.next(); cv, cvr = cv_b.next()
        xin3 = xin[:, :].rearrange("p (k t) -> p k t", k=12)
        cv3 = cv[:, :].rearrange("p (k t) -> p k t", k=12)
        for hf in range(2):
            P.dma("sp" if hf == 0 else "pool", xin3[:, hf * 6:(hf + 1) * 6, :],
                  xbcT[hf * 768:(hf + 1) * 768, t0:t0 + SB3 + 3].rearrange("(k p) t -> p k t", p=128), writes=[xinr])
        for kc in range(12):
            tm, tmr = ctmp_b.next()
            P.op("dve", lambda h: h.tensor_scalar(tm[:], xin3[:, kc, 3:SB3 + 3], cw[:, kc * 4 + 3:kc * 4 + 4], cb[:, kc:kc + 1],
                                                  ALU.mult, ALU.add), reads=[xinr, cr], writes=[tmr])
            for j in (2, 1, 0):
                P.op("dve", lambda h: h.scalar_tensor_tensor(tm[:], xin3[:, kc, j:SB3 + j], cw[:, kc * 4 + j:kc * 4 + j + 1], tm[:],
                                                             ALU.mult, ALU.add), reads=[xinr, cr, tmr], writes=[tmr])
            P.op("act", lambda h: h.activation(cv3[:, kc, :], tm[:], AF.Silu), reads=[tmr], writes=[cvr])
        if sbi == 0:
            P.op("pool", lambda h: h.memset(cv3[:, :, 0:112], 0.0), writes=[cvr])
        dtin, dtinr = dtin_b.next(); dtf, dtfr = dtf_b.next()
        P.dma("sp", dtin[:], dtT[:, t0:t0 + SB3], writes=[dtinr])
        P.op("act", lambda h: h.activation(dtf[:, 0:SB3], dtin[:], AF.Exp, bias=dtb_t[:, 0:1]), reads=[dtinr, cr], writes=[dtfr])
        P.op("act", lambda h: h.activation(dtf[:, 0:SB3], dtf[:, 0:SB3], AF.Ln, bias=1.0), reads=[dtfr], writes=[dtfr])
        if sbi == 0:
            P.op("pool", lambda h: h.memset(dtf[:, 0:112], 0.0), writes=[dtfr])
        P.op("dve", lambda h: h.tensor_scalar(dtf[:, SB3:2 * SB3], dtf[:, 0:SB3], acol[:, 0:1], None, ALU.mult),
             reads=[dtfr, cr], writes=[dtfr])
        for ci in range(3):
            c0 = ci * 128
            tc0 = t0 + c0
            xtok, xtokr = xtok_b.next(); btok, btokr = btok_b.next(); bct, bctr = bct_b.next(); dta, dtar = dta_b.next()
            for kc in range(8):
                ps, psr = mbank.next()
                P.op("pe", lambda h: h.transpose(ps[:, 0:128], cv3[:, kc, c0:c0 + 128], ct["ident"][:, :]), reads=[cvr, cr], writes=[psr])
                if kc % 2 == 0:
                    P.op("act", lambda h: h.copy(xtok[:, kc * 128:(kc + 1) * 128], ps[:, 0:128]), reads=[psr], writes=[xtokr[kc]])
                else:
                    P.op("dve", lambda h: h.tensor_copy(xtok[:, kc * 128:(kc + 1) * 128], ps[:, 0:128]), reads=[psr], writes=[xtokr[kc]])
            for g in range(2):
                ps, psr = mbank.next()
                P.op("pe", lambda h: h.transpose(ps[:, 0:128], cv3[:, 8 + g, c0:c0 + 128], ct["ident"][:, :]), reads=[cvr, cr], writes=[psr])
                P.op("act", lambda h: h.copy(btok[:, g * 128:(g + 1) * 128], ps[:, 0:128]), reads=[psr], writes=[btokr])
            P.op("act", lambda h: h.copy(bct[:, :].rearrange("p (k t) -> p k t", k=4), cv3[:, 8:12, c0:c0 + 128]),
                 reads=[cvr], writes=[bctr])
            ps, psr = mbank.next()
            for q in range(2):
                P.op("pe", lambda h: h.transpose(ps[:, q * 16:(q + 1) * 16], dtf[:, q * SB3 + c0:q * SB3 + c0 + 128], ct["ident"][0:16, 0:16]),
                     reads=[dtfr, cr], writes=[psr])
            P.op("dve", lambda h: h.tensor_copy(dta[:, 0:32], ps[:, 0:32]), reads=[psr], writes=[dtar])
            te, ter = te_b.next()
            ps, psr = mbank.next()
            mm(ps[:, 0:16], psr, ct["tri"][:, :], dta[:, 16:32], [cr, dtar])
            mm(ps[:, 16:32], psr, ct["ones"][:, :], dta[:, 16:32], [cr, dtar])
            P.op("dve", lambda h: h.tensor_copy(dta[:, 32:48], ps[:, 0:16]), reads=[psr], writes=[dtar])
            P.op("dve", lambda h: h.tensor_tensor(te[:, 0:16], ps[:, 16:32], dta[:, 32:48], ALU.subtract), reads=[psr, dtar], writes=[ter])
            P.op("act", lambda h: h.activation(te[:, 16:32], ps[:, 16:32], AF.Exp), reads=[psr], writes=[ter])
            P.op("act", lambda h: h.activation(te[:, 0:16], te[:, 0:16], AF.Exp), reads=[ter], writes=[ter])
            P.op("dve", lambda h: h.tensor_tensor(dta[:, 48:64], dta[:, 0:16], te[:, 0:16], ALU.mult), reads=[dtar, ter], writes=[dtar])
            P.op("act", lambda h: h.activation(te[:, 32:48], dta[:, 32:48], AF.Exp), reads=[dtar], writes=[ter])
            xdt, xdtr = xdt_b.next(); xs, xsr = xs_b.next()
            x3 = xtok[:, :].rearrange("p (j d) -> p j d", j=16)
            P.op("dve", lambda h: h.tensor_tensor(xdt[:, :].rearrange("p (j d) -> p j d", j=16), x3,
                                                  dta[:, 0:16].unsqueeze(2).to_broadcast([128, 16, 64]), ALU.mult),
                 reads=xtokr + [dtar], writes=xdtr)
            P.op("dve", lambda h: h.tensor_tensor(xs[:, :].rearrange("p (j d) -> p j d", j=16), x3,
                                                  dta[:, 48:64].unsqueeze(2).to_broadcast([128, 16, 64]), ALU.mult),
                 reads=xtokr + [dtar], writes=xsr)
            cbt, cbtr = cbt_b.next()
            for g in range(2):
                ps, psr = mbank.next()
                mm(ps[:, 0:128], psr, bct[:, g * 128:(g + 1) * 128], bct[:, (2 + g) * 128:(3 + g) * 128], [bctr])
                P.op("act", lambda h: h.copy(cbt[:, g * 128:(g + 1) * 128], ps[:, 0:128]), reads=[psr], writes=[cbtr])
            inter = []
            for g in range(2):
                ps, psr = sbank.next()
                mm(ps[:, :], psr, bct[:, (2 + g) * 128:(3 + g) * 128], stbf[:, g * 512:(g + 1) * 512], [bctr, stbf_r[g]])
                tmpi, tmpir = tmpi_b.next()
                P.op("dve", lambda h: h.tensor_tensor(tmpi[:, :].rearrange("p (j d) -> p j d", j=8),
                                                      ps[:, :].rearrange("p (j d) -> p j d", j=8),
                                                      te[:, 32 + g * 8:40 + g * 8].unsqueeze(2).to_broadcast([128, 8, 64]), ALU.mult),
                     reads=[psr, ter], writes=[tmpir])
                inter.append((tmpi, tmpir))
            def head_gen(j, slot):
                g = j // 8
                yb, ybr = ybank[g]
                seg, segr = seg_s[slot]; wj, wjr = wj_s[slot]
                ps, psr = cbanks[slot]
                mm(ps[:, 0:128], psr, dta[:, 16 + j:17 + j].to_broadcast([128, 128]), ct["tri"][:, :], [dtar, cr])
                yield
                P.op("dve", lambda h: h.scalar_tensor_tensor(seg[:], ps[:, 0:128], dta[:, 32 + j:33 + j], ct["maskneg"][:, :],
                                                             ALU.subtract, ALU.add), reads=[psr, dtar, cr], writes=[segr])
                yield
                P.op("act", lambda h: h.activation(seg[:], seg[:], AF.Exp), reads=[segr], writes=[segr])
                yield
                P.op("dve", lambda h: h.tensor_tensor(wj[:], seg[:], cbt[:, g * 128:(g + 1) * 128], ALU.mult), reads=[segr, cbtr], writes=[wjr])
                yield
                jj = j % 8
                mm(yb[:, jj * 64:(jj + 1) * 64], ybr, wj[:, :], xdt[:, j * 64:(j + 1) * 64], [wjr, xdtr[j]], True, True)

            pending = list(range(16))
            active = {}
            free_slots = list(range(NSL))
            while pending or active:
                if pending and free_slots:
                    slot = free_slots.pop(0)
                    active[slot] = head_gen(pending.pop(0), slot)
                for slot in sorted(active):
                    try:
                        next(active[slot])
                    except StopIteration:
                        del active[slot]
                        free_slots.append(slot)
            for g in range(2):
                ps, psr = sbank.next()
                mm(ps[:, :], psr, btok[:, g * 128:(g + 1) * 128], xs[:, g * 512:(g + 1) * 512], [btokr] + xsr[g * 8:(g + 1) * 8])
                for jj in range(8):
                    j = g * 8 + jj
                    P.op("dve", lambda h: h.scalar_tensor_tensor(state[:, j * 64:(j + 1) * 64], state[:, j * 64:(j + 1) * 64],
                                                                 te[:, 16 + j:17 + j], ps[:, jj * 64:(jj + 1) * 64], ALU.mult, ALU.add),
                         reads=[psr, ter, state_r[j]], writes=[state_r[j]])
                P.op("act", lambda h: h.copy(stbf[:, g * 512:(g + 1) * 512], state[:, g * 512:(g + 1) * 512]),
                     reads=state_r[g * 8:(g + 1) * 8], writes=[stbf_r[g]])
            zt, ztr = z_b.next(); yy, yyr = yy_b.next(); sq, sqr = sq_b.next(); st, str_ = st_b.next()
            P.dma("sp", zt[:], z[tc0:tc0 + 128, :], writes=[ztr])
            P.op("dve", lambda h: h.tensor_tensor(yy[:], xtok[:], dsk[:], ALU.mult), reads=xtokr + [cr], writes=[yyr])
            for g in range(2):
                yb, ybr = ybank[g]
                tmpi, tmpir = inter[g]
                P.op("dve", lambda h: h.tensor_tensor(yy[:, g * 512:(g + 1) * 512], yy[:, g * 512:(g + 1) * 512], tmpi[:, :], ALU.add),
                     reads=[tmpir, yyr], writes=[yyr])
                P.op("dve", lambda h: h.tensor_tensor(yy[:, g * 512:(g + 1) * 512], yy[:, g * 512:(g + 1) * 512], yb[:, :], ALU.add),
                     reads=[ybr, yyr], writes=[yyr])
            P.op("act", lambda h: h.activation(zt[:], zt[:], AF.Silu), reads=[ztr], writes=[ztr])
            P.op("dve", lambda h: h.tensor_tensor(yy[:], yy[:], zt[:], ALU.mult), reads=[yyr, ztr], writes=[yyr])
            P.op("act", lambda h: h.activation(sq[:], yy[:], AF.Square), reads=[yyr], writes=[sqr])
            P.op("dve", lambda h: h.tensor_reduce(st[:, 0:2], sq[:, :].rearrange("p (g c) -> p g c", g=2), AX.X, ALU.add),
                 reads=[sqr], writes=[str_])
            P.op("dve", lambda h: h.tensor_scalar(st[:, 0:2], st[:, 0:2], 1.0 / 512, 1e-5, ALU.mult, ALU.add), reads=[str_], writes=[str_])
            P.op("act", lambda h: h.activation(st[:, 0:2], st[:, 0:2], AF.Ln), reads=[str_], writes=[str_])
            P.op("act", lambda h: h.activation(st[:, 2:4], st[:, 0:2], AF.Exp, scale=-0.5), reads=[str_], writes=[str_])
            for g in range(2):
                P.op("dve", lambda h: h.scalar_tensor_tensor(yy[:, g * 512:(g + 1) * 512], yy[:, g * 512:(g + 1) * 512],
                                                             st[:, 2 + g:3 + g], nw[:, g * 512:(g + 1) * 512], ALU.mult, ALU.mult),
                     reads=[yyr, str_, cr], writes=[yyr])
            P.dma("sp", out[tc0:tc0 + 128, :], yy[:], reads=[yyr])

D_MODEL = 2048
FFN = 5632
_PROGS = {}


def _add_barrier(P):
    toks = []
    for q, sls in P.slots.items():
        for sl in sls:
            if sl.uses > 0:
                toks.append((sl.key, sl.sem, 16 * sl.uses))
    for q in ("sp", "pool", "act"):
        for t in toks:
            P._wait(P.E[q], t)


def _build_tp(N, Kmix, Cnext, first):
    key = ("tp", N, Kmix, Cnext, first)
    if key in _PROGS:
        return _PROGS[key]
    P = Prog()
    d = lambda n, s: P.dram(n, s, F32, "ExternalInput")
    L = LinCtx(P)
    resT = d("resT", [D_MODEL, N])
    mask = d("mask", [1, N])
    if not first:
        mixT = d("mixT", [Kmix, N])
        Wout = d("Wout", [16, 128, Kmix])
        gam_f = d("gam_f", [128, 16])
        Wup = d("Wup", [88, 128, D_MODEL])
        cw = d("cw", [128, 44 * 3])
        cb = d("cb", [128, 44])
        Wdown = d("Wdown", [16, 128, FFN])
        res1 = P.dram("res1", [D_MODEL, N], F32, "Internal")
        up = P.dram("up", [2 * FFN, N], F32, "Internal")
        res2 = P.dram("res2", [D_MODEL, N], F32, "ExternalOutput")
        linear_stage(L, mode="plain", K=Kmix, C=D_MODEL, N=N, W=Wout, dst=res1, src=mixT, res=resT)
        _add_barrier(P)
        linear_stage(L, mode="norm", K=D_MODEL, C=2 * FFN, N=N, W=Wup, dst=up, src=res1, gamma=gam_f, mask=mask)
        _add_barrier(P)
        linear_stage(L, mode="convglu", K=FFN, C=D_MODEL, N=N, W=Wdown, dst=res2, src=up[0:FFN, :], src2=up[FFN:2 * FFN, :],
                     convw=cw, convb=cb, res=res1)
        src_next = res2
    else:
        src_next = resT
    if Cnext:
        gam_n = d("gam_n", [128, 16])
        CBn = (Cnext + 127) // 128
        Win = d("Win", [CBn, 128, D_MODEL])
        znext = P.dram("znext", [Cnext, N], F32, "ExternalOutput")
        if not first:
            _add_barrier(P)
        linear_stage(L, mode="norm", K=D_MODEL, C=Cnext, N=N, W=Win, dst=znext, src=src_next, gamma=gam_n, mask=mask)
    nc = P.finalize()
    _PROGS[key] = nc
    return nc


def _build_attn(PN):
    key = ("attn", PN)
    if key in _PROGS:
        return _PROGS[key]
    P = Prog()
    d = lambda n, s: P.dram(n, s, F32, "ExternalInput")
    qT = d("qT", [64, 4, PN]); kT = d("kT", [64, PN]); vv = d("v", [PN, 64])
    qwd = d("qw", [64, 1]); kwd = d("kw", [64, 1]); sk = d("sinks", [1, 4]); mk = d("masks", [4, 128, 512])
    out = P.dram("out", [PN, 256], F32, "ExternalOutput")
    attn_stage(P, PN=PN, qT=qT, kT=kT, v=vv, qw=qwd, kw=kwd, sinks=sk, masks=mk, out=out)
    nc = P.finalize()
    _PROGS[key] = nc
    return nc


def _build_rwkv(PN):
    key = ("rwkv", PN)
    if key in _PROGS:
        return _PROGS[key]
    P = Prog()
    d = lambda n, s: P.dram(n, s, F32, "ExternalInput")
    C = rwkv_consts()
    rkvp = d("rkvp", [PN + 1, 768]); xwT = d("xwT", [64, PN + 1]); xaT = d("xaT", [64, PN + 1]); xgT = d("xgT", [160, PN + 1])
    mu_rkv = d("mu_rkv", [1, 768]); mu_w = d("mu_w", [64, 1]); mu_a = d("mu_a", [64, 1]); mu_g = d("mu_g", [160, 1])
    rows = d("rows", [7, 256]); wup = d("w_up", [64, 256]); aup = d("a_up", [64, 256]); gup = d("g_up", [160, 256])
    consts = {k: d("c_" + k, list(v.shape)) for k, v in C.items()}
    out = P.dram("out", [PN, 256], F32, "ExternalOutput")
    rwkv_stage(P, PN=PN, rkvp=rkvp, xwT=xwT, xaT=xaT, xgT=xgT, mu_rkv=mu_rkv, mu_w=mu_w, mu_a=mu_a, mu_g=mu_g, rows=rows,
               w_up=wup, a_up=aup, g_up=gup, consts=consts, out=out)
    nc = P.finalize()
    _PROGS[key] = nc
    return nc


def _build_ssd(PN):
    key = ("ssd", PN)
    if key in _PROGS:
        return _PROGS[key]
    P = Prog()
    d = lambda n, s: P.dram(n, s, F32, "ExternalInput")
    C = ssd_consts()
    xbcT = d("xbcT", [1536, PN + 3]); convw = d("convw", [128, 48]); convb = d("convb", [128, 12]); z = d("z", [PN, 1024])
    dtT = d("dtT", [16, PN]); dtb = d("dtb", [16, 1]); alog = d("alog", [16, 1]); dskip = d("dskip", [1, 1024])
    normw = d("normw", [1, 1024])
    consts = {k: d("c_" + k, [128, 128]) for k in C}
    out = P.dram("out", [PN, 1024], F32, "ExternalOutput")
    ssd_stage(P, PN=PN, xbcT=xbcT, convw=convw, convb=convb, z=z, dtT=dtT, dtb=dtb, alog=alog, dskip=dskip, normw=normw,
              consts=consts, out=out)
    nc = P.finalize()
    _PROGS[key] = nc
    return nc


def _c(a):
    return np.ascontiguousarray(a, dtype=np.float32)


def _gam(g):
    return _c(g.reshape(16, 128).T)


def _launch(nc, maps):
    res = run_bass_kernel_spmd(nc, maps, core_ids=list(range(8)))
    return res.results


def kernel(x, meta_tokens, mix_norm_w, ffn_norm_w, ar_w_in, ar_shift_mu, attn_q_norm_w, attn_k_norm_w, attn_sinks,
           rwkv_w0, rwkv_w_up, rwkv_a0, rwkv_a_up, rwkv_g_up, rwkv_k_k, rwkv_k_a, rwkv_r_k, rwkv_ln_w, rwkv_ln_b,
           ar_w_out, ssd_w_in, ssd_conv_w, ssd_conv_b, ssd_dt_bias, ssd_a_log, ssd_d, ssd_norm_w, ssd_w_out,
           ffn_w_up, ffn_conv_w, ffn_conv_b, ffn_w_down):
    f = lambda a: np.asarray(a, dtype=np.float32)
    x = f(x)
    B, SEQ, D = x.shape
    depth = mix_norm_w.shape[0]
    PN = SEQ + 128
    TOT = B * PN
    stride = TOT // 8
    assert stride * 8 == TOT
    N = -(-(stride + 2) // 704) * 704
    H = N - stride

    def windows_T(glob):
        Cc = glob.shape[1]
        outs = []
        for c in range(8):
            lo = c * stride - H
            w = np.zeros((Cc, N), np.float32)
            s0 = max(lo, 0)
            w[:, s0 - lo:] = glob[s0:lo + N].T
            outs.append(w)
        return outs

    def unwindow(outs, name):
        Cc = outs[0][name].shape[0]
        glob = np.empty((TOT, Cc), np.float32)
        for c in range(8):
            glob[c * stride:(c + 1) * stride] = outs[c][name][:, H:].T
        return glob

    res_glob = np.zeros((TOT, D), np.float32)
    valid = np.zeros((TOT, 1), np.float32)
    for b in range(B):
        res_glob[b * PN + 112:b * PN + 128] = f(meta_tokens)
        res_glob[b * PN + 128:(b + 1) * PN] = x[b]
        valid[b * PN + 112:(b + 1) * PN] = 1.0
    mask_w = [_c(w) for w in windows_T(valid)]

    def in_proj_weights(layer):
        i = layer // 2
        if layer % 2 == 0:
            return tile_w(f(ar_w_in[i])), ar_w_in.shape[2]
        return tile_w(f(ssd_w_in[i])), ssd_w_in.shape[2]

    Win, Cn = in_proj_weights(0)
    nc = _build_tp(N, 0, Cn, True)
    res_w = windows_T(res_glob)
    gam_n = _gam(f(mix_norm_w[0]))
    outs = _launch(nc, [{"resT": res_w[c], "mask": mask_w[c], "gam_n": gam_n, "Win": Win} for c in range(8)])
    z_glob = unwindow(outs, "znext")
    del Win

    amasks = attn_masks()
    rconsts = rwkv_consts()
    sconsts = ssd_consts()
    for layer in range(depth):
        i = layer // 2
        if layer % 2 == 0:
            nc = _build_attn(PN)
            maps = []
            for c in range(8):
                b, g = c // 4, c % 4
                zb = z_glob[b * PN:(b + 1) * PN]
                maps.append({"qT": _c(zb[:, g * 256:(g + 1) * 256].reshape(PN, 4, 64).transpose(2, 1, 0)),
                             "kT": _c(zb[:, 1024 + g * 64:1024 + (g + 1) * 64].T),
                             "v": _c(zb[:, 1280 + g * 64:1280 + (g + 1) * 64]),
                             "qw": _c(f(attn_q_norm_w[i])[:, None]), "kw": _c(f(attn_k_norm_w[i])[:, None]),
                             "sinks": _c(f(attn_sinks[i])[None, 4 * g:4 * g + 4]), "masks": amasks})
            outs = _launch(nc, maps)
            mix_glob = np.empty((TOT, 2048), np.float32)
            for c in range(8):
                b, g = c // 4, c % 4
                mix_glob[b * PN:(b + 1) * PN, g * 256:(g + 1) * 256] = outs[c]["out"]
            nc = _build_rwkv(PN)
            mu = f(ar_shift_mu[i])
            w0, a0, k_k, k_a = f(rwkv_w0[i]), f(rwkv_a0[i]), f(rwkv_k_k[i]), f(rwkv_k_a[i])
            r_k, ln_w, ln_b = f(rwkv_r_k[i]).reshape(-1), f(rwkv_ln_w[i]), f(rwkv_ln_b[i])
            w_up, a_up, g_up = f(rwkv_w_up[i]), f(rwkv_a_up[i]), f(rwkv_g_up[i])
            maps = []
            for c in range(8):
                b, u = c // 4, c % 4
                z0 = z_glob[b * PN:(b + 1) * PN, 1536:]
                cs = slice(256 * u, 256 * u + 256)
                rkv = np.concatenate([z0[:, 0:1024][:, cs], z0[:, 1024:2048][:, cs], z0[:, 2048:3072][:, cs]], 1)
                m = {"rkvp": np.concatenate([np.zeros((1, 768), np.float32), rkv], 0),
                     "xwT": np.concatenate([np.zeros((64, 1), np.float32), z0[:, 3072:3136].T], 1),
                     "xaT": np.concatenate([np.zeros((64, 1), np.float32), z0[:, 3136:3200].T], 1),
                     "xgT": np.concatenate([np.zeros((160, 1), np.float32), z0[:, 3200:3360].T], 1),
                     "mu_rkv": np.concatenate([mu[0:1024][cs], mu[1024:2048][cs], mu[2048:3072][cs]])[None, :],
                     "mu_w": mu[3072:3136, None], "mu_a": mu[3136:3200, None], "mu_g": mu[3200:3360, None],
                     "rows": np.stack([w0[cs], a0[cs], k_k[cs], k_a[cs], r_k[cs], ln_w[cs], ln_b[cs]]),
                     "w_up": w_up[:, cs], "a_up": a_up[:, cs], "g_up": g_up[:, cs]}
                for k, v in rconsts.items():
                    m["c_" + k] = v
                maps.append({k: _c(v) for k, v in m.items()})
            outs = _launch(nc, maps)
            for c in range(8):
                b, u = c // 4, c % 4
                mix_glob[b * PN:(b + 1) * PN, 1024 + u * 256:1024 + (u + 1) * 256] = outs[c]["out"]
            Wout = tile_w(f(ar_w_out[i]))
            Kmix = 2048
        else:
            nc = _build_ssd(PN)
            conv_w, conv_b = f(ssd_conv_w[i]), f(ssd_conv_b[i])
            dt_bias, a_log, d_skip, norm_w = f(ssd_dt_bias[i]), f(ssd_a_log[i]), f(ssd_d[i]), f(ssd_norm_w[i])
            maps = []
            for c in range(8):
                b, cidx = c // 4, c % 4
                z0 = z_glob[b * PN:(b + 1) * PN]
                g0 = 2 * cidx
                xs_ = slice(4096 + g0 * 512, 4096 + g0 * 512 + 1024)
                bs_ = slice(8192 + g0 * 128, 8192 + g0 * 128 + 256)
                cs_ = slice(9216 + g0 * 128, 9216 + g0 * 128 + 256)
                chan = np.concatenate([np.arange(4096)[g0 * 512:g0 * 512 + 1024], 4096 + np.arange(1024)[g0 * 128:g0 * 128 + 256],
                                       5120 + np.arange(1024)[g0 * 128:g0 * 128 + 256]])
                xbc = np.concatenate([z0[:, xs_], z0[:, bs_], z0[:, cs_]], 1)
                hs = slice(g0 * 8, g0 * 8 + 16)
                m = {"xbcT": np.concatenate([np.zeros((1536, 3), np.float32), xbc.T], 1),
                     "convw": conv_w[:, chan].T.reshape(12, 128, 4).transpose(1, 0, 2).reshape(128, 48),
                     "convb": conv_b[chan].reshape(12, 128).T,
                     "z": z0[:, g0 * 512:g0 * 512 + 1024],
                     "dtT": z0[:, 10240 + g0 * 8:10240 + g0 * 8 + 16].T, "dtb": dt_bias[hs, None], "alog": a_log[hs, None],
                     "dskip": np.repeat(d_skip[hs], 64)[None, :], "normw": norm_w[g0 * 512:g0 * 512 + 1024][None, :]}
                for k, v in sconsts.items():
                    m["c_" + k] = v
                maps.append({k: _c(v) for k, v in m.items()})
            outs = _launch(nc, maps)
            mix_glob = np.empty((TOT, 4096), np.float32)
            for c in range(8):
                b, cidx = c // 4, c % 4
                mix_glob[b * PN:(b + 1) * PN, cidx * 1024:(cidx + 1) * 1024] = outs[c]["out"]
            Wout = tile_w(f(ssd_w_out[i]))
            Kmix = 4096
        del z_glob, outs
        last = (layer == depth - 1)
        if not last:
            Win, Cn = in_proj_weights(layer + 1)
        else:
            Win, Cn = None, 0
        nc = _build_tp(N, Kmix, Cn, False)
        mix_w = windows_T(mix_glob)
        res_w = windows_T(res_glob)
        del mix_glob
        cwv = f(ffn_conv_w[layer])
        shared = {"Wout": Wout, "gam_f": _gam(f(ffn_norm_w[layer])), "Wup": tile_w(f(ffn_w_up[layer])),
                  "cw": _c(cwv.T.reshape(44, 128, 3).transpose(1, 0, 2).reshape(128, 132)),
                  "cb": _c(f(ffn_conv_b[layer]).reshape(44, 128).T), "Wdown": tile_w(f(ffn_w_down[layer]))}
        if not last:
            shared["gam_n"] = _gam(f(mix_norm_w[layer + 1]))
            shared["Win"] = Win
        maps = []
        for c in range(8):
            m = dict(shared)
            m["mixT"] = mix_w[c]
            m["resT"] = res_w[c]
            m["mask"] = mask_w[c]
            maps.append(m)
        outs = _launch(nc, maps)
        del maps, shared, mix_w, res_w, Wout, Win
        res_glob = unwindow(outs, "res2")
        if not last:
            z_glob = unwindow(outs, "znext")
        del outs
    out = np.empty((B, SEQ, D), np.float32)
    for b in range(B):
        out[b] = res_glob[b * PN + 128:(b + 1) * PN]
    return out
```
